# Optimizing a Trainium2 kernel written in Bass

```python
import math
import jax, jax.numpy as jnp
from jax import lax
import numpy as np

D_MODEL = 2048
BATCH = 4
SEQ = 2048
DEPTH = 2

GRID_W = 64
CTX_LEN = 256
HEAD_DIM = 128
N_HEADS = D_MODEL // HEAD_DIM
MLSTM_HEADS = N_HEADS // 2
DIFF_HEADS = N_HEADS - MLSTM_HEADS
MLSTM_W = MLSTM_HEADS * HEAD_DIM
DIFF_W = DIFF_HEADS * HEAD_DIM
DIFF_QK_DIM = HEAD_DIM // 2
MLSTM_CHUNK = 128
MLSTM_CONV = 5
NA_HEADS = N_HEADS
NA_WIN_ROWS = 8
NA_WIN_COLS = 16
D_FF = 5632
ATTN_BLOCK = 128
ROPE_BASE = 10000.0
NORM_EPS = 1e-6
FFN_RES = 0.5
N_MOD = 9
N_EVEN = (DEPTH + 1) // 2
N_ODD = DEPTH // 2
EVEN_SPLIT = (MLSTM_W, MLSTM_W, MLSTM_W, MLSTM_W, 4 * MLSTM_HEADS, DIFF_W, DIFF_W, DIFF_W)
EVEN_IN = sum(EVEN_SPLIT)
ODD_IN = 3 * NA_HEADS * HEAD_DIM

kernel_name = 'hybrid_mlstm_diffattn_natten_macaron'


def rms_norm(x, g):
    xf = x.astype(jnp.float32)
    xf = xf * lax.rsqrt(jnp.mean(xf * xf, axis=-1, keepdims=True) + NORM_EPS)
    return xf.astype(x.dtype) * g


def swiglu(h, w_i, w_o):
    gt, up = jnp.split(h @ w_i, 2, axis=-1)
    return (jax.nn.silu(gt) * up) @ w_o


def ffn_sublayer(h, mod, g_pre, g_post, w_i, w_o):
    shift, scale, gate = mod
    u = rms_norm(h, g_pre) * (1 + scale) + shift
    return h + FFN_RES * gate * rms_norm(swiglu(u, w_i, w_o), g_post)


def split_cols(t, sizes):
    return jnp.split(t, [int(s) for s in np.cumsum(sizes)[:-1]], axis=-1)


def to_heads(t, n_heads):
    B, T, _ = t.shape
    return t.reshape(B, T, n_heads, -1).transpose(0, 2, 1, 3)


def from_heads(t):
    B, H, T, d = t.shape
    return t.transpose(0, 2, 1, 3).reshape(B, T, H * d)


def axial_rope_angles(n_tok, dim):
    t = jnp.arange(n_tok)
    row = (t // GRID_W).astype(jnp.float32)
    col = (t % GRID_W).astype(jnp.float32)
    inv = ROPE_BASE ** (-jnp.arange(dim // 4, dtype=jnp.float32) / (dim // 4))
    return jnp.stack([row[:, None] * inv, col[:, None] * inv], axis=1)


def apply_axial_rope(x, ang):
    shp = x.shape
    xr = x.reshape(shp[:-1] + (2, 2, shp[-1] // 4)).astype(jnp.float32)
    x1, x2 = xr[..., 0, :], xr[..., 1, :]
    cos, sin = jnp.cos(ang), jnp.sin(ang)
    out = jnp.stack([x1 * cos - x2 * sin, x2 * cos + x1 * sin], axis=-2)
    return out.reshape(shp).astype(x.dtype)


def centred_dwconv(x, w):
    K = w.shape[0]
    return lax.conv_general_dilated(x, w[:, None, :], window_strides=(1,), padding=[(K // 2, K // 2)],
                                    dimension_numbers=('NWC', 'WIO', 'NWC'), feature_group_count=x.shape[-1])


def mlstm_prep(q, k, v, gates, conv_w, gate_b):
    qk = jax.nn.silu(centred_dwconv(jnp.concatenate([q, k], axis=-1), conv_w))
    q, k = jnp.split(qk, 2, axis=-1)
    f32 = jnp.float32
    q = to_heads(q, MLSTM_HEADS).astype(f32)
    k = to_heads(k, MLSTM_HEADS).astype(f32) * HEAD_DIM ** -0.5
    v = to_heads(v, MLSTM_HEADS).astype(f32)
    B, T, _ = gates.shape
    g = (gates.reshape(B, T, 4, MLSTM_HEADS) + gate_b).astype(f32).transpose(2, 0, 3, 1)
    fwd = (g[0], jax.nn.log_sigmoid(g[1]))
    bwd = (g[2], jax.nn.log_sigmoid(g[3]))
    return q, k, v, fwd, bwd


def mlstm_scan(q, k, v, log_i, log_f, state):
    B, H, T, d = q.shape
    nc = T // MLSTM_CHUNK

    def chunks(a):
        return jnp.moveaxis(a.reshape(a.shape[:2] + (nc, MLSTM_CHUNK) + a.shape[3:]), 2, 0)

    lower = jnp.tril(jnp.ones((MLSTM_CHUNK, MLSTM_CHUNK), dtype=bool))

    def step(carry, inp):
        C, n, m = carry
        qc, kc, vc, li, lf = inp
        b = jnp.cumsum(lf, axis=-1)
        dmat = jnp.where(lower, b[..., :, None] - b[..., None, :] + li[..., None, :], -jnp.inf)
        inter = b + m[..., None]
        m_t = jnp.maximum(inter, jnp.max(dmat, axis=-1))
        w_in = jnp.exp(dmat - m_t[..., None])
        w_st = jnp.exp(inter - m_t)
        s = jnp.einsum('bhtd,bhsd->bhts', qc, kc) * w_in
        num = w_st[..., None] * jnp.einsum('bhtd,bhde->bhte', qc, C) + jnp.einsum('bhts,bhse->bhte', s, vc)
        den = w_st * jnp.einsum('bhtd,bhd->bht', qc, n) + jnp.sum(s, axis=-1)
        h = num / jnp.maximum(jnp.abs(den), jnp.exp(-m_t))[..., None]
        b_end = b[..., -1]
        g = b_end[..., None] - b + li
        m_new = jnp.maximum(b_end + m, jnp.max(g, axis=-1))
        w_old = jnp.exp(b_end + m - m_new)
        w_tok = jnp.exp(g - m_new[..., None])
        C = w_old[..., None, None] * C + jnp.einsum('bhsd,bhse->bhde', kc * w_tok[..., None], vc)
        n = w_old[..., None] * n + jnp.einsum('bhs,bhsd->bhd', w_tok, kc)
        return (C, n, m_new), h

    state, h = lax.scan(step, state, (chunks(q), chunks(k), chunks(v), chunks(log_i), chunks(log_f)))
    return state, jnp.moveaxis(h, 0, 2).reshape(B, H, T, d)


def mlstm_bidir(q, k, v, gf, gb, st_f, st_b):
    st_f, h_f = mlstm_scan(q, k, v, gf[0], gf[1], st_f)
    rev = lambda a: jnp.flip(a, axis=2)
    st_b, h_b = mlstm_scan(rev(q), rev(k), rev(v), rev(gb[0]), rev(gb[1]), st_b)
    return st_f, st_b, h_f + rev(h_b)


def mlstm_mixer(lat, cx, conv_w, gate_b, head_g, need_ctx):
    q_l, k_l, v_l, gf_l, gb_l = mlstm_prep(lat[0], lat[1], lat[2], lat[4], conv_w, gate_b)
    q_c, k_c, v_c, gf_c, gb_c = mlstm_prep(cx[0], cx[1], cx[2], cx[4], conv_w, gate_b)
    B = q_c.shape[0]
    zero = (jnp.zeros((B, MLSTM_HEADS, HEAD_DIM, HEAD_DIM), jnp.float32),
            jnp.zeros((B, MLSTM_HEADS, HEAD_DIM), jnp.float32),
            jnp.zeros((B, MLSTM_HEADS), jnp.float32))
    st_f, st_b, h_c = mlstm_bidir(q_c, k_c, v_c, gf_c, gb_c, zero, zero)
    _, _, h_l = mlstm_bidir(q_l, k_l, v_l, gf_l, gb_l, st_f, st_b)
    g = head_g.reshape(MLSTM_HEADS, 1, HEAD_DIM)

    def finish(h, o):
        return from_heads(rms_norm(h, g)).astype(o.dtype) * jax.nn.sigmoid(o)

    return finish(h_l, lat[3]), (finish(h_c, cx[3]) if need_ctx else None)


def diff_heads_qk(t):
    B, T, _ = t.shape
    return t.reshape(B, T, DIFF_HEADS, 2, DIFF_QK_DIM).transpose(0, 3, 2, 1, 4)


def diff_scores_out(q, k, v, lam):
    s = jnp.einsum('bghqd,bghkd->bghqk', q, k).astype(jnp.float32) * DIFF_QK_DIM ** -0.5
    p = jax.nn.softmax(s, axis=-1)
    w = (p[:, 0] - lam * p[:, 1]).astype(v.dtype)
    return jnp.einsum('bhqk,bhkd->bhqd', w, v)


def diff_attention(lat, cx, lam_p, subln_g, layer_idx, need_ctx):
    q_l, k_l, v_l = lat
    q_c, k_c, v_c = cx
    B, S, _ = q_l.shape
    ang = axial_rope_angles(S, DIFF_QK_DIM)
    q_l = apply_axial_rope(diff_heads_qk(q_l), ang)
    k_l = apply_axial_rope(diff_heads_qk(k_l), ang)
    q_c, k_c = diff_heads_qk(q_c), diff_heads_qk(k_c)
    v_l, v_c = to_heads(v_l, DIFF_HEADS), to_heads(v_c, DIFF_HEADS)
    lam_init = 0.8 - 0.6 * math.exp(-0.3 * layer_idx)
    lp = lam_p.astype(jnp.float32)
    lam = jnp.exp(jnp.sum(lp[0] * lp[1])) - jnp.exp(jnp.sum(lp[2] * lp[3])) + lam_init
    k_all = jnp.concatenate([k_l, k_c], axis=3)
    v_all = jnp.concatenate([v_l, v_c], axis=2)
    nb = S // ATTN_BLOCK
    qb = jnp.moveaxis(q_l.reshape(B, 2, DIFF_HEADS, nb, ATTN_BLOCK, DIFF_QK_DIM), 3, 0)
    ob = lax.map(lambda qi: diff_scores_out(qi, k_all, v_all, lam), qb)
    o_l = jnp.moveaxis(ob, 0, 2).reshape(B, DIFF_HEADS, S, HEAD_DIM)

    def finish(o):
        return from_heads(rms_norm(o, subln_g) * (1 - lam_init))

    o_c = finish(diff_scores_out(q_c, k_c, v_c, lam)) if need_ctx else None
    return finish(o_l), o_c


def na_attention(lat, cx, rpb, need_ctx):
    scale = HEAD_DIM ** -0.5
    q_l = to_heads(lat[0], NA_HEADS) * scale
    k_l, v_l = to_heads(lat[1], NA_HEADS), to_heads(lat[2], NA_HEADS)
    q_c = to_heads(cx[0], NA_HEADS) * scale
    k_c, v_c = to_heads(cx[1], NA_HEADS), to_heads(cx[2], NA_HEADS)
    B, H, S, d = q_l.shape
    rows = S // GRID_W
    kr = min(NA_WIN_ROWS, rows)
    qg = q_l.reshape(B, H, rows, GRID_W, d)
    kg = k_l.reshape(B, H, rows, GRID_W, d)
    vg = v_l.reshape(B, H, rows, GRID_W, d)
    cols = jnp.arange(GRID_W)
    c_start = jnp.clip(cols - NA_WIN_COLS // 2, 0, GRID_W - NA_WIN_COLS)
    c_idx = c_start[:, None] + jnp.arange(NA_WIN_COLS)
    c_bias = c_idx - cols[:, None] + (NA_WIN_COLS - 1)
    n_win = kr * NA_WIN_COLS

    def row_fn(args):
        q_row, r = args
        rs = jnp.clip(r - kr // 2, 0, rows - kr)
        k_band = lax.dynamic_slice_in_dim(kg, rs, kr, axis=2)
        v_band = lax.dynamic_slice_in_dim(vg, rs, kr, axis=2)
        k_win = k_band[:, :, :, c_idx]
        v_win = v_band[:, :, :, c_idx]
        r_bias = rs + jnp.arange(kr) - r + (NA_WIN_ROWS - 1)
        bias = rpb[:, r_bias[:, None, None], c_bias[None]].transpose(0, 2, 1, 3)
        s_win = jnp.einsum('bhwd,bhrwcd->bhwrc', q_row, k_win).astype(jnp.float32) + bias.astype(jnp.float32)
        s_ctx = jnp.einsum('bhwd,bhld->bhwl', q_row, k_c).astype(jnp.float32)
        s = jnp.concatenate([s_win.reshape(B, H, GRID_W, n_win), s_ctx], axis=-1)
        p = jax.nn.softmax(s, axis=-1).astype(v_l.dtype)
        p_win = p[..., :n_win].reshape(B, H, GRID_W, kr, NA_WIN_COLS)
        return (jnp.einsum('bhwrc,bhrwcd->bhwd', p_win, v_win)
                + jnp.einsum('bhwl,bhld->bhwd', p[..., n_win:], v_c))

    o = lax.map(row_fn, (jnp.moveaxis(qg, 2, 0), jnp.arange(rows)))
    o_l = from_heads(jnp.moveaxis(o, 0, 2).reshape(B, H, S, d))
    o_c = None
    if need_ctx:
        pc = jax.nn.softmax(jnp.einsum('bhqd,bhkd->bhqk', q_c, k_c).astype(jnp.float32), axis=-1)
        o_c = from_heads(jnp.einsum('bhqk,bhkd->bhqd', pc.astype(v_c.dtype), v_c))
    return o_l, o_c


def even_mixer(ux, uy, w_in, w_out, conv_w, gate_b, head_g, lam_p, subln_g, layer_idx, need_ctx):
    px = split_cols(ux @ w_in, EVEN_SPLIT)
    py = split_cols(uy @ w_in, EVEN_SPLIT)
    ha_x, ha_y = mlstm_mixer(px[:5], py[:5], conv_w, gate_b, head_g, need_ctx)
    hb_x, hb_y = diff_attention(px[5:], py[5:], lam_p, subln_g, layer_idx, need_ctx)
    out_x = jnp.concatenate([ha_x, hb_x], axis=-1) @ w_out
    out_y = jnp.concatenate([ha_y, hb_y], axis=-1) @ w_out if need_ctx else None
    return out_x, out_y


def odd_mixer(ux, uy, w_in, w_out, rpb, need_ctx):
    px = jnp.split(ux @ w_in, 3, axis=-1)
    py = jnp.split(uy @ w_in, 3, axis=-1)
    o_x, o_y = na_attention(px, py, rpb, need_ctx)
    return o_x @ w_out, (o_y @ w_out if need_ctx else None)


def setup_inputs(seed: int = 0) -> dict:
    key = jax.random.key(seed)
    ks = jax.random.split(key, 20)
    nrm = jax.random.normal
    f32 = jnp.float32
    D = D_MODEL
    lin = jnp.linspace(3.0, 6.0, MLSTM_HEADS)
    zer = jnp.zeros((MLSTM_HEADS,), f32)
    gate_base = jnp.stack([zer, lin, zer, lin])
    return {
        'x': nrm(ks[0], (BATCH, SEQ, D), f32),
        'c': nrm(ks[1], (BATCH, D), f32),
        'ctx': nrm(ks[2], (BATCH, CTX_LEN, D), f32),
        'c_ctx': nrm(ks[3], (D,), f32),
        'w_ada': nrm(ks[4], (DEPTH, D, N_MOD * D), f32) * (0.5 * D ** -0.5),
        'b_ada': nrm(ks[5], (DEPTH, N_MOD * D), f32) * 0.02,
        'norm_g': 1.0 + 0.05 * nrm(ks[6], (DEPTH, 6, D), f32),
        'w_ffn_in': nrm(ks[7], (DEPTH, 2, D, 2 * D_FF), f32) * D ** -0.5,
        'w_ffn_out': nrm(ks[8], (DEPTH, 2, D_FF, D), f32) * D_FF ** -0.5,
        'w_in_even': nrm(ks[9], (N_EVEN, D, EVEN_IN), f32) * D ** -0.5,
        'w_out_even': nrm(ks[10], (N_EVEN, MLSTM_W + DIFF_W, D), f32) * (MLSTM_W + DIFF_W) ** -0.5,
        'mlstm_conv': nrm(ks[11], (N_EVEN, MLSTM_CONV, 2 * MLSTM_W), f32) * MLSTM_CONV ** -0.5,
        'mlstm_gate_b': gate_base[None] + 0.1 * nrm(ks[12], (N_EVEN, 4, MLSTM_HEADS), f32),
        'mlstm_head_g': 1.0 + 0.05 * nrm(ks[13], (N_EVEN, MLSTM_W), f32),
        'diff_lambda': 0.1 * nrm(ks[14], (N_EVEN, 4, DIFF_QK_DIM), f32),
        'diff_subln_g': 1.0 + 0.05 * nrm(ks[15], (N_EVEN, HEAD_DIM), f32),
        'w_in_odd': nrm(ks[16], (N_ODD, D, ODD_IN), f32) * D ** -0.5,
        'w_out_odd': nrm(ks[17], (N_ODD, NA_HEADS * HEAD_DIM, D), f32) * (NA_HEADS * HEAD_DIM) ** -0.5,
        'na_rpb': 0.1 * nrm(ks[18], (N_ODD, NA_HEADS, 2 * NA_WIN_ROWS - 1, 2 * NA_WIN_COLS - 1), f32),
    }


def reference(x, c, ctx, c_ctx, w_ada, b_ada, norm_g, w_ffn_in, w_ffn_out, w_in_even, w_out_even,
              mlstm_conv, mlstm_gate_b, mlstm_head_g, diff_lambda, diff_subln_g, w_in_odd, w_out_odd, na_rpb):
    B = x.shape[0]
    y = ctx
    for l in range(DEPTH):
        last = l == DEPTH - 1
        mx = (jax.nn.silu(c) @ w_ada[l] + b_ada[l]).reshape(B, N_MOD, 1, D_MODEL)
        my = (jax.nn.silu(c_ctx) @ w_ada[l] + b_ada[l]).reshape(N_MOD, D_MODEL)
        g = norm_g[l]
        x = ffn_sublayer(x, (mx[:, 0], mx[:, 1], mx[:, 2]), g[0], g[1], w_ffn_in[l, 0], w_ffn_out[l, 0])
        y = ffn_sublayer(y, (my[0], my[1], my[2]), g[0], g[1], w_ffn_in[l, 0], w_ffn_out[l, 0])
        ux = rms_norm(x, g[2]) * (1 + mx[:, 4]) + mx[:, 3]
        uy = rms_norm(y, g[2]) * (1 + my[4]) + my[3]
        if l % 2 == 0:
            e = l // 2
            mix_x, mix_y = even_mixer(ux, uy, w_in_even[e], w_out_even[e], mlstm_conv[e], mlstm_gate_b[e],
                                      mlstm_head_g[e], diff_lambda[e], diff_subln_g[e], l, not last)
        else:
            o = l // 2
            mix_x, mix_y = odd_mixer(ux, uy, w_in_odd[o], w_out_odd[o], na_rpb[o], not last)
        x = x + mx[:, 5] * rms_norm(mix_x, g[3])
        x = ffn_sublayer(x, (mx[:, 6], mx[:, 7], mx[:, 8]), g[4], g[5], w_ffn_in[l, 1], w_ffn_out[l, 1])
        if not last:
            y = y + my[5] * rms_norm(mix_y, g[3])
            y = ffn_sublayer(y, (my[6], my[7], my[8]), g[4], g[5], w_ffn_in[l, 1], w_ffn_out[l, 1])
    return x
```

```python
import numpy as np
import concourse.bass as bass
import concourse.mybir as mybir
from concourse.bass_utils import run_bass_kernel_spmd
from contextlib import ExitStack

F32 = mybir.dt.float32
BF16 = mybir.dt.bfloat16
AF = mybir.ActivationFunctionType
ALU = mybir.AluOpType
AX = mybir.AxisListType

SAME_ENGINE_SYNC = True


class Bld:
    def __init__(self, nc, es):
        self.nc = nc
        self.es = es
        self.eng = {'pe': nc.tensor, 'act': nc.scalar, 'dve': nc.vector, 'pool': nc.gpsimd, 'sp': nc.sync}
        self.semh = {}
        self.cnt = {}
        for k in ['pe', 'act', 'dve', 'pool']:
            self.semh[k] = es.enter_context(nc.semaphore("s_" + k))
            self.cnt[k] = 0
        self.seen = {k: {} for k in self.eng}
        self.lastw = {}
        self.readers = {}
        self.nwaits = 0
        self.ninst = 0

    def _deps(self, reads, writes, waw=True):
        need = {}
        def add(t):
            if t is None:
                return
            s, v = t
            if need.get(s, 0) < v:
                need[s] = v
        for r in reads:
            add(self.lastw.get(r))
        for w in writes:
            if waw:
                add(self.lastw.get(w))
            for s, v in self.readers.get(w, {}).items():
                add((s, v))
        return need

    def _wait(self, e, need):
        for s, v in need.items():
            if s == e and (e == 'pe' or not SAME_ENGINE_SYNC):
                continue
            if self.seen[e].get(s, 0) >= v:
                continue
            self.eng[e].wait_ge(self.semh[s], v)
            self.seen[e][s] = v
            self.nwaits += 1

    def _commit(self, tok, reads, writes):
        s, v = tok
        for r in reads:
            d = self.readers.setdefault(r, {})
            if d.get(s, 0) < v:
                d[s] = v
        for w in writes:
            self.lastw[w] = tok
            self.readers[w] = {}

    def op(self, e, fn, reads=(), writes=()):
        need = self._deps(reads, writes)
        self._wait(e, need)
        inst = fn(self.eng[e])
        self.cnt[e] += 1
        inst.then_inc(self.semh[e], 1)
        self._commit((e, self.cnt[e]), reads, writes)
        self.ninst += 1
        return inst

    def dma(self, q, out, in_, reads=(), writes=(), stream=None, waw=True, **kw):
        need = self._deps(reads, writes, waw=waw)
        self._wait(q, need)
        if stream is None:
            stream = writes[0]
        key = ('dma', stream)
        if key not in self.semh:
            self.semh[key] = self.es.enter_context(self.nc.semaphore("d%d" % len(self.semh)))
            self.cnt[key] = 0
        inst = self.eng[q].dma_start(out=out, in_=in_, **kw)
        self.cnt[key] += 16
        inst.then_inc(self.semh[key], 16)
        self._commit((key, self.cnt[key]), reads, writes)
        self.ninst += 1
        return inst

    def finish(self, resources, e='sp'):
        need = {}
        for r in resources:
            t = self.lastw.get(r)
            if t is not None and need.get(t[0], 0) < t[1]:
                need[t[0]] = t[1]
        self._wait(e, need)


def _barrier(self):
    for e in self.eng:
        need = {s: v for s, v in self.cnt.items() if v > 0}
        self._wait(e, need)
Bld.barrier = _barrier


def _allgather(self, src_t, dst_t, groups, reads, writes):
    need = self._deps(reads, writes)
    self._wait('pool', need)
    key = ('cc', 'cc')
    if key not in self.semh:
        self.semh[key] = self.es.enter_context(self.nc.semaphore("ccsem"))
        self.cnt[key] = 0
    inst = self.nc.gpsimd.collective_compute("AllGather", ALU.bypass, replica_groups=groups, ins=[src_t.ap().opt()], outs=[dst_t.ap().opt()])
    self.cnt[key] += 1
    inst.then_inc(self.semh[key])
    self._commit((key, self.cnt[key]), reads, writes)
    self.ninst += 1
    return inst
Bld.allgather = _allgather


D = 2048
DFF = 5632
KC = D // 128
FC = DFF // 128
EPS = 1e-6

SB_BASE = 16512
SB_TOTAL = 229344
CONST_OFF = SB_TOTAL - 1152


class Arena:
    def __init__(self, nc):
        self.nc = nc
        self.n = 0
    def at(self, off, shape, dt, name=None):
        self.n += 1
        nm = "%s_%d" % (name or "t", self.n)
        esz = 4 if dt == F32 else 2
        sz = esz
        for s in shape[1:]:
            sz *= s
        assert off % 32 == 0 and off + sz <= SB_TOTAL, (nm, off, sz)
        return self.nc.alloc_sbuf_tensor_at(nm, list(shape), dt, offset=off), (off + sz + 31) // 32 * 32


def blocks_of(T):
    nt = T // 128
    nb = (T + 511) // 512
    per = (nt + nb - 1) // nb
    out = []
    s = 0
    while s < nt:
        e = min(nt, s + per)
        out.append((s * 128, (e - s) * 128))
        s = e
    return out


class Ctx:
    pass


def setup_consts(b, ar, nc, ident_dram, off):
    c = Ctx()
    c.idb, off = ar.at(off, [128, 128], BF16, "idb")
    c.idf, off = ar.at(off, [128, 128], F32, "idf")
    c.stat, off = ar.at(off, [128, 64], F32, "stat")
    b.dma('pool', c.idb[:], ident_dram, writes=['idb'])
    b.dma('sp', c.idf[:], ident_dram, writes=['idf'])
    c.off = off
    c.statn = 0
    return c


def emit_rstd(b, c, src_ap, src_res, junk_ap, junk_res, n):
    i = c.statn % 16
    c.statn += 1
    ss = c.stat[:, 4 * i:4 * i + 1]
    rs = c.stat[:, 4 * i + 1:4 * i + 2]
    r1 = 'stat%d' % i
    b.op('act', lambda e: e.activation(out=junk_ap, in_=src_ap, func=AF.Square, accum_out=ss),
         reads=[src_res], writes=[junk_res, r1])
    b.op('dve', lambda e: e.tensor_scalar(out=rs, in0=ss, scalar1=1.0 / n, scalar2=EPS, op0=ALU.mult, op1=ALU.add),
         reads=[r1], writes=[r1 + 'b'])
    b.op('act', lambda e: e.sqrt(out=rs, in_=rs), reads=[r1 + 'b'], writes=[r1 + 'b'])
    b.op('dve', lambda e: e.reciprocal(out=rs, in_=rs), reads=[r1 + 'b'], writes=[r1 + 'b'])
    return rs, r1 + 'b'


def emit_pre(b, c, ar, nc, base, x_dram, x_res, tiles, modrow, g_dram, i_shift, i_scale, uT=None, u_out=None, u_res=None, ps=None):
    off = base
    XT = []
    for i in range(2):
        t_, off = ar.at(off, [128, D], F32, "XT"); XT.append(t_)
    Uf, off = ar.at(off, [128, D], F32, "Uf")
    UB = []
    for i in range(2):
        t_, off = ar.at(off, [128, D], BF16, "Ub"); UB.append(t_)
    A, off = ar.at(off, [128, D], F32, "A")
    SH, off = ar.at(off, [128, D], F32, "SH")
    GP, off = ar.at(off, [128, D], F32, "GP")
    b.dma('sp', GP[:], g_dram.partition_broadcast(128), writes=['GP'])
    cur = None
    for n, (t, mr) in enumerate(tiles):
        if mr != cur:
            cur = mr
            b.dma('sp', A[:], modrow[mr, i_scale * D:(i_scale + 1) * D].partition_broadcast(128), reads=['modrow'], writes=['A'])
            b.dma('sp', SH[:], modrow[mr, i_shift * D:(i_shift + 1) * D].partition_broadcast(128), reads=['modrow'], writes=['SH'])
            b.op('dve', lambda e: e.scalar_tensor_tensor(out=A[:], in0=A[:], scalar=1.0, in1=GP[:], op0=ALU.add, op1=ALU.mult),
                 reads=['A', 'GP'], writes=['A'])
        xt = XT[n % 2]; xr = 'XT%d' % (n % 2)
        ub = UB[n % 2]; ur = 'UB%d' % (n % 2)
        b.dma('sp', xt[:], x_dram[t * 128:(t + 1) * 128, :], reads=[x_res], writes=[xr])
        rs, rr = emit_rstd(b, c, xt[:], xr, Uf[:], 'Uf', D)
        b.op('dve', lambda e: e.scalar_tensor_tensor(out=Uf[:], in0=xt[:], scalar=rs, in1=A[:], op0=ALU.mult, op1=ALU.mult),
             reads=[xr, rr, 'A'], writes=['Uf'])
        b.op('dve', lambda e: e.tensor_tensor(out=ub[:], in0=Uf[:], in1=SH[:], op=ALU.add), reads=['Uf', 'SH'], writes=[ur])
        if uT is not None:
            emit_transpose_tile(b, c, ub, ur, uT, 'uT', t, KC, ps, n)
        else:
            if callable(u_out):
                dap, dres = u_out(t)
                b.dma('sp', dap, ub[:], reads=[ur], writes=[dres], waw=False)
            else:
                b.dma('sp', u_out[t * 128:(t + 1) * 128, :], ub[:], reads=[ur], writes=[u_res], waw=False)


def emit_transpose_tile(b, c, src, src_res, dstT, dst_res, t, nch, ps, n):
    for h in range(0, nch, 8):
        k = (n * ((nch + 7) // 8) + h // 8) % len(ps)
        p = ps[k]; pr = 'psT%d' % k
        m = min(8, nch - h)
        for cc in range(m):
            ch = h + cc
            b.op('pe', lambda e: e.transpose(p[:, cc, :], src[:, ch * 128:(ch + 1) * 128], c.idb[:]),
                 reads=[src_res, 'idb'], writes=[pr])
        eng = 'act' if (k % 2 == 0) else 'dve'
        if eng == 'act':
            b.op('act', lambda e: e.copy(out=dstT[:, h:h + m, t * 128:(t + 1) * 128], in_=p[:, 0:m, :]), reads=[pr], writes=[dst_res])
        else:
            b.op('dve', lambda e: e.tensor_copy(out=dstT[:, h:h + m, t * 128:(t + 1) * 128], in_=p[:, 0:m, :]), reads=[pr], writes=[dst_res])


def emit_linear_T(b, ar, nc, base, srcT, src_res, nk, T, w_dram, w_res, ncols, outT, out_res, pbanks, colblk=256, nbuf=2):
    off = base
    WB = []
    for i in range(nbuf):
        t_, off = ar.at(off, [128, nk, colblk], BF16, "WB"); WB.append(t_)
    blks = blocks_of(T)
    nb = len(blks)
    wv = w_dram.rearrange("(c p) n -> p c n", p=128)
    it = 0
    for cb in range(ncols // colblk):
        wb = WB[cb % nbuf]; wr = 'WB%d' % (cb % nbuf)
        b.dma('pool', wb[:], wv[:, :, cb * colblk:(cb + 1) * colblk], reads=[w_res], writes=[wr])
        for dd in range(colblk // 128):
            cch = cb * (colblk // 128) + dd
            par = it % 2; it += 1
            for k in range(nk):
                for bi, (s0, sl) in enumerate(blks):
                    pb = pbanks[par * nb + bi]; pr = 'pb%d' % (par * nb + bi)
                    b.op('pe', lambda e: e.matmul(pb[:, 0:sl], lhsT=wb[:, k, dd * 128:(dd + 1) * 128], rhs=srcT[:, k, s0:s0 + sl],
                                                  start=(k == 0), stop=(k == nk - 1)),
                         reads=[wr, src_res], writes=[pr])
            for bi, (s0, sl) in enumerate(blks):
                pb = pbanks[par * nb + bi]; pr = 'pb%d' % (par * nb + bi)
                if bi % 2 == 0:
                    b.op('act', lambda e: e.copy(out=outT[:, cch, s0:s0 + sl], in_=pb[:, 0:sl]), reads=[pr], writes=[out_res])
                else:
                    b.op('dve', lambda e: e.tensor_copy(out=outT[:, cch, s0:s0 + sl], in_=pb[:, 0:sl]), reads=[pr], writes=[out_res])


def emit_ffn_m1(b, ar, nc, base, uT, T, w_in, w_res, actT, pbanks, SG, colblk=256):
    off = base
    WG = []; WU = []
    for i in range(2):
        t_, off = ar.at(off, [128, KC, colblk], BF16, "WG"); WG.append(t_)
        t_, off = ar.at(off, [128, KC, colblk], BF16, "WU"); WU.append(t_)
    blks = blocks_of(T)
    nb = len(blks)
    wv = w_in.rearrange("(c p) n -> p c n", p=128)
    for cb in range(DFF // colblk):
        wg = WG[cb % 2]; wu = WU[cb % 2]
        gr = 'WG%d' % (cb % 2); ur = 'WU%d' % (cb % 2)
        b.dma('pool', wg[:], wv[:, :, cb * colblk:(cb + 1) * colblk], reads=[w_res], writes=[gr])
        b.dma('pool', wu[:], wv[:, :, DFF + cb * colblk:DFF + (cb + 1) * colblk], reads=[w_res], writes=[ur])
        for dd in range(colblk // 128):
            j = cb * (colblk // 128) + dd
            for (wt, wr, pofs) in ((wg, gr, 0), (wu, ur, nb)):
                for k in range(KC):
                    for bi, (s0, sl) in enumerate(blks):
                        pb = pbanks[pofs + bi]; pr = 'pb%d' % (pofs + bi)
                        b.op('pe', lambda e: e.matmul(pb[:, 0:sl], lhsT=wt[:, k, dd * 128:(dd + 1) * 128], rhs=uT[:, k, s0:s0 + sl],
                                                      start=(k == 0), stop=(k == KC - 1)),
                             reads=[wr, 'uT'], writes=[pr])
            for bi, (s0, sl) in enumerate(blks):
                pg = pbanks[bi]; pu = pbanks[nb + bi]
                b.op('act', lambda e: e.activation(out=SG[:, s0:s0 + sl], in_=pg[:, 0:sl], func=AF.Silu),
                     reads=['pb%d' % bi], writes=['SG%d' % bi])
                b.op('dve', lambda e: e.tensor_tensor(out=actT[:, j, s0:s0 + sl], in0=pu[:, 0:sl], in1=SG[:, s0:s0 + sl], op=ALU.mult),
                     reads=['pb%d' % (nb + bi), 'SG%d' % bi], writes=['actT'])


def emit_post(b, c, ar, nc, base, outT, outT_res, x_dram, x_res, xo_dram, xo_res, tiles, modrow, g_dram, i_gate, res_scale, pq):
    off = base
    XT = []; XO = []
    for i in range(2):
        t_, off = ar.at(off, [128, D], F32, "QX"); XT.append(t_)
        t_, off = ar.at(off, [128, D], F32, "QO"); XO.append(t_)
    G, off = ar.at(off, [128, D], F32, "G")
    GP, off = ar.at(off, [128, D], F32, "GPq")
    junk, off = ar.at(off, [128, D], F32, "junk")
    b.dma('sp', GP[:], g_dram.partition_broadcast(128), writes=['GPq'])
    cur = None
    for n, (t, mr) in enumerate(tiles):
        if mr != cur:
            cur = mr
            b.dma('sp', G[:], modrow[mr, i_gate * D:(i_gate + 1) * D].partition_broadcast(128), reads=['modrow'], writes=['G'])
            b.op('dve', lambda e: e.scalar_tensor_tensor(out=G[:], in0=G[:], scalar=float(res_scale), in1=GP[:], op0=ALU.mult, op1=ALU.mult),
                 reads=['G', 'GPq'], writes=['G'])
        p = pq[n % 2]; pr = 'pq%d' % (n % 2)
        xt = XT[n % 2]; xr = 'QX%d' % (n % 2)
        xo = XO[n % 2]; xor_ = 'QO%d' % (n % 2)
        b.dma('sp', xt[:], x_dram[t * 128:(t + 1) * 128, :], reads=[x_res], writes=[xr])
        for ch in range(KC):
            b.op('pe', lambda e: e.transpose(p[:, ch * 128:(ch + 1) * 128], outT[:, ch, t * 128:(t + 1) * 128], c.idf[:]),
                 reads=[outT_res, 'idf'], writes=[pr])
        rs, rr = emit_rstd(b, c, p[:], pr, junk[:], 'junk', D)
        b.op('dve', lambda e: e.scalar_tensor_tensor(out=xo[:], in0=p[:], scalar=rs, in1=G[:], op0=ALU.mult, op1=ALU.mult),
             reads=[pr, rr, 'G'], writes=[xor_])
        b.op('dve', lambda e: e.tensor_tensor(out=xo[:], in0=xo[:], in1=xt[:], op=ALU.add), reads=[xor_, xr], writes=[xor_])
        b.dma('sp', xo_dram[t * 128:(t + 1) * 128, :], xo[:], reads=[xor_], writes=[xo_res], waw=False)


STOP = 99

NTOK = 2304
NTIL = 18
PADW = 2312
POS_X = 2
POS_C = 2054


def padpos(tok):
    return POS_X + tok if tok < 2048 else POS_C + (tok - 2048)


def emit_load_uT(b, c, ar, base, u_dram, u_res, ntile, uT, PS):
    off = base
    UB = []
    for i in range(2):
        t_, off = ar.at(off, [128, D], BF16, "UBl"); UB.append(t_)
    for t in range(ntile):
        ub = UB[t % 2]; ur = 'UBl%d' % (t % 2)
        if callable(u_dram):
            sap, sres = u_dram(t)
            b.dma('sp', ub[:], sap, reads=[sres], writes=[ur])
        else:
            b.dma('sp', ub[:], u_dram[t * 128:(t + 1) * 128, :], reads=[u_res], writes=[ur])
        emit_transpose_tile(b, c, ub, ur, uT, 'uT', t, KC, PS.psT, t)
    return off


def emit_mlstm(b, c, ar, nc, PS, u_dram, w_mh, w_g, conv_w, gate_b, head_g, cmat, hmix, hmix_res):
    P0 = SB_BASE
    QK, o1 = ar.at(P0, [128, 8, PADW], BF16, "QK")
    VX, o2 = ar.at(o1, [128, 4, NTIL, 130], BF16, "VX")
    OS, o3 = ar.at(o2, [128, 4, NTIL, 128], BF16, "OS")
    GT, o4 = ar.at(o3, [128, NTIL, 16], F32, "GT")
    SP, o5 = ar.at(o4, [128, NTIL, 16], F32, "SP")
    CW, o6 = ar.at(o5, [128, 8, 5], F32, "CW")
    GB, o7 = ar.at(o6, [128, 16], F32, "GB")
    HG, o8 = ar.at(o7, [128, 512], F32, "HG")
    CM, o9 = ar.at(o8, [128, 5, 128], F32, "CM")
    TRB, o10 = ar.at(o9, [128, 2, 128], BF16, "TRB")
    P1 = (o10 + 31) // 32 * 32
    b.dma('sp', CW[:], conv_w, writes=['CW'])
    b.dma('sp', GB[:], gate_b.partition_broadcast(128), writes=['GB'])
    b.dma('sp', HG[:], head_g.partition_broadcast(128), writes=['HG'])
    b.dma('sp', CM[:], cmat, writes=['CM'])
    b.op('dve', lambda e: e.tensor_copy(out=TRB[:], in_=CM[:, 1:3, :]), reads=['CM'], writes=['TRB'])
    b.op('dve', lambda e: e.memset(QK[:], 0.0), writes=['QK'])
    b.op('dve', lambda e: e.memset(VX[:], 1.0), writes=['VX'])

    if STOP == 10:
        b.barrier(); return
    uT, q1 = ar.at(P1, [128, KC, NTOK], BF16, "uT")
    WM = []
    q = q1
    for i in range(2):
        t_, q = ar.at(q, [128, KC, 512], BF16, "WM"); WM.append(t_)
    WG, q = ar.at(q, [128, KC, 16], BF16, "WG")
    emit_load_uT(b, c, ar, q, u_dram, 'u', NTIL, uT, PS)
    b.dma('pool', WG[:], w_g.rearrange("(c p) n -> p c n", p=128), writes=['WG'])
    if STOP == 11:
        b.barrier(); return
    for t in range(NTIL):
        pb = PS.pb[t % 2]; pr = 'pb%d' % (t % 2)
        for k in range(KC):
            b.op('pe', lambda e: e.matmul(pb[:, 0:16], lhsT=uT[:, k, t * 128:(t + 1) * 128], rhs=WG[:, k, :], start=(k == 0), stop=(k == KC - 1)),
                 reads=['uT', 'WG'], writes=[pr])
        b.op('dve', lambda e: e.tensor_tensor(out=GT[:, t, :], in0=pb[:, 0:16], in1=GB[:], op=ALU.add), reads=[pr, 'GB'], writes=['GT'])
    if STOP == 12:
        b.barrier(); return
    b.op('act', lambda e: e.activation(out=SP[:], in_=GT[:], func=AF.Exp, scale=-1.0), reads=['GT'], writes=['SP'])
    b.op('dve', lambda e: e.tensor_scalar(out=SP[:], in0=SP[:], scalar1=1.0, scalar2=None, op0=ALU.add), reads=['SP'], writes=['SP'])
    b.op('act', lambda e: e.activation(out=SP[:], in_=SP[:], func=AF.Ln), reads=['SP'], writes=['SP'])
    if STOP == 13:
        b.barrier(); return
    blks = blocks_of(NTOK)
    it = 0
    for h in range(4):
        if STOP in (15, 20, 21) and h == 1:
            b.barrier(); return
        if STOP == 17 and h == 2:
            b.barrier(); return
        wm = WM[h % 2]; wr = 'WM%d' % (h % 2)
        b.dma('pool', wm[:], w_mh[h].rearrange("(c p) n -> p c n", p=128), reads=['w'], writes=[wr])
        for qk in range(2):
            for (s0, sl) in blks:
                pb = PS.pb[2 + it % 4]; pr = 'pb%d' % (2 + it % 4); it += 1
                for k in range(KC):
                    b.op('pe', lambda e: e.matmul(pb[:, 0:sl], lhsT=wm[:, k, qk * 128:(qk + 1) * 128], rhs=uT[:, k, s0:s0 + sl],
                                                  start=(k == 0), stop=(k == KC - 1)), reads=[wr, 'uT'], writes=[pr])
                segs = []
                if s0 < 2048:
                    e_ = min(s0 + sl, 2048)
                    segs.append((s0, e_ - s0))
                if s0 + sl > 2048:
                    s_ = max(s0, 2048)
                    segs.append((s_, s0 + sl - s_))
                for (a, l) in segs:
                    pp = padpos(a)
                    b.op('act', lambda e: e.copy(out=QK[:, h * 2 + qk, pp:pp + l], in_=pb[:, a - s0:a - s0 + l]), reads=[pr], writes=['QK'])
        if STOP == 14 and h == 0:
            b.barrier(); return
        if STOP == 16 and h == 1:
            b.barrier(); return
        for t in range(NTIL):
            pb = PS.pb[t % 2]; pr = 'pb%d' % (t % 2)
            for k in range(KC):
                b.op('pe', lambda e: e.matmul(pb[:, 0:256], lhsT=uT[:, k, t * 128:(t + 1) * 128], rhs=wm[:, k, 256:512], start=(k == 0), stop=(k == KC - 1)),
                     reads=['uT', wr], writes=[pr])
            if t % 2 == 0:
                b.op('dve', lambda e: e.tensor_copy(out=VX[:, h, t, 0:128], in_=pb[:, 0:128]), reads=[pr], writes=['VX'])
                b.op('dve', lambda e: e.tensor_copy(out=OS[:, h, t, :], in_=pb[:, 128:256]), reads=[pr], writes=['OS'])
            else:
                b.op('act', lambda e: e.copy(out=VX[:, h, t, 0:128], in_=pb[:, 0:128]), reads=[pr], writes=['VX'])
                b.op('act', lambda e: e.copy(out=OS[:, h, t, :], in_=pb[:, 128:256]), reads=[pr], writes=['OS'])
    b.barrier()
    if STOP == 1:
        return

    q = P1
    CQ, q = ar.at(q, [128, 2, PADW], F32, "CQ")
    EBK, q = ar.at(q, [128, 4, NTOK], F32, "EBK")
    QKs, q = ar.at(q, [128, 4, NTOK], BF16, "QKs")
    KT, q = ar.at(q, [128, 2, NTIL, 128], BF16, "KT")
    HA, q = ar.at(q, [128, NTIL, 128], F32, "HA")
    HS, q = ar.at(q, [128, NTIL, 128], F32, "HS")
    HO, q = ar.at(q, [128, NTIL, 128], BF16, "HO")
    LB, q = ar.at(q, [128, 4, 128], F32, "LB")
    CS, q = ar.at(q, [128, 2, 130], F32, "CS")
    CSb, q = ar.at(q, [128, 2, 130], BF16, "CSb")
    SW, q = ar.at(q, [128, 4, 128], BF16, "SW")
    ST, q = ar.at(q, [128, 64], F32, "ST")
    order_f = [16, 17] + list(range(16))
    order_b = [17, 16] + list(range(15, -1, -1))
    for h in range(4):
        for qk in range(2):
            src = QK[:, h * 2 + qk, :]
            dst = CQ[:, qk, 2:PADW - 2]
            L = PADW - 4
            b.op('dve', lambda e: e.tensor_scalar(out=dst, in0=src[:, 0:L], scalar1=CW[:, h * 2 + qk, 0:1], scalar2=None, op0=ALU.mult),
                 reads=['QK', 'CW'], writes=['CQ%d' % qk])
            for kk in range(1, 5):
                b.op('dve', lambda e: e.scalar_tensor_tensor(out=dst, in0=src[:, kk:kk + L], scalar=CW[:, h * 2 + qk, kk:kk + 1], in1=dst,
                                                             op0=ALU.mult, op1=ALU.add), reads=['QK', 'CW', 'CQ%d' % qk], writes=['CQ%d' % qk])
            b.op('act', lambda e: e.activation(out=dst, in_=dst, func=AF.Silu), reads=['CQ%d' % qk], writes=['CQ%d' % qk])
        if STOP == 2:
            b.barrier(); return
        n = 0
        for d in range(2):
            gi = 0 if d == 0 else 2
            for t in range(NTIL):
                lsp = LB[:, (n % 2) * 2, :]; lli = LB[:, (n % 2) * 2 + 1, :]
                lr = 'LB%d' % (n % 2)
                pa = PS.pb[(n % 2) * 2]; pbb = PS.pb[(n % 2) * 2 + 1]
                par = 'pb%d' % ((n % 2) * 2); pbr = 'pb%d' % ((n % 2) * 2 + 1)
                n += 1
                csp = SP[:, t, (gi + 1) * 4 + h:(gi + 1) * 4 + h + 1]
                cli = GT[:, t, gi * 4 + h:gi * 4 + h + 1]
                b.op('dve', lambda e: e.tensor_scalar(out=lsp, in0=CM[:, 3, :], scalar1=csp, scalar2=None, op0=ALU.mult), reads=['CM', 'SP'], writes=[lr])
                b.op('dve', lambda e: e.tensor_scalar(out=lli, in0=CM[:, 3, :], scalar1=cli, scalar2=None, op0=ALU.mult), reads=['CM', 'GT'], writes=[lr])
                tri = CM[:, 1 + d, :]
                b.op('pe', lambda e: e.matmul(pa[:, 0:128], lhsT=lsp, rhs=tri, start=True, stop=True), reads=[lr, 'CM'], writes=[par])
                b.op('pe', lambda e: e.matmul(pbb[:, 0:128], lhsT=lli, rhs=CM[:, 0, :], start=True, stop=False), reads=[lr, 'CM'], writes=[pbr])
                b.op('pe', lambda e: e.matmul(pbb[:, 0:128], lhsT=lsp, rhs=tri, start=False, stop=True), reads=[lr, 'CM'], writes=[pbr])
                b.op('act', lambda e: e.activation(out=EBK[:, d * 2, t * 128:(t + 1) * 128], in_=pa[:, 0:128], func=AF.Exp, scale=-1.0), reads=[par], writes=['EBK'])
                b.op('act', lambda e: e.activation(out=EBK[:, d * 2 + 1, t * 128:(t + 1) * 128], in_=pbb[:, 0:128], func=AF.Exp), reads=[pbr], writes=['EBK'])
        if STOP == 3:
            b.barrier(); return
        for d in range(2):
            for (a, l) in ((0, 2048), (2048, 256)):
                pp = padpos(a)
                b.op('dve', lambda e: e.tensor_tensor(out=QKs[:, d * 2, a:a + l], in0=CQ[:, 0, pp:pp + l], in1=EBK[:, d * 2, a:a + l], op=ALU.mult),
                     reads=['CQ0', 'EBK'], writes=['QKs'])
                b.op('dve', lambda e: e.scalar_tensor_tensor(out=QKs[:, d * 2 + 1, a:a + l], in0=CQ[:, 1, pp:pp + l], scalar=128.0 ** -0.5, in1=EBK[:, d * 2 + 1, a:a + l],
                                                             op0=ALU.mult, op1=ALU.mult), reads=['CQ1', 'EBK'], writes=['QKs'])
        n = 0
        for d in range(2):
            for t0 in range(0, NTIL, 8):
                k_ = n % 2; n += 1
                p = PS.psT[k_]; pr = 'psT%d' % k_
                m = min(8, NTIL - t0)
                for cc in range(m):
                    t = t0 + cc
                    b.op('pe', lambda e: e.transpose(p[:, cc, :], QKs[:, d * 2 + 1, t * 128:(t + 1) * 128], c.idb[:]), reads=['QKs', 'idb'], writes=[pr])
                b.op('act', lambda e: e.copy(out=KT[:, d, t0:t0 + m, :], in_=p[:, 0:m, :]), reads=[pr], writes=['KT'])
        if STOP == 4:
            b.barrier(); return
        b.op('dve', lambda e: e.memset(CS[:], 0.0), writes=['CS0', 'CS1'])
        b.op('dve', lambda e: e.memset(CSb[:], 0.0), writes=['CSb0', 'CSb1'])
        for i in range(NTIL):
            for d in range(2):
                t = (order_f if d == 0 else order_b)[i]
                pS = PS.pb[d * 3]; pN = PS.pb[d * 3 + 1]; pU = PS.pb[d * 3 + 2]
                rS = 'pb%d' % (d * 3); rN = 'pb%d' % (d * 3 + 1); rU = 'pb%d' % (d * 3 + 2)
                qT = QKs[:, d * 2, t * 128:(t + 1) * 128]
                kT = QKs[:, d * 2 + 1, t * 128:(t + 1) * 128]
                sw = SW[:, d * 2 + (i % 2), :]; swr = 'SW%d' % (d * 2 + (i % 2))
                b.op('pe', lambda e: e.matmul(pS[:, 0:128], lhsT=kT, rhs=qT, start=True, stop=True), reads=['QKs'], writes=[rS])
                b.op('dve', lambda e: e.tensor_tensor(out=sw, in0=pS[:, 0:128], in1=TRB[:, d, :], op=ALU.mult), reads=[rS, 'TRB'], writes=[swr])
                b.op('pe', lambda e: e.matmul(pN[:, 0:129], lhsT=sw, rhs=VX[:, h, t, 0:129], start=True, stop=(i == 0)), reads=[swr, 'VX'], writes=[rN])
                if i > 0:
                    b.op('pe', lambda e: e.matmul(pN[:, 0:129], lhsT=qT, rhs=CSb[:, d, 0:129], start=False, stop=True), reads=['QKs', 'CSb%d' % d], writes=[rN])
                b.op('pe', lambda e: e.matmul(pU[:, 0:129], lhsT=KT[:, d, t, :], rhs=VX[:, h, t, 0:129], start=True, stop=True), reads=['KT', 'VX'], writes=[rU])
                dn = ST[:, d * 2:d * 2 + 1]; dr = 'STd%d' % d
                b.op('act', lambda e: e.activation(out=dn, in_=pN[:, 128:129], func=AF.Abs), reads=[rN], writes=[dr])
                b.op('dve', lambda e: e.tensor_scalar(out=dn, in0=dn, scalar1=1.0, scalar2=None, op0=ALU.max), reads=[dr], writes=[dr])
                b.op('dve', lambda e: e.reciprocal(out=dn, in_=dn), reads=[dr], writes=[dr])
                if d == 0:
                    b.op('dve', lambda e: e.tensor_scalar(out=HA[:, t, :], in0=pN[:, 0:128], scalar1=dn, scalar2=None, op0=ALU.mult), reads=[rN, dr], writes=['HA'])
                else:
                    b.op('dve', lambda e: e.tensor_scalar(out=HS[:, t, :], in0=pN[:, 0:128], scalar1=dn, scalar2=None, op0=ALU.mult), reads=[rN, dr], writes=['HS'])
                if i < NTIL - 1:
                    col = t * 128 + (127 if d == 0 else 0)
                    dec = EBK[:, d * 2, col:col + 1]
                    b.op('dve', lambda e: e.tensor_tensor(out=CS[:, d, 0:129], in0=pU[:, 0:129], in1=CS[:, d, 0:129], op=ALU.add), reads=[rU, 'CS%d' % d], writes=['CS%d' % d])
                    b.op('dve', lambda e: e.tensor_scalar(out=CS[:, d, :], in0=CS[:, d, :], scalar1=dec, scalar2=None, op0=ALU.mult), reads=['CS%d' % d, 'EBK'], writes=['CS%d' % d])
                    b.op('act', lambda e: e.copy(out=CSb[:, d, :], in_=CS[:, d, :]), reads=['CS%d' % d], writes=['CSb%d' % d])
        if STOP == 5:
            b.barrier(); return
        b.op('dve', lambda e: e.tensor_tensor(out=HA[:], in0=HA[:], in1=HS[:], op=ALU.add), reads=['HA', 'HS'], writes=['HA'])
        b.op('dve', lambda e: e.tensor_tensor(out=HS[:], in0=HA[:], in1=HA[:], op=ALU.mult), reads=['HA'], writes=['HS'])
        ssq = ST[:, 8:8 + NTIL]
        b.op('dve', lambda e: e.tensor_reduce(out=ssq, in_=HS[:], axis=AX.X, op=ALU.add), reads=['HS'], writes=['STs'])
        b.op('dve', lambda e: e.tensor_scalar(out=ssq, in0=ssq, scalar1=1.0 / 128, scalar2=EPS, op0=ALU.mult, op1=ALU.add), reads=['STs'], writes=['STs'])
        b.op('act', lambda e: e.sqrt(out=ssq, in_=ssq), reads=['STs'], writes=['STs'])
        b.op('dve', lambda e: e.reciprocal(out=ssq, in_=ssq), reads=['STs'], writes=['STs'])
        for t in range(NTIL):
            b.op('dve', lambda e: e.scalar_tensor_tensor(out=HS[:, t, :], in0=HA[:, t, :], scalar=ST[:, 8 + t:9 + t], in1=HG[:, h * 128:(h + 1) * 128],
                                                         op0=ALU.mult, op1=ALU.mult), reads=['HA', 'STs', 'HG'], writes=['HS'])
        b.op('act', lambda e: e.activation(out=HA[:], in_=OS[:, h, :, :], func=AF.Exp, scale=-1.0), reads=['OS', 'HA'], writes=['HA'])
        b.op('dve', lambda e: e.tensor_scalar(out=HA[:], in0=HA[:], scalar1=1.0, scalar2=None, op0=ALU.add), reads=['HA'], writes=['HA'])
        b.op('dve', lambda e: e.reciprocal(out=HA[:], in_=HA[:]), reads=['HA'], writes=['HA'])
        b.op('dve', lambda e: e.tensor_tensor(out=HO[:], in0=HS[:], in1=HA[:], op=ALU.mult), reads=['HS', 'HA'], writes=['HO'])
        if callable(hmix):
            hmix(b, h * 128, HO, 'HO')
        else:
            b.dma('sp', hmix.rearrange("(t p) n -> p t n", p=128)[:, :, h * 128:(h + 1) * 128], HO[:], reads=['HO'], writes=[hmix_res], waw=False)
    b.barrier()


def emit_diff(b, c, ar, nc, PS, u_dram, w_dh, cos_d, sin_d, lam_p, subln_g, cmat, hmix, hmix_res, lam_init):
    P0 = SB_BASE
    QD, o = ar.at(P0, [128, 4, NTOK], BF16, "QD")
    KD, o = ar.at(o, [128, 4, NTOK], BF16, "KD")
    VD, o = ar.at(o, [128, NTIL, 512], BF16, "VD")
    LP, o = ar.at(o, [128, 4, 64], F32, "LP")
    SGN, o = ar.at(o, [128, 128], F32, "SGN")
    LAM, o = ar.at(o, [128, 8], F32, "LAM")
    ROT, o = ar.at(o, [128, 128], F32, "ROT")
    P1 = o
    b.dma('sp', LP[:], lam_p.partition_broadcast(128), writes=['LP'])
    b.dma('sp', SGN[:], subln_g.partition_broadcast(128), writes=['SGN'])
    b.dma('sp', ROT[:], cmat[:, 4, :], writes=['ROT'])
    b.op('dve', lambda e: e.tensor_tensor(out=LP[:, 0, :], in0=LP[:, 0, :], in1=LP[:, 1, :], op=ALU.mult), reads=['LP'], writes=['LP'])
    b.op('dve', lambda e: e.tensor_tensor(out=LP[:, 2, :], in0=LP[:, 2, :], in1=LP[:, 3, :], op=ALU.mult), reads=['LP'], writes=['LP'])
    b.op('dve', lambda e: e.tensor_reduce(out=LAM[:, 0:1], in_=LP[:, 0, :], axis=AX.X, op=ALU.add), reads=['LP'], writes=['LAM'])
    b.op('dve', lambda e: e.tensor_reduce(out=LAM[:, 1:2], in_=LP[:, 2, :], axis=AX.X, op=ALU.add), reads=['LP'], writes=['LAM'])
    b.op('act', lambda e: e.activation(out=LAM[:, 0:2], in_=LAM[:, 0:2], func=AF.Exp), reads=['LAM'], writes=['LAM'])
    b.op('dve', lambda e: e.tensor_tensor(out=LAM[:, 2:3], in0=LAM[:, 0:1], in1=LAM[:, 1:2], op=ALU.subtract), reads=['LAM'], writes=['LAM'])
    b.op('dve', lambda e: e.tensor_scalar(out=LAM[:, 2:3], in0=LAM[:, 2:3], scalar1=float(lam_init), scalar2=None, op0=ALU.add), reads=['LAM'], writes=['LAM'])
    b.op('dve', lambda e: e.tensor_scalar(out=LAM[:, 3:4], in0=LAM[:, 2:3], scalar1=-1.0, scalar2=None, op0=ALU.mult), reads=['LAM'], writes=['LAM'])

    uT, q = ar.at(P1, [128, KC, NTOK], BF16, "uT")
    WD = []
    for i in range(2):
        t_, q = ar.at(q, [128, KC, 384], BF16, "WD"); WD.append(t_)
    COS, q = ar.at(q, [128, 2048], F32, "COS")
    SIN, q = ar.at(q, [128, 2048], F32, "SIN")
    XF, q = ar.at(q, [128, 2, 512], F32, "XF")
    T1, q = ar.at(q, [128, 2, 512], F32, "T1")
    b.dma('sp', COS[:], cos_d, writes=['COS'])
    b.dma('sp', SIN[:], sin_d, writes=['SIN'])
    emit_load_uT(b, c, ar, q, u_dram, 'u', NTIL, uT, PS)
    blks = blocks_of(NTOK)
    it = 0
    for h in range(4):
        wd = WD[h % 2]; wr = 'WD%d' % (h % 2)
        b.dma('pool', wd[:], w_dh[h].rearrange("(c p) n -> p c n", p=128), reads=['w'], writes=[wr])
        for qk in range(2):
            dst = QD if qk == 0 else KD
            dres = 'QD' if qk == 0 else 'KD'
            for (s0, sl) in blks:
                par = it % 2; it += 1
                pb = PS.pb[par * 2]; pr = 'pb%d' % (par * 2)
                p2 = PS.pb[par * 2 + 1]; p2r = 'pb%d' % (par * 2 + 1)
                for k in range(KC):
                    b.op('pe', lambda e: e.matmul(pb[:, 0:sl], lhsT=wd[:, k, qk * 128:(qk + 1) * 128], rhs=uT[:, k, s0:s0 + sl],
                                                  start=(k == 0), stop=(k == KC - 1)), reads=[wr, 'uT'], writes=[pr])
                if s0 >= 2048:
                    b.op('act', lambda e: e.copy(out=dst[:, h, s0:s0 + sl], in_=pb[:, 0:sl]), reads=[pr], writes=[dres])
                    continue
                xf = XF[:, par, 0:sl]; xr = 'XF%d' % par
                t1 = T1[:, par, 0:sl]; tr = 'T1%d' % par
                b.op('act', lambda e: e.copy(out=xf, in_=pb[:, 0:sl]), reads=[pr], writes=[xr])
                b.op('pe', lambda e: e.matmul(p2[:, 0:sl], lhsT=ROT[:], rhs=xf, start=True, stop=True), reads=['ROT', xr], writes=[p2r])
                b.op('dve', lambda e: e.tensor_tensor(out=t1, in0=p2[:, 0:sl], in1=SIN[:, s0:s0 + sl], op=ALU.mult), reads=[p2r, 'SIN'], writes=[tr])
                b.op('dve', lambda e: e.tensor_tensor(out=xf, in0=xf, in1=COS[:, s0:s0 + sl], op=ALU.mult), reads=[xr, 'COS'], writes=[xr])
                b.op('dve', lambda e: e.tensor_tensor(out=dst[:, h, s0:s0 + sl], in0=xf, in1=t1, op=ALU.add), reads=[xr, tr], writes=[dres])
        for t in range(NTIL):
            pb = PS.pb[4 + t % 2]; pr = 'pb%d' % (4 + t % 2)
            for k in range(KC):
                b.op('pe', lambda e: e.matmul(pb[:, 0:128], lhsT=uT[:, k, t * 128:(t + 1) * 128], rhs=wd[:, k, 256:384], start=(k == 0), stop=(k == KC - 1)),
                     reads=['uT', wr], writes=[pr])
            if t % 2 == 0:
                b.op('dve', lambda e: e.tensor_copy(out=VD[:, t, h * 128:(h + 1) * 128], in_=pb[:, 0:128]), reads=[pr], writes=['VD'])
            else:
                b.op('act', lambda e: e.copy(out=VD[:, t, h * 128:(h + 1) * 128], in_=pb[:, 0:128]), reads=[pr], writes=['VD'])
    b.barrier()

    q = P1
    PF, q = ar.at(q, [128, 2, NTOK], F32, "PF")
    WB, q = ar.at(q, [128, 2, NTOK], BF16, "WB")
    WT, q = ar.at(q, [128, 2, NTIL, 128], BF16, "WT")
    HOD, q = ar.at(q, [128, NTIL, 128], BF16, "HOD")
    JK, q = ar.at(q, [128, 128], F32, "JK")
    OF, q = ar.at(q, [128, 128], F32, "OF")
    ST, q = ar.at(q, [128, 32], F32, "STD")
    SA = PS.pq[0]
    SB_ = PS.pb[4]
    n = 0
    for h in range(4):
        for qt in range(NTIL):
            lat = qt < 16
            par = n % 2; n += 1
            for g in range(2):
                qT = QD[g * 64:(g + 1) * 64, h, qt * 128:(qt + 1) * 128]
                if lat:
                    for kb in range(4):
                        b.op('pe', lambda e: e.matmul(SA[:, kb * 512:(kb + 1) * 512], lhsT=qT, rhs=KD[g * 64:(g + 1) * 64, h, kb * 512:(kb + 1) * 512],
                                                      start=True, stop=True), reads=['QD', 'KD'], writes=['pq0'])
                b.op('pe', lambda e: e.matmul(SB_[:, 0:256], lhsT=qT, rhs=KD[g * 64:(g + 1) * 64, h, 2048:2304], start=True, stop=True),
                     reads=['QD', 'KD'], writes=['pb4'])
                mx = ST[:, g * 8:g * 8 + 1]; mx2 = ST[:, g * 8 + 1:g * 8 + 2]; nb_ = ST[:, g * 8 + 2:g * 8 + 3]
                sa = ST[:, g * 8 + 3:g * 8 + 4]; sb2 = ST[:, g * 8 + 4:g * 8 + 5]
                sr = 'STD%d' % g
                b.op('dve', lambda e: e.tensor_reduce(out=mx2, in_=SB_[:, 0:256], axis=AX.X, op=ALU.max), reads=['pb4'], writes=[sr])
                if lat:
                    b.op('dve', lambda e: e.tensor_reduce(out=mx, in_=SA[:], axis=AX.X, op=ALU.max), reads=['pq0'], writes=[sr])
                    b.op('dve', lambda e: e.tensor_tensor(out=mx2, in0=mx, in1=mx2, op=ALU.max), reads=[sr], writes=[sr])
                b.op('dve', lambda e: e.tensor_scalar(out=nb_, in0=mx2, scalar1=-0.125, scalar2=None, op0=ALU.mult), reads=[sr], writes=[sr])
                pfr = 'PF%d' % g
                if lat:
                    b.op('act', lambda e: e.activation(out=PF[:, g, 0:2048], in_=SA[:], func=AF.Exp, bias=nb_, scale=0.125, accum_out=sa),
                         reads=['pq0', sr], writes=[pfr, sr + 'a'])
                b.op('act', lambda e: e.activation(out=PF[:, g, 2048:2304], in_=SB_[:, 0:256], func=AF.Exp, bias=nb_, scale=0.125, accum_out=sb2),
                     reads=['pb4', sr], writes=[pfr, sr + 'b'])
                if lat:
                    b.op('dve', lambda e: e.tensor_tensor(out=sb2, in0=sa, in1=sb2, op=ALU.add), reads=[sr + 'a', sr + 'b'], writes=[sr + 'b'])
                b.op('dve', lambda e: e.reciprocal(out=sb2, in_=sb2), reads=[sr + 'b'], writes=[sr + 'b'])
            r0 = ST[:, 4:5]; r1 = ST[:, 12:13]
            b.op('dve', lambda e: e.tensor_tensor(out=r1, in0=r1, in1=LAM[:, 3:4], op=ALU.mult), reads=['STD1b', 'LAM'], writes=['STD1b'])
            k0 = 0 if lat else 2048
            wb = WB[:, par, :]; wbr = 'WBd%d' % par
            b.op('dve', lambda e: e.tensor_scalar(out=PF[:, 1, k0:NTOK], in0=PF[:, 1, k0:NTOK], scalar1=r1, scalar2=None, op0=ALU.mult), reads=['PF1', 'STD1b'], writes=['PF1'])
            b.op('dve', lambda e: e.scalar_tensor_tensor(out=wb[:, k0:NTOK], in0=PF[:, 0, k0:NTOK], scalar=r0, in1=PF[:, 1, k0:NTOK], op0=ALU.mult, op1=ALU.add),
                 reads=['PF0', 'PF1', 'STD0b'], writes=[wbr])
            kts = list(range(NTIL)) if lat else [16, 17]
            wt = WT[:, par]; wtr = 'WT%d' % par
            gi = 0
            for g0 in range(0, len(kts), 8):
                grp = kts[g0:g0 + 8]
                pk = gi % 2; gi += 1
                p = PS.psT[pk]; pr = 'psT%d' % pk
                for cc, kt in enumerate(grp):
                    b.op('pe', lambda e: e.transpose(p[:, cc, :], wb[:, kt * 128:(kt + 1) * 128], c.idb[:]), reads=[wbr, 'idb'], writes=[pr])
                if pk == 0:
                    b.op('act', lambda e: e.copy(out=wt[:, grp[0]:grp[0] + len(grp), :], in_=p[:, 0:len(grp), :]), reads=[pr], writes=[wtr])
                else:
                    b.op('dve', lambda e: e.tensor_copy(out=wt[:, grp[0]:grp[0] + len(grp), :], in_=p[:, 0:len(grp), :]), reads=[pr], writes=[wtr])
            po = PS.pb[5]
            for i, kt in enumerate(kts):
                b.op('pe', lambda e: e.matmul(po[:, 0:128], lhsT=wt[:, kt, :], rhs=VD[:, kt, h * 128:(h + 1) * 128], start=(i == 0), stop=(i == len(kts) - 1)),
                     reads=[wtr, 'VD'], writes=['pb5'])
            ss = ST[:, 16:17]
            b.op('act', lambda e: e.activation(out=JK[:], in_=po[:, 0:128], func=AF.Square, accum_out=ss), reads=['pb5'], writes=['JK', 'STDn'])
            b.op('dve', lambda e: e.tensor_scalar(out=ss, in0=ss, scalar1=1.0 / 128, scalar2=EPS, op0=ALU.mult, op1=ALU.add), reads=['STDn'], writes=['STDn'])
            b.op('act', lambda e: e.sqrt(out=ss, in_=ss), reads=['STDn'], writes=['STDn'])
            b.op('dve', lambda e: e.reciprocal(out=ss, in_=ss), reads=['STDn'], writes=['STDn'])
            b.op('dve', lambda e: e.tensor_scalar(out=ss, in0=ss, scalar1=float(1.0 - lam_init), scalar2=None, op0=ALU.mult), reads=['STDn'], writes=['STDn'])
            b.op('dve', lambda e: e.scalar_tensor_tensor(out=HOD[:, qt, :], in0=po[:, 0:128], scalar=ss, in1=SGN[:], op0=ALU.mult, op1=ALU.mult),
                 reads=['pb5', 'STDn', 'SGN'], writes=['HOD'])
        if callable(hmix):
            hmix(b, 512 + h * 128, HOD, 'HOD')
        else:
            b.dma('sp', hmix.rearrange("(t p) n -> p t n", p=128)[:, :, 512 + h * 128:512 + (h + 1) * 128], HOD[:], reads=['HOD'], writes=[hmix_res], waw=False)
    b.barrier()


def emit_na(b, c, ar, nc, PS, u_dram, w_nh, tt_d, hmix, hmix_res):
    P0 = SB_BASE
    uT, o = ar.at(P0, [128, KC, NTOK], BF16, "uT")
    QN, o = ar.at(o, [128, 4, 2048], BF16, "QN")
    KN, o = ar.at(o, [128, 4, NTOK], BF16, "KN")
    VN, o = ar.at(o, [128, NTIL, 512], BF16, "VN")
    VS, o = ar.at(o, [128, 15, 512], BF16, "VS")
    TT, o = ar.at(o, [64, 4, 16, 64], F32, "TT")
    WN = []
    for i in range(2):
        t_, o = ar.at(o, [128, KC, 384], BF16, "WN"); WN.append(t_)
    SWB, o = ar.at(o, [64, 2, 512], F32, "SWB")
    PW, o = ar.at(o, [64, 2, 768], BF16, "PW")
    PT, o = ar.at(o, [128, 2, 6, 64], BF16, "PT")
    HOUT, o = ar.at(o, [64, 32, 128], BF16, "HOUT")
    ST, o = ar.at(o, [64, 32], F32, "STN")
    emit_load_uT(b, c, ar, o, u_dram, 'u', NTIL, uT, PS)
    blks = blocks_of(NTOK)
    it = 0
    n = 0
    for grp in range(2):
        b.dma('sp', TT[:], tt_d[grp * 4:(grp + 1) * 4].rearrange("h w d c -> w h d c"), reads=['w'], writes=['TT'])
        for h in range(4):
            H = grp * 4 + h
            wn = WN[H % 2]; wr = 'WN%d' % (H % 2)
            b.dma('pool', wn[:], w_nh[H].rearrange("(c p) n -> p c n", p=128), reads=['w'], writes=[wr])
            for qk in range(2):
                for (s0, sl) in blks:
                    if qk == 0 and s0 >= 2048:
                        continue
                    pb = PS.pb[it % 4]; pr = 'pb%d' % (it % 4); it += 1
                    for k in range(KC):
                        b.op('pe', lambda e: e.matmul(pb[:, 0:sl], lhsT=wn[:, k, qk * 128:(qk + 1) * 128], rhs=uT[:, k, s0:s0 + sl],
                                                      start=(k == 0), stop=(k == KC - 1)), reads=[wr, 'uT'], writes=[pr])
                    if qk == 0:
                        b.op('act', lambda e: e.mul(out=QN[:, h, s0:s0 + sl], in_=pb[:, 0:sl], mul=128.0 ** -0.5), reads=[pr], writes=['QN'])
                    else:
                        b.op('dve', lambda e: e.tensor_copy(out=KN[:, h, s0:s0 + sl], in_=pb[:, 0:sl]), reads=[pr], writes=['KN'])
            for t in range(NTIL + 15):
                pb = PS.pb[4 + t % 2]; pr = 'pb%d' % (4 + t % 2)
                if t < NTIL:
                    tok0 = t * 128; dst = VN[:, t, h * 128:(h + 1) * 128]; dr = 'VN'
                else:
                    tok0 = 64 + (t - NTIL) * 128; dst = VS[:, t - NTIL, h * 128:(h + 1) * 128]; dr = 'VS'
                for k in range(KC):
                    b.op('pe', lambda e: e.matmul(pb[:, 0:128], lhsT=uT[:, k, tok0:tok0 + 128], rhs=wn[:, k, 256:384], start=(k == 0), stop=(k == KC - 1)),
                         reads=['uT', wr], writes=[pr])
                if t % 2 == 0:
                    b.op('dve', lambda e: e.tensor_copy(out=dst, in_=pb[:, 0:128]), reads=[pr], writes=[dr])
                else:
                    b.op('act', lambda e: e.copy(out=dst, in_=pb[:, 0:128]), reads=[pr], writes=[dr])
        if STOP == 30:
            b.barrier(); return
        for h in range(4):
            H = grp * 4 + h
            for r in range(32):
                par = n % 2; n += 1
                rs = min(max(r - 4, 0), 24)
                dr0 = rs - r + 7
                pA = PS.pb[par]; rA = 'pb%d' % par
                pB = PS.pb[2 + par]; rB = 'pb%d' % (2 + par)
                pO = PS.pb[4 + par]; rO = 'pb%d' % (4 + par)
                p = PS.psT[par]; rT = 'psT%d' % par
                qT = QN[:, h, r * 64:(r + 1) * 64]
                b.op('pe', lambda e: e.matmul(pA[0:64, 0:512], lhsT=qT, rhs=KN[:, h, rs * 64:rs * 64 + 512], start=True, stop=True), reads=['QN', 'KN'], writes=[rA])
                b.op('pe', lambda e: e.matmul(pB[0:64, 0:256], lhsT=qT, rhs=KN[:, h, 2048:2304], start=True, stop=True), reads=['QN', 'KN'], writes=[rB])
                sw = SWB[:, par, :]; swr = 'SWB%d' % par
                b.op('dve', lambda e: e.tensor_tensor(out=sw.rearrange("p (a c) -> p a c", c=64), in0=pA[0:64, 0:512].rearrange("p (a c) -> p a c", c=64),
                                                      in1=TT[:, h, dr0:dr0 + 8, :], op=ALU.add), reads=[rA, 'TT'], writes=[swr])
                m1 = ST[:, par * 8:par * 8 + 1]; m2 = ST[:, par * 8 + 1:par * 8 + 2]; nb_ = ST[:, par * 8 + 2:par * 8 + 3]
                s1 = ST[:, par * 8 + 3:par * 8 + 4]; s2 = ST[:, par * 8 + 4:par * 8 + 5]
                sr = 'STN%d' % par
                b.op('dve', lambda e: e.tensor_reduce(out=m1, in_=sw, axis=AX.X, op=ALU.max), reads=[swr], writes=[sr])
                b.op('dve', lambda e: e.tensor_reduce(out=m2, in_=pB[0:64, 0:256], axis=AX.X, op=ALU.max), reads=[rB], writes=[sr])
                b.op('dve', lambda e: e.tensor_tensor(out=m2, in0=m1, in1=m2, op=ALU.max), reads=[sr], writes=[sr])
                b.op('dve', lambda e: e.tensor_scalar(out=nb_, in0=m2, scalar1=-1.0, scalar2=None, op0=ALU.mult), reads=[sr], writes=[sr])
                pw = PW[:, par, :]; pwr = 'PW%d' % par
                b.op('act', lambda e: e.activation(out=pw[:, 0:512], in_=sw, func=AF.Exp, bias=nb_, scale=1.0, accum_out=s1), reads=[swr, sr], writes=[pwr, sr + 'a'])
                b.op('act', lambda e: e.activation(out=pw[:, 512:768], in_=pB[0:64, 0:256], func=AF.Exp, bias=nb_, scale=1.0, accum_out=s2), reads=[rB, sr], writes=[pwr, sr + 'b'])
                b.op('dve', lambda e: e.tensor_tensor(out=s2, in0=s1, in1=s2, op=ALU.add), reads=[sr + 'a', sr + 'b'], writes=[sr + 'b'])
                b.op('dve', lambda e: e.reciprocal(out=s2, in_=s2), reads=[sr + 'b'], writes=[sr + 'b'])
                if STOP == 31:
                    continue
                for cc in range(6):
                    b.op('pe', lambda e: e.transpose(p[:, cc, 0:64], pw[:, cc * 128:(cc + 1) * 128], c.idb[0:64, 0:64]), reads=[pwr, 'idb'], writes=[rT])
                if STOP == 32:
                    continue
                pt = PT[:, par]; ptr_ = 'PT%d' % par
                b.op('act', lambda e: e.copy(out=pt, in_=p[:, 0:6, 0:64]), reads=[rT], writes=[ptr_])
                if STOP == 33:
                    continue
                for cc in range(6):
                    if cc < 4:
                        if rs % 2 == 0:
                            rhs = VN[:, rs // 2 + cc, h * 128:(h + 1) * 128]; vr = 'VN'
                        else:
                            rhs = VS[:, (rs - 1) // 2 + cc, h * 128:(h + 1) * 128]; vr = 'VS'
                    else:
                        rhs = VN[:, 16 + cc - 4, h * 128:(h + 1) * 128]; vr = 'VN'
                    b.op('pe', lambda e: e.matmul(pO[0:64, 0:128], lhsT=pt[:, cc, :], rhs=rhs, start=(cc == 0), stop=(cc == 5)), reads=[ptr_, vr], writes=[rO])
                if STOP == 34:
                    continue
                b.op('dve', lambda e: e.tensor_scalar(out=HOUT[:, r, :], in0=pO[0:64, 0:128], scalar1=s2, scalar2=None, op0=ALU.mult), reads=[rO, sr + 'b'], writes=['HOUT'])
            if callable(hmix):
                hmix(b, H * 128, HOUT, 'HOUT')
            else:
                b.dma('sp', hmix.rearrange("(r w) n -> w r n", w=64)[:, :, H * 128:(H + 1) * 128], HOUT[:], reads=['HOUT'], writes=[hmix_res], waw=False)
    b.barrier()


class PSum:
    def __init__(self, nc, es):
        big = [es.enter_context(nc.psum_tensor("pq%d" % i, [128, 2048], F32)) for i in range(2)]
        self.pq = big
        self.pb = [big[i // 4][:, (i % 4) * 512:(i % 4 + 1) * 512] for i in range(8)]
        self.psT = [self.pb[6 + i].bitcast(BF16).rearrange("p (c n) -> p c n", n=128) for i in range(2)]


def emit_mod(b, c, ar, nc, PS, cvec, w_ada_l, b_ada_l, modrow, mres):
    o = SB_BASE
    CT, o = ar.at(o, [128, KC, 2], F32, "CT")
    WA = []
    for i in range(2):
        t_, o = ar.at(o, [128, KC, 512], F32, "WA"); WA.append(t_)
    BA, o = ar.at(o, [2, 4608], F32, "BA")
    MR, o = ar.at(o, [2, 4608], F32, "MR")
    for r in range(2):
        b.dma('sp', CT[:, :, r:r + 1], cvec[r].rearrange("(c p o) -> p c o", p=128, o=1), writes=['CT'], waw=False, allow_slow_non_contiguous=True)
    b.op('act', lambda e: e.activation(out=CT[:], in_=CT[:], func=AF.Silu), reads=['CT'], writes=['CT'])
    wv = w_ada_l.rearrange("(c p) n -> p c n", p=128)
    for qq in range(4):
        b.dma('sp', BA[:], b_ada_l[qq * 4608:(qq + 1) * 4608].partition_broadcast(2), writes=['BA'])
        for j in range(9):
            nb = qq * 9 + j
            wa = WA[nb % 2]; wr = 'WA%d' % (nb % 2)
            b.dma('sp' if nb % 2 == 0 else 'pool', wa[:], wv[:, :, nb * 512:(nb + 1) * 512], reads=['w'], writes=[wr])
            pb = PS.pb[nb % 2]; pr = 'pb%d' % (nb % 2)
            for k in range(KC):
                b.op('pe', lambda e: e.matmul(pb[0:2, 0:512], lhsT=CT[:, k, :], rhs=wa[:, k, :], start=(k == 0), stop=(k == KC - 1)), reads=['CT', wr], writes=[pr])
            b.op('dve', lambda e: e.tensor_tensor(out=MR[:, j * 512:(j + 1) * 512], in0=pb[0:2, 0:512], in1=BA[:, j * 512:(j + 1) * 512], op=ALU.add),
                 reads=[pr, 'BA'], writes=['MR'])
        b.dma('sp', modrow[:, qq * 4608:(qq + 1) * 4608], MR[:], reads=['MR'], writes=[mres], waw=False)
    b.barrier()


def emit_ffn(b, c, ar, nc, PS, x_dram, x_res, xo_dram, xo_res, tiles, T, modrow, g_pre, g_post, imod, w_in, w_out):
    actT, R1 = ar.at(SB_BASE, [128, FC, T], BF16, "actT")
    uT, e1 = ar.at(R1, [128, KC, T], BF16, "uT")
    emit_pre(b, c, ar, nc, e1, x_dram, x_res, tiles, modrow, g_pre, imod[0], imod[1], uT=uT, ps=PS.psT)
    b.barrier()
    SG, e2 = ar.at(e1 + 4 * 8192, [128, T], F32, "SG")
    emit_ffn_m1(b, ar, nc, e1, uT, T, w_in, 'w', actT, PS.pb, SG)
    b.barrier()
    outT, e3 = ar.at(R1, [128, KC, T], F32, "outT")
    emit_linear_T(b, ar, nc, e3, actT, 'actT', FC, T, w_out, 'w', D, outT, 'outT', PS.pb, colblk=128, nbuf=3)
    b.barrier()
    emit_post(b, c, ar, nc, SB_BASE, outT, 'outT', x_dram, x_res, xo_dram, xo_res, tiles, modrow, g_post, imod[2], 0.5, PS.pq)
    b.barrier()


def emit_premix(b, c, ar, nc, PS, x_dram, x_res, u_dram, u_res, tiles, modrow, g_pre):
    emit_pre(b, c, ar, nc, SB_BASE, x_dram, x_res, tiles, modrow, g_pre, 3, 4, uT=None, u_out=u_dram, u_res=u_res)
    b.barrier()


def emit_postmix(b, c, ar, nc, PS, h_dram, h_res, w_mix, x_dram, x_res, xo_dram, xo_res, tiles, T, modrow, g_post):
    hT, o = ar.at(SB_BASE, [128, KC, T], BF16, "hT")
    outT, o2 = ar.at(o, [128, KC, T], F32, "outT")
    off = o2
    UB = []
    for i in range(2):
        t_, off = ar.at(off, [128, D], BF16, "HBl"); UB.append(t_)
    off2 = off
    off = off + 3 * 4096 + 64
    for n, (t, mr) in enumerate(tiles):
        ub = UB[n % 2]; ur = 'HBl%d' % (n % 2)
        if callable(h_dram):
            h_dram(b, n, ub, ur, off2)
        else:
            b.dma('sp', ub[:], h_dram[t * 128:(t + 1) * 128, :], reads=[h_res], writes=[ur])
        emit_transpose_tile(b, c, ub, ur, hT, 'hT', t, KC, PS.psT, n)
    emit_linear_T(b, ar, nc, off, hT, 'hT', KC, T, w_mix, 'w', D, outT, 'outT', PS.pb, colblk=256, nbuf=2)
    b.barrier()
    emit_post(b, c, ar, nc, o2, outT, 'outT', x_dram, x_res, xo_dram, xo_res, tiles, modrow, g_post, 5, 1.0, PS.pq)
    b.barrier()


import numpy as np
import ml_dtypes
BF = ml_dtypes.bfloat16

def cmat():
    s = np.arange(128)[:, None]; t = np.arange(128)[None, :]
    cm = np.zeros((128, 5, 128), np.float32)
    cm[:, 0] = (s == t)
    cm[:, 1] = (s <= t)
    cm[:, 2] = (s >= t)
    cm[:, 3] = 1.0
    for p2 in range(128):
        j = (p2 // 16) % 2
        if j == 0:
            cm[p2 + 16, 4, p2] = -1.0
        else:
            cm[p2 - 16, 4, p2] = 1.0
    return cm

def pack_even_mlstm(z, hh):
    W = z['w_in_even'][0]
    Hs = [4 * hh + h for h in range(4)]
    w_mh = np.stack([np.concatenate([W[:, g * 1024 + H * 128: g * 1024 + (H + 1) * 128] for g in range(4)], 1) for H in Hs], 0)
    w_g = np.stack([W[:, 4096 + g * 8 + H] for g in range(4) for H in Hs], 1)
    conv = z['mlstm_conv'][0]
    cw = np.zeros((128, 8, 5), np.float32)
    for h, H in enumerate(Hs):
        cw[:, h * 2 + 0, :] = conv[:, H * 128:(H + 1) * 128].T
        cw[:, h * 2 + 1, :] = conv[:, 1024 + H * 128:1024 + (H + 1) * 128].T
    gb = np.array([z['mlstm_gate_b'][0][g, H] for g in range(4) for H in Hs], np.float32)
    hg = z['mlstm_head_g'][0][4 * hh * 128:(4 * hh + 4) * 128]
    return dict(w_mh=np.ascontiguousarray(w_mh), w_g=np.ascontiguousarray(w_g), conv_w=cw, gate_b=gb, head_g=np.ascontiguousarray(hg))

def pack_even_diff(z, hh):
    W = z['w_in_even'][0]
    Hs = [4 * hh + h for h in range(4)]
    base = 4128
    w_dh = np.stack([np.concatenate([W[:, base + g * 1024 + H * 128: base + g * 1024 + (H + 1) * 128] for g in range(3)], 1) for H in Hs], 0)
    return dict(w_dh=np.ascontiguousarray(w_dh), lam_p=np.ascontiguousarray(z['diff_lambda'][0].reshape(-1)), subln_g=z['diff_subln_g'][0])

def rope_tables():
    t = np.arange(2048)
    row = (t // 64).astype(np.float32); col = (t % 64).astype(np.float32)
    inv = (np.float32(10000.0) ** (-np.arange(16, dtype=np.float32) / 16)).astype(np.float32)
    ang = np.stack([row[:, None] * inv, col[:, None] * inv], 1)
    cos = np.zeros((128, 2048), np.float32); sin = np.zeros((128, 2048), np.float32)
    for p in range(128):
        A = (p // 32) % 2; i = p % 16
        cos[p] = np.cos(ang[:, A, i]); sin[p] = np.sin(ang[:, A, i])
    return cos, sin

NEG = -30000.0
def pack_na(z, hh):
    W = z['w_in_odd'][0]
    Hs = [8 * hh + h for h in range(8)]
    w_nh = np.stack([np.concatenate([W[:, g * 2048 + H * 128: g * 2048 + (H + 1) * 128] for g in range(3)], 1) for H in Hs], 0)
    rpb = z['na_rpb'][0]
    w = np.arange(64)
    c_start = np.clip(w - 8, 0, 48)
    kc = np.arange(64)
    inwin = (kc[None, :] >= c_start[:, None]) & (kc[None, :] < c_start[:, None] + 16)
    cidx = np.clip(kc[None, :] - w[:, None] + 15, 0, 30)
    tt = np.full((8, 64, 16, 64), NEG, np.float32)
    for i, H in enumerate(Hs):
        for dr in range(15):
            vals = rpb[H, dr][cidx]
            tt[i, :, dr, :] = np.where(inwin, vals, np.float32(NEG))
    return dict(w_nh=np.ascontiguousarray(w_nh), tt=tt)


TX = 1152
TILES9 = [(t, 0) for t in range(8)] + [(8, 1)]
TILES8 = [(t, 0) for t in range(8)]
PAIRS = [[0, 1], [2, 3], [4, 5], [6, 7]]


def _din(nc, name, shape, dt=F32):
    return nc.dram_tensor(name, list(shape), dt, kind="ExternalInput").ap()


def _dout(nc, name, shape, dt=F32):
    return nc.dram_tensor(name, list(shape), dt, kind="ExternalOutput").ap()


class Exch:
    def __init__(self, nc, name, ntile, tpc, ncol):
        self.name = name; self.ntile = ntile; self.tpc = tpc; self.ncol = ncol
        self.nch = (ntile + tpc - 1) // tpc
        self.src = [nc.dram_tensor("%s_s%d" % (name, k), [tpc * 128, ncol], BF16) for k in range(self.nch)]
        self.dst = [nc.dram_tensor("%s_g%d" % (name, k), [2 * tpc * 128, ncol], BF16) for k in range(self.nch)]

    def sres(self, k):
        return "%s_s%d" % (self.name, k)

    def gres(self, k):
        return "%s_g%d" % (self.name, k)

    def local_tile(self, t):
        k = t // self.tpc
        r0 = (t % self.tpc) * 128
        return self.src[k].ap()[r0:r0 + 128, :], self.sres(k)

    def gathered_tile(self, rank, t):
        k = t // self.tpc
        r0 = rank * self.tpc * 128 + (t % self.tpc) * 128
        return self.dst[k].ap()[r0:r0 + 128, :], self.gres(k)

    def gather(self, b):
        for k in range(self.nch):
            b.allgather(self.src[k], self.dst[k], PAIRS, [self.sres(k)], [self.gres(k)])


def build_fused():
    nc = bass.Bass("TRN2", target_bir_lowering=False)
    x = _din(nc, "x", [TX, D]); cvec = _din(nc, "cvec", [2, D]); sel = _din(nc, "sel", [128, 2])
    w_ada = _din(nc, "w_ada", [2, D, 9 * D]); b_ada = _din(nc, "b_ada", [2, 9 * D]); g = _din(nc, "g", [2, 6, D])
    w_in = _din(nc, "w_in", [2, 2, D, 2 * DFF]); w_out = _din(nc, "w_out", [2, 2, DFF, D]); ident = _din(nc, "ident", [128, 128])
    w_mh = _din(nc, "w_mh", [4, D, 512]); w_g = _din(nc, "w_g", [D, 16]); conv_w = _din(nc, "conv_w", [128, 8, 5])
    gate_b = _din(nc, "gate_b", [16]); head_g = _din(nc, "head_g", [512]); cm = _din(nc, "cmat", [128, 5, 128])
    w_dh = _din(nc, "w_dh", [4, D, 384]); cos_d = _din(nc, "cos", [128, 2048]); sin_d = _din(nc, "sin", [128, 2048])
    lam_p = _din(nc, "lam_p", [256]); subln_g = _din(nc, "subln_g", [128])
    w_mix0 = _din(nc, "w_mix0", [D, D]); w_mix1 = _din(nc, "w_mix1", [D, D])
    w_nh = _din(nc, "w_nh", [8, D, 384]); tt = _din(nc, "tt", [8, 64, 16, 64])
    y = _dout(nc, "y", [1024, D])
    modrow0 = nc.dram_tensor("modrow0", [2, 9 * D], F32).ap(); modrow1 = nc.dram_tensor("modrow1", [2, 9 * D], F32).ap()
    xs = [nc.dram_tensor("xs%d" % i, [TX, D], F32).ap() for i in range(6)]
    EU0 = Exch(nc, "eu0", 9, 3, D); EU1 = Exch(nc, "eu1", 9, 3, D)
    EH0 = Exch(nc, "eh0", 18, 6, 1024); EH1 = Exch(nc, "eh1", 16, 4, 1024)

    def full_u(E):
        def f(T):
            if T < 8:
                return E.gathered_tile(0, T)
            if T < 16:
                return E.gathered_tile(1, T - 8)
            return E.gathered_tile(T - 16, 8)
        return f

    def h_writer0(b, col0, HO, hres):
        for k in range(EH0.nch):
            dst = EH0.src[k].ap().rearrange("(t p) n -> p t n", p=128)[:, :, col0:col0 + 128]
            b.dma('sp', dst, HO[:, k * 6:(k + 1) * 6, :], reads=[hres], writes=[EH0.sres(k)], waw=False)

    def h_writer1(b, col0, HOUT, hres):
        for k in range(EH1.nch):
            dst = EH1.src[k].ap().rearrange("(r w) n -> w r n", w=64)[:, :, col0:col0 + 128]
            b.dma('sp', dst, HOUT[:, k * 8:(k + 1) * 8, :], reads=[hres], writes=[EH1.sres(k)], waw=False)

    with ExitStack() as es:
        b = Bld(nc, es); ar = Arena(nc); PS = PSum(nc, es)
        c = setup_consts(b, ar, nc, ident, CONST_OFF)
        SEL, _ = ar.at(c.off, [128, 2], F32, "SEL")
        b.dma('sp', SEL[:], sel, writes=['SEL'])

        def h_loader(E, layer):
            def f(b, n, ub, ur, off2):
                A, o = ar.at(off2, [128, D], BF16, "hA")
                B, o = ar.at(o, [128, D], BF16, "hB")
                for (cand, buf, br) in ((0, A, 'hA'), (1, B, 'hB')):
                    if layer == 0:
                        T = (n if n < 8 else 16) if cand == 0 else (8 + n if n < 8 else 17)
                        for r in range(2):
                            sap, sres = E.gathered_tile(r, T)
                            b.dma('sp', buf[:].rearrange("p (k r c) -> p k r c", k=2, r=2, c=512)[:, :, r, :], sap.rearrange("p (k c) -> p k c", k=2),
                                  reads=[sres], writes=[br], waw=False)
                    else:
                        T = n if cand == 0 else 8 + n
                        for r in range(2):
                            sap, sres = E.gathered_tile(r, T)
                            b.dma('sp', buf[:, r * 1024:(r + 1) * 1024], sap, reads=[sres], writes=[br], waw=False)
                b.op('dve', lambda e: e.tensor_scalar(out=A[:], in0=A[:], scalar1=SEL[:, 0:1], scalar2=None, op0=ALU.mult), reads=['hA', 'SEL'], writes=['hA'])
                b.op('dve', lambda e: e.scalar_tensor_tensor(out=ub[:], in0=B[:], scalar=SEL[:, 1:2], in1=A[:], op0=ALU.mult, op1=ALU.add),
                     reads=['hB', 'hA', 'SEL'], writes=[ur])
            return f

        emit_mod(b, c, ar, nc, PS, cvec, w_ada[0], b_ada[0], modrow0, 'modrow')
        emit_ffn(b, c, ar, nc, PS, x, 'x', xs[0], 'xs0', TILES9, TX, modrow0, g[0, 0], g[0, 1], (0, 1, 2), w_in[0, 0], w_out[0, 0])
        emit_premix(b, c, ar, nc, PS, xs[0], 'xs0', EU0.local_tile, None, TILES9, modrow0, g[0, 2])
        EU0.gather(b)
        emit_mlstm(b, c, ar, nc, PS, full_u(EU0), w_mh, w_g, conv_w, gate_b, head_g, cm, h_writer0, None)
        emit_diff(b, c, ar, nc, PS, full_u(EU0), w_dh, cos_d, sin_d, lam_p, subln_g, cm, h_writer0, None, 0.2)
        EH0.gather(b)
        emit_postmix(b, c, ar, nc, PS, h_loader(EH0, 0), None, w_mix0, xs[0], 'xs0', xs[1], 'xs1', TILES9, TX, modrow0, g[0, 3])
        emit_ffn(b, c, ar, nc, PS, xs[1], 'xs1', xs[2], 'xs2', TILES9, TX, modrow0, g[0, 4], g[0, 5], (6, 7, 8), w_in[0, 1], w_out[0, 1])
        emit_mod(b, c, ar, nc, PS, cvec, w_ada[1], b_ada[1], modrow1, 'modrow')
        emit_ffn(b, c, ar, nc, PS, xs[2], 'xs2', xs[3], 'xs3', TILES9, TX, modrow1, g[1, 0], g[1, 1], (0, 1, 2), w_in[1, 0], w_out[1, 0])
        emit_premix(b, c, ar, nc, PS, xs[3], 'xs3', EU1.local_tile, None, TILES9, modrow1, g[1, 2])
        EU1.gather(b)
        emit_na(b, c, ar, nc, PS, full_u(EU1), w_nh, tt, h_writer1, None)
        EH1.gather(b)
        emit_postmix(b, c, ar, nc, PS, h_loader(EH1, 1), None, w_mix1, xs[3], 'xs3', xs[4], 'xs4', TILES8, 1024, modrow1, g[1, 3])
        emit_ffn(b, c, ar, nc, PS, xs[4], 'xs4', y, 'y', TILES8, 1024, modrow1, g[1, 4], g[1, 5], (6, 7, 8), w_in[1, 1], w_out[1, 1])
        b.finish(['y'])
        print('fused program: ninst', b.ninst, 'nwaits', b.nwaits, 'sems', len(b.semh), 'cnt', {k: v for k, v in b.cnt.items() if isinstance(k, str)})
    return nc


def kernel(x, c, ctx, c_ctx, w_ada, b_ada, norm_g, w_ffn_in, w_ffn_out, w_in_even, w_out_even,
           mlstm_conv, mlstm_gate_b, mlstm_head_g, diff_lambda, diff_subln_g, w_in_odd, w_out_odd, na_rpb):
    z = dict(x=x, c=c, ctx=ctx, c_ctx=c_ctx, w_ada=w_ada, b_ada=b_ada, norm_g=norm_g, w_ffn_in=w_ffn_in, w_ffn_out=w_ffn_out,
             w_in_even=w_in_even, w_out_even=w_out_even, mlstm_conv=mlstm_conv, mlstm_gate_b=mlstm_gate_b, mlstm_head_g=mlstm_head_g,
             diff_lambda=diff_lambda, diff_subln_g=diff_subln_g, w_in_odd=w_in_odd, w_out_odd=w_out_odd, na_rpb=na_rpb)
    z = {k: np.ascontiguousarray(np.asarray(v, dtype=np.float32)) for k, v in z.items()}
    ident = np.eye(128, dtype=np.float32)
    ncores = 8
    cm = cmat(); cos, sin = rope_tables()
    packs = []
    for hh in range(2):
        m = pack_even_mlstm(z, hh); m.update(pack_even_diff(z, hh)); m.update(pack_na(z, hh))
        packs.append(m)
    maps = []
    for cid in range(ncores):
        bb, hh = cid // 2, cid % 2
        xt = np.concatenate([z['x'][bb, hh * 1024:(hh + 1) * 1024], z['ctx'][bb, hh * 128:(hh + 1) * 128]], 0)
        selv = np.zeros((128, 2), np.float32); selv[:, hh] = 1.0
        m = dict(x=np.ascontiguousarray(xt), cvec=np.stack([z['c'][bb], z['c_ctx']], 0), sel=selv, w_ada=z['w_ada'], b_ada=z['b_ada'], g=z['norm_g'],
                 w_in=z['w_ffn_in'], w_out=z['w_ffn_out'], ident=ident, cmat=cm, cos=cos, sin=sin,
                 w_mix0=z['w_out_even'][0], w_mix1=z['w_out_odd'][0])
        m.update(packs[hh])
        maps.append(m)
    res = run_bass_kernel_spmd(build_fused(), maps, core_ids=list(range(ncores)))
    out = np.zeros((4, 2048, D), np.float32)
    for cid in range(ncores):
        bb, hh = cid // 2, cid % 2
        out[bb, hh * 1024:(hh + 1) * 1024] = res.results[cid]['y']
    return out
```

```python
import numpy as np
import concourse.bass as bass
import concourse.mybir as mybir
from concourse.bass_utils import run_bass_kernel_spmd
from contextlib import ExitStack

F32 = mybir.dt.float32
BF16 = mybir.dt.bfloat16
AF = mybir.ActivationFunctionType
ALU = mybir.AluOpType
AX = mybir.AxisListType

SAME_ENGINE_SYNC = True


class Bld:
    def __init__(self, nc, es):
        self.nc = nc
        self.es = es
        self.eng = {'pe': nc.tensor, 'act': nc.scalar, 'dve': nc.vector, 'pool': nc.gpsimd, 'sp': nc.sync}
        self.semh = {}
        self.cnt = {}
        for k in ['pe', 'act', 'dve', 'pool']:
            self.semh[k] = es.enter_context(nc.semaphore("s_" + k))
            self.cnt[k] = 0
        self.seen = {k: {} for k in self.eng}
        self.lastw = {}
        self.readers = {}
        self.nwaits = 0
        self.ninst = 0

    def _deps(self, reads, writes, waw=True):
        need = {}
        def add(t):
            if t is None:
                return
            s, v = t
            if need.get(s, 0) < v:
                need[s] = v
        for r in reads:
            add(self.lastw.get(r))
        for w in writes:
            if waw:
                add(self.lastw.get(w))
            for s, v in self.readers.get(w, {}).items():
                add((s, v))
        return need

    def _wait(self, e, need):
        for s, v in need.items():
            if s == e and (e == 'pe' or not SAME_ENGINE_SYNC):
                continue
            if self.seen[e].get(s, 0) >= v:
                continue
            self.eng[e].wait_ge(self.semh[s], v)
            self.seen[e][s] = v
            self.nwaits += 1

    def _commit(self, tok, reads, writes):
        s, v = tok
        for r in reads:
            d = self.readers.setdefault(r, {})
            if d.get(s, 0) < v:
                d[s] = v
        for w in writes:
            self.lastw[w] = tok
            self.readers[w] = {}

    def op(self, e, fn, reads=(), writes=()):
        need = self._deps(reads, writes)
        self._wait(e, need)
        inst = fn(self.eng[e])
        self.cnt[e] += 1
        inst.then_inc(self.semh[e], 1)
        self._commit((e, self.cnt[e]), reads, writes)
        self.ninst += 1
        return inst

    def dma(self, q, out, in_, reads=(), writes=(), stream=None, waw=True, **kw):
        need = self._deps(reads, writes, waw=waw)
        self._wait(q, need)
        if stream is None:
            stream = writes[0]
        key = ('dma', stream)
        if key not in self.semh:
            self.semh[key] = self.es.enter_context(self.nc.semaphore("d%d" % len(self.semh)))
            self.cnt[key] = 0
        inst = self.eng[q].dma_start(out=out, in_=in_, **kw)
        self.cnt[key] += 16
        inst.then_inc(self.semh[key], 16)
        self._commit((key, self.cnt[key]), reads, writes)
        self.ninst += 1
        return inst

    def finish(self, resources, e='sp'):
        need = {}
        for r in resources:
            t = self.lastw.get(r)
            if t is not None and need.get(t[0], 0) < t[1]:
                need[t[0]] = t[1]
        self._wait(e, need)


def _barrier(self):
    for e in self.eng:
        need = {s: v for s, v in self.cnt.items() if v > 0}
        self._wait(e, need)
Bld.barrier = _barrier


def _allgather(self, src_t, dst_t, groups, reads, writes):
    need = self._deps(reads, writes)
    self._wait('pool', need)
    key = ('cc', 'cc')
    if key not in self.semh:
        self.semh[key] = self.es.enter_context(self.nc.semaphore("ccsem"))
        self.cnt[key] = 0
    inst = self.nc.gpsimd.collective_compute("AllGather", ALU.bypass, replica_groups=groups, ins=[src_t.ap().opt()], outs=[dst_t.ap().opt()])
    self.cnt[key] += 1
    inst.then_inc(self.semh[key])
    self._commit((key, self.cnt[key]), reads, writes)
    self.ninst += 1
    return inst
Bld.allgather = _allgather


D = 2048
DFF = 5632
KC = D // 128
FC = DFF // 128
EPS = 1e-6

SB_BASE = 16512
SB_TOTAL = 229344
CONST_OFF = SB_TOTAL - 1152


class Arena:
    def __init__(self, nc):
        self.nc = nc
        self.n = 0
    def at(self, off, shape, dt, name=None):
        self.n += 1
        nm = "%s_%d" % (name or "t", self.n)
        esz = 4 if dt == F32 else 2
        sz = esz
        for s in shape[1:]:
            sz *= s
        assert off % 32 == 0 and off + sz <= SB_TOTAL, (nm, off, sz)
        return self.nc.alloc_sbuf_tensor_at(nm, list(shape), dt, offset=off), (off + sz + 31) // 32 * 32


def blocks_of(T):
    nt = T // 128
    nb = (T + 511) // 512
    per = (nt + nb - 1) // nb
    out = []
    s = 0
    while s < nt:
        e = min(nt, s + per)
        out.append((s * 128, (e - s) * 128))
        s = e
    return out


class Ctx:
    pass


def setup_consts(b, ar, nc, ident_dram, off):
    c = Ctx()
    c.idb, off = ar.at(off, [128, 128], BF16, "idb")
    c.idf, off = ar.at(off, [128, 128], F32, "idf")
    c.stat, off = ar.at(off, [128, 64], F32, "stat")
    b.dma('pool', c.idb[:], ident_dram, writes=['idb'])
    b.dma('sp', c.idf[:], ident_dram, writes=['idf'])
    c.off = off
    c.statn = 0
    return c


def emit_rstd(b, c, src_ap, src_res, junk_ap, junk_res, n):
    i = c.statn % 16
    c.statn += 1
    ss = c.stat[:, 4 * i:4 * i + 1]
    rs = c.stat[:, 4 * i + 1:4 * i + 2]
    r1 = 'stat%d' % i
    b.op('act', lambda e: e.activation(out=junk_ap, in_=src_ap, func=AF.Square, accum_out=ss),
         reads=[src_res], writes=[junk_res, r1])
    b.op('dve', lambda e: e.tensor_scalar(out=rs, in0=ss, scalar1=1.0 / n, scalar2=EPS, op0=ALU.mult, op1=ALU.add),
         reads=[r1], writes=[r1 + 'b'])
    b.op('act', lambda e: e.sqrt(out=rs, in_=rs), reads=[r1 + 'b'], writes=[r1 + 'b'])
    b.op('dve', lambda e: e.reciprocal(out=rs, in_=rs), reads=[r1 + 'b'], writes=[r1 + 'b'])
    return rs, r1 + 'b'


def emit_pre(b, c, ar, nc, base, x_dram, x_res, tiles, modrow, g_dram, i_shift, i_scale, uT=None, u_out=None, u_res=None, ps=None):
    off = base
    XT = []
    for i in range(2):
        t_, off = ar.at(off, [128, D], F32, "XT"); XT.append(t_)
    Uf, off = ar.at(off, [128, D], F32, "Uf")
    UB = []
    for i in range(2):
        t_, off = ar.at(off, [128, D], BF16, "Ub"); UB.append(t_)
    A, off = ar.at(off, [128, D], F32, "A")
    SH, off = ar.at(off, [128, D], F32, "SH")
    GP, off = ar.at(off, [128, D], F32, "GP")
    b.dma('sp', GP[:], g_dram.partition_broadcast(128), writes=['GP'])
    cur = [None]

    def stage_a(n):
        t, mr = tiles[n]
        if mr != cur[0]:
            cur[0] = mr
            b.dma('sp', A[:], modrow[mr, i_scale * D:(i_scale + 1) * D].partition_broadcast(128), reads=['modrow'], writes=['A'])
            b.dma('sp', SH[:], modrow[mr, i_shift * D:(i_shift + 1) * D].partition_broadcast(128), reads=['modrow'], writes=['SH'])
            b.op('dve', lambda e: e.scalar_tensor_tensor(out=A[:], in0=A[:], scalar=1.0, in1=GP[:], op0=ALU.add, op1=ALU.mult),
                 reads=['A', 'GP'], writes=['A'])
        xt = XT[n % 2]; xr = 'XT%d' % (n % 2)
        ub = UB[n % 2]; ur = 'UB%d' % (n % 2)
        b.dma('sp', xt[:], x_dram[t * 128:(t + 1) * 128, :], reads=[x_res], writes=[xr])
        rs, rr = emit_rstd(b, c, xt[:], xr, Uf[:], 'Uf', D)
        b.op('dve', lambda e: e.scalar_tensor_tensor(out=Uf[:], in0=xt[:], scalar=rs, in1=A[:], op0=ALU.mult, op1=ALU.mult),
             reads=[xr, rr, 'A'], writes=['Uf'])
        b.op('dve', lambda e: e.tensor_tensor(out=ub[:], in0=Uf[:], in1=SH[:], op=ALU.add), reads=['Uf', 'SH'], writes=[ur])

    def stage_b(n):
        t, mr = tiles[n]
        ub = UB[n % 2]; ur = 'UB%d' % (n % 2)
        if uT is not None:
            emit_transpose_tile(b, c, ub, ur, uT, 'uT', t, KC, ps, n)
        elif callable(u_out):
            dap, dres = u_out(t)
            b.dma('sp', dap, ub[:], reads=[ur], writes=[dres], waw=False)
        else:
            b.dma('sp', u_out[t * 128:(t + 1) * 128, :], ub[:], reads=[ur], writes=[u_res], waw=False)

    stage_a(0)
    for n in range(len(tiles)):
        if n + 1 < len(tiles) and tiles[n + 1][1] == tiles[n][1]:
            stage_a(n + 1)
            stage_b(n)
        else:
            stage_b(n)
            if n + 1 < len(tiles):
                stage_a(n + 1)


def emit_transpose_tile(b, c, src, src_res, dstT, dst_res, t, nch, ps, n):
    for h in range(0, nch, 8):
        k = (n * ((nch + 7) // 8) + h // 8) % len(ps)
        p = ps[k]; pr = 'psT%d' % k
        m = min(8, nch - h)
        for cc in range(m):
            ch = h + cc
            b.op('pe', lambda e: e.transpose(p[:, cc, :], src[:, ch * 128:(ch + 1) * 128], c.idb[:]),
                 reads=[src_res, 'idb'], writes=[pr])
        eng = 'act' if (k % 2 == 0) else 'dve'
        if eng == 'act':
            b.op('act', lambda e: e.copy(out=dstT[:, h:h + m, t * 128:(t + 1) * 128], in_=p[:, 0:m, :]), reads=[pr], writes=[dst_res])
        else:
            b.op('dve', lambda e: e.tensor_copy(out=dstT[:, h:h + m, t * 128:(t + 1) * 128], in_=p[:, 0:m, :]), reads=[pr], writes=[dst_res])


def emit_linear_T(b, ar, nc, base, srcT, src_res, nk, T, w_dram, w_res, ncols, outT, out_res, pbanks, colblk=256, nbuf=2):
    off = base
    WB = []
    for i in range(nbuf):
        t_, off = ar.at(off, [128, nk, colblk], BF16, "WB"); WB.append(t_)
    blks = blocks_of(T)
    nb = len(blks)
    wv = w_dram.rearrange("(c p) n -> p c n", p=128)
    it = 0
    for cb in range(ncols // colblk):
        wb = WB[cb % nbuf]; wr = 'WB%d' % (cb % nbuf)
        b.dma('pool', wb[:], wv[:, :, cb * colblk:(cb + 1) * colblk], reads=[w_res], writes=[wr])
        for dd in range(colblk // 128):
            cch = cb * (colblk // 128) + dd
            par = it % 2; it += 1
            for k in range(nk):
                for bi, (s0, sl) in enumerate(blks):
                    pb = pbanks[par * nb + bi]; pr = 'pb%d' % (par * nb + bi)
                    b.op('pe', lambda e: e.matmul(pb[:, 0:sl], lhsT=wb[:, k, dd * 128:(dd + 1) * 128], rhs=srcT[:, k, s0:s0 + sl],
                                                  start=(k == 0), stop=(k == nk - 1)),
                         reads=[wr, src_res], writes=[pr])
            for bi, (s0, sl) in enumerate(blks):
                pb = pbanks[par * nb + bi]; pr = 'pb%d' % (par * nb + bi)
                if bi % 2 == 0:
                    b.op('act', lambda e: e.copy(out=outT[:, cch, s0:s0 + sl], in_=pb[:, 0:sl]), reads=[pr], writes=[out_res])
                else:
                    b.op('dve', lambda e: e.tensor_copy(out=outT[:, cch, s0:s0 + sl], in_=pb[:, 0:sl]), reads=[pr], writes=[out_res])


def emit_ffn_m1(b, ar, nc, base, uT, T, w_in, w_res, actT, pbanks, SG, colblk=256):
    off = base
    WG = []; WU = []
    for i in range(2):
        t_, off = ar.at(off, [128, KC, colblk], BF16, "WG"); WG.append(t_)
        t_, off = ar.at(off, [128, KC, colblk], BF16, "WU"); WU.append(t_)
    blks = blocks_of(T)
    nb = len(blks)
    wv = w_in.rearrange("(c p) n -> p c n", p=128)
    for cb in range(DFF // colblk):
        wg = WG[cb % 2]; wu = WU[cb % 2]
        gr = 'WG%d' % (cb % 2); ur = 'WU%d' % (cb % 2)
        b.dma('pool', wg[:], wv[:, :, cb * colblk:(cb + 1) * colblk], reads=[w_res], writes=[gr])
        b.dma('pool', wu[:], wv[:, :, DFF + cb * colblk:DFF + (cb + 1) * colblk], reads=[w_res], writes=[ur])
        for dd in range(colblk // 128):
            j = cb * (colblk // 128) + dd
            for (wt, wr, pofs) in ((wg, gr, 0), (wu, ur, nb)):
                for k in range(KC):
                    for bi, (s0, sl) in enumerate(blks):
                        pb = pbanks[pofs + bi]; pr = 'pb%d' % (pofs + bi)
                        b.op('pe', lambda e: e.matmul(pb[:, 0:sl], lhsT=wt[:, k, dd * 128:(dd + 1) * 128], rhs=uT[:, k, s0:s0 + sl],
                                                      start=(k == 0), stop=(k == KC - 1)),
                             reads=[wr, 'uT'], writes=[pr])
            for bi, (s0, sl) in enumerate(blks):
                pg = pbanks[bi]; pu = pbanks[nb + bi]
                b.op('act', lambda e: e.activation(out=SG[:, s0:s0 + sl], in_=pg[:, 0:sl], func=AF.Silu),
                     reads=['pb%d' % bi], writes=['SG%d' % bi])
                b.op('dve', lambda e: e.tensor_tensor(out=actT[:, j, s0:s0 + sl], in0=pu[:, 0:sl], in1=SG[:, s0:s0 + sl], op=ALU.mult),
                     reads=['pb%d' % (nb + bi), 'SG%d' % bi], writes=['actT'])


def emit_post(b, c, ar, nc, base, outT, outT_res, x_dram, x_res, xo_dram, xo_res, tiles, modrow, g_dram, i_gate, res_scale, pq, premix=None):
    off = base
    XT = []; XO = []
    for i in range(2):
        t_, off = ar.at(off, [128, D], F32, "QX"); XT.append(t_)
        t_, off = ar.at(off, [128, D], F32, "QO"); XO.append(t_)
    G, off = ar.at(off, [128, D], F32, "G")
    GP, off = ar.at(off, [128, D], F32, "GPq")
    junk, off = ar.at(off, [128, D], F32, "junk")
    b.dma('sp', GP[:], g_dram.partition_broadcast(128), writes=['GPq'])
    if premix is not None:
        A2, off = ar.at(off, [128, D], F32, "A2")
        SH2, off = ar.at(off, [128, D], F32, "SH2")
        GP2, off = ar.at(off, [128, D], F32, "GP2")
        Uf2, off = ar.at(off, [128, D], F32, "Uf2")
        UB2 = []
        for i in range(2):
            t_, off = ar.at(off, [128, D], BF16, "UB2"); UB2.append(t_)
        b.dma('sp', GP2[:], premix['g_pre'].partition_broadcast(128), writes=['GP2'])
    cur = None
    for n, (t, mr) in enumerate(tiles):
        if mr != cur:
            cur = mr
            b.dma('sp', G[:], modrow[mr, i_gate * D:(i_gate + 1) * D].partition_broadcast(128), reads=['modrow'], writes=['G'])
            b.op('dve', lambda e: e.scalar_tensor_tensor(out=G[:], in0=G[:], scalar=float(res_scale), in1=GP[:], op0=ALU.mult, op1=ALU.mult),
                 reads=['G', 'GPq'], writes=['G'])
            if premix is not None:
                isc, ish = premix['i_scale'], premix['i_shift']
                b.dma('sp', A2[:], modrow[mr, isc * D:(isc + 1) * D].partition_broadcast(128), reads=['modrow'], writes=['A2'])
                b.dma('sp', SH2[:], modrow[mr, ish * D:(ish + 1) * D].partition_broadcast(128), reads=['modrow'], writes=['SH2'])
                b.op('dve', lambda e: e.scalar_tensor_tensor(out=A2[:], in0=A2[:], scalar=1.0, in1=GP2[:], op0=ALU.add, op1=ALU.mult),
                     reads=['A2', 'GP2'], writes=['A2'])
        p = pq[n % 2]; pr = 'pq%d' % (n % 2)
        xt = XT[n % 2]; xr = 'QX%d' % (n % 2)
        xo = XO[n % 2]; xor_ = 'QO%d' % (n % 2)
        b.dma('sp', xt[:], x_dram[t * 128:(t + 1) * 128, :], reads=[x_res], writes=[xr])
        for ch in range(KC):
            b.op('pe', lambda e: e.transpose(p[:, ch * 128:(ch + 1) * 128], outT[:, ch, t * 128:(t + 1) * 128], c.idf[:]),
                 reads=[outT_res, 'idf'], writes=[pr])
        rs, rr = emit_rstd(b, c, p[:], pr, junk[:], 'junk', D)
        b.op('dve', lambda e: e.scalar_tensor_tensor(out=xo[:], in0=p[:], scalar=rs, in1=G[:], op0=ALU.mult, op1=ALU.mult),
             reads=[pr, rr, 'G'], writes=[xor_])
        b.op('dve', lambda e: e.tensor_tensor(out=xo[:], in0=xo[:], in1=xt[:], op=ALU.add), reads=[xor_, xr], writes=[xor_])
        b.dma('sp', xo_dram[t * 128:(t + 1) * 128, :], xo[:], reads=[xor_], writes=[xo_res], waw=False)
        if premix is not None:
            rs2, rr2 = emit_rstd(b, c, xo[:], xor_, junk[:], 'junk', D)
            ub2 = UB2[n % 2]; ur2 = 'UB2%d' % (n % 2)
            b.op('dve', lambda e: e.scalar_tensor_tensor(out=Uf2[:], in0=xo[:], scalar=rs2, in1=A2[:], op0=ALU.mult, op1=ALU.mult),
                 reads=[xor_, rr2, 'A2'], writes=['Uf2'])
            b.op('dve', lambda e: e.tensor_tensor(out=ub2[:], in0=Uf2[:], in1=SH2[:], op=ALU.add), reads=['Uf2', 'SH2'], writes=[ur2])
            dap, dres = premix['u_out'](t)
            b.dma('sp', dap, ub2[:], reads=[ur2], writes=[dres], waw=False)


STOP = 99

NTOK = 2304
NTIL = 18
PADW = 2312
POS_X = 2
POS_C = 2054


def padpos(tok):
    return POS_X + tok if tok < 2048 else POS_C + (tok - 2048)


def emit_load_uT(b, c, ar, base, u_dram, u_res, ntile, uT, PS):
    off = base
    UB = []
    for i in range(2):
        t_, off = ar.at(off, [128, D], BF16, "UBl"); UB.append(t_)
    for t in range(ntile):
        ub = UB[t % 2]; ur = 'UBl%d' % (t % 2)
        if callable(u_dram):
            sap, sres = u_dram(t)
            b.dma('sp', ub[:], sap, reads=[sres], writes=[ur])
        else:
            b.dma('sp', ub[:], u_dram[t * 128:(t + 1) * 128, :], reads=[u_res], writes=[ur])
        emit_transpose_tile(b, c, ub, ur, uT, 'uT', t, KC, PS.psT, t)
    return off


def emit_mlstm(b, c, ar, nc, PS, u_dram, w_mh, w_g, conv_w, gate_b, head_g, cmat, hmix, hmix_res):
    P0 = SB_BASE
    QK, o1 = ar.at(P0, [128, 8, PADW], BF16, "QK")
    VX, o2 = ar.at(o1, [128, 4, NTIL, 130], BF16, "VX")
    OS, o3 = ar.at(o2, [128, 4, NTIL, 128], BF16, "OS")
    GT, o4 = ar.at(o3, [128, NTIL, 16], F32, "GT")
    SP, o5 = ar.at(o4, [128, NTIL, 16], F32, "SP")
    CW, o6 = ar.at(o5, [128, 8, 5], F32, "CW")
    GB, o7 = ar.at(o6, [128, 16], F32, "GB")
    HG, o8 = ar.at(o7, [128, 512], F32, "HG")
    CM, o9 = ar.at(o8, [128, 5, 128], F32, "CM")
    TRB, o10 = ar.at(o9, [128, 2, 128], BF16, "TRB")
    P1 = (o10 + 31) // 32 * 32
    b.dma('sp', CW[:], conv_w, writes=['CW'])
    b.dma('sp', GB[:], gate_b.partition_broadcast(128), writes=['GB'])
    b.dma('sp', HG[:], head_g.partition_broadcast(128), writes=['HG'])
    b.dma('sp', CM[:], cmat, writes=['CM'])
    b.op('dve', lambda e: e.tensor_copy(out=TRB[:], in_=CM[:, 1:3, :]), reads=['CM'], writes=['TRB'])
    b.op('dve', lambda e: e.memset(QK[:], 0.0), writes=['QK'])
    b.op('dve', lambda e: e.memset(VX[:], 1.0), writes=['VX'])

    if STOP == 10:
        b.barrier(); return
    uT, q1 = ar.at(P1, [128, KC, NTOK], BF16, "uT")
    WM = []
    q = q1
    for i in range(2):
        t_, q = ar.at(q, [128, KC, 512], BF16, "WM"); WM.append(t_)
    WG, q = ar.at(q, [128, KC, 16], BF16, "WG")
    emit_load_uT(b, c, ar, q, u_dram, 'u', NTIL, uT, PS)
    b.dma('pool', WG[:], w_g.rearrange("(c p) n -> p c n", p=128), writes=['WG'])
    if STOP == 11:
        b.barrier(); return
    for t in range(NTIL):
        pb = PS.pb[t % 2]; pr = 'pb%d' % (t % 2)
        for k in range(KC):
            b.op('pe', lambda e: e.matmul(pb[:, 0:16], lhsT=uT[:, k, t * 128:(t + 1) * 128], rhs=WG[:, k, :], start=(k == 0), stop=(k == KC - 1)),
                 reads=['uT', 'WG'], writes=[pr])
        b.op('dve', lambda e: e.tensor_tensor(out=GT[:, t, :], in0=pb[:, 0:16], in1=GB[:], op=ALU.add), reads=[pr, 'GB'], writes=['GT'])
    if STOP == 12:
        b.barrier(); return
    b.op('act', lambda e: e.activation(out=SP[:], in_=GT[:], func=AF.Exp, scale=-1.0), reads=['GT'], writes=['SP'])
    b.op('dve', lambda e: e.tensor_scalar(out=SP[:], in0=SP[:], scalar1=1.0, scalar2=None, op0=ALU.add), reads=['SP'], writes=['SP'])
    b.op('act', lambda e: e.activation(out=SP[:], in_=SP[:], func=AF.Ln), reads=['SP'], writes=['SP'])
    if STOP == 13:
        b.barrier(); return
    blks = blocks_of(NTOK)
    it = 0
    for h in range(4):
        if STOP in (15, 20, 21) and h == 1:
            b.barrier(); return
        if STOP == 17 and h == 2:
            b.barrier(); return
        wm = WM[h % 2]; wr = 'WM%d' % (h % 2)
        b.dma('pool', wm[:], w_mh[h].rearrange("(c p) n -> p c n", p=128), reads=['w'], writes=[wr])
        for qk in range(2):
            for (s0, sl) in blks:
                pb = PS.pb[2 + it % 4]; pr = 'pb%d' % (2 + it % 4); it += 1
                for k in range(KC):
                    b.op('pe', lambda e: e.matmul(pb[:, 0:sl], lhsT=wm[:, k, qk * 128:(qk + 1) * 128], rhs=uT[:, k, s0:s0 + sl],
                                                  start=(k == 0), stop=(k == KC - 1)), reads=[wr, 'uT'], writes=[pr])
                segs = []
                if s0 < 2048:
                    e_ = min(s0 + sl, 2048)
                    segs.append((s0, e_ - s0))
                if s0 + sl > 2048:
                    s_ = max(s0, 2048)
                    segs.append((s_, s0 + sl - s_))
                for (a, l) in segs:
                    pp = padpos(a)
                    b.op('act', lambda e: e.copy(out=QK[:, h * 2 + qk, pp:pp + l], in_=pb[:, a - s0:a - s0 + l]), reads=[pr], writes=['QK'])
        if STOP == 14 and h == 0:
            b.barrier(); return
        if STOP == 16 and h == 1:
            b.barrier(); return
        for t in range(NTIL):
            pb = PS.pb[t % 2]; pr = 'pb%d' % (t % 2)
            for k in range(KC):
                b.op('pe', lambda e: e.matmul(pb[:, 0:256], lhsT=uT[:, k, t * 128:(t + 1) * 128], rhs=wm[:, k, 256:512], start=(k == 0), stop=(k == KC - 1)),
                     reads=['uT', wr], writes=[pr])
            if t % 2 == 0:
                b.op('dve', lambda e: e.tensor_copy(out=VX[:, h, t, 0:128], in_=pb[:, 0:128]), reads=[pr], writes=['VX'])
                b.op('dve', lambda e: e.tensor_copy(out=OS[:, h, t, :], in_=pb[:, 128:256]), reads=[pr], writes=['OS'])
            else:
                b.op('act', lambda e: e.copy(out=VX[:, h, t, 0:128], in_=pb[:, 0:128]), reads=[pr], writes=['VX'])
                b.op('act', lambda e: e.copy(out=OS[:, h, t, :], in_=pb[:, 128:256]), reads=[pr], writes=['OS'])
    b.barrier()
    if STOP == 1:
        return

    q = P1
    CQ, q = ar.at(q, [128, 2, PADW], F32, "CQ")
    EBK, q = ar.at(q, [128, 4, NTOK], F32, "EBK")
    QKs, q = ar.at(q, [128, 4, NTOK], BF16, "QKs")
    KT, q = ar.at(q, [128, 2, NTIL, 128], BF16, "KT")
    HA, q = ar.at(q, [128, NTIL, 128], F32, "HA")
    HS, q = ar.at(q, [128, NTIL, 128], F32, "HS")
    HO, q = ar.at(q, [128, NTIL, 128], BF16, "HO")
    LB, q = ar.at(q, [128, 4, 128], F32, "LB")
    CS, q = ar.at(q, [128, 2, 130], F32, "CS")
    CSb, q = ar.at(q, [128, 2, 130], BF16, "CSb")
    SW, q = ar.at(q, [128, 4, 128], BF16, "SW")
    ST, q = ar.at(q, [128, 64], F32, "ST")
    order_f = [16, 17] + list(range(16))
    order_b = [17, 16] + list(range(15, -1, -1))
    for h in range(4):
        for qk in range(2):
            src = QK[:, h * 2 + qk, :]
            dst = CQ[:, qk, 2:PADW - 2]
            L = PADW - 4
            b.op('dve', lambda e: e.tensor_scalar(out=dst, in0=src[:, 0:L], scalar1=CW[:, h * 2 + qk, 0:1], scalar2=None, op0=ALU.mult),
                 reads=['QK', 'CW'], writes=['CQ%d' % qk])
            for kk in range(1, 5):
                b.op('dve', lambda e: e.scalar_tensor_tensor(out=dst, in0=src[:, kk:kk + L], scalar=CW[:, h * 2 + qk, kk:kk + 1], in1=dst,
                                                             op0=ALU.mult, op1=ALU.add), reads=['QK', 'CW', 'CQ%d' % qk], writes=['CQ%d' % qk])
            b.op('act', lambda e: e.activation(out=dst, in_=dst, func=AF.Silu), reads=['CQ%d' % qk], writes=['CQ%d' % qk])
        if STOP == 2:
            b.barrier(); return
        n = 0
        for d in range(2):
            gi = 0 if d == 0 else 2
            for t in range(NTIL):
                lsp = LB[:, (n % 2) * 2, :]; lli = LB[:, (n % 2) * 2 + 1, :]
                lr = 'LB%d' % (n % 2)
                pa = PS.pb[(n % 2) * 2]; pbb = PS.pb[(n % 2) * 2 + 1]
                par = 'pb%d' % ((n % 2) * 2); pbr = 'pb%d' % ((n % 2) * 2 + 1)
                n += 1
                csp = SP[:, t, (gi + 1) * 4 + h:(gi + 1) * 4 + h + 1]
                cli = GT[:, t, gi * 4 + h:gi * 4 + h + 1]
                b.op('dve', lambda e: e.tensor_scalar(out=lsp, in0=CM[:, 3, :], scalar1=csp, scalar2=None, op0=ALU.mult), reads=['CM', 'SP'], writes=[lr])
                b.op('dve', lambda e: e.tensor_scalar(out=lli, in0=CM[:, 3, :], scalar1=cli, scalar2=None, op0=ALU.mult), reads=['CM', 'GT'], writes=[lr])
                tri = CM[:, 1 + d, :]
                b.op('pe', lambda e: e.matmul(pa[:, 0:128], lhsT=lsp, rhs=tri, start=True, stop=True), reads=[lr, 'CM'], writes=[par])
                b.op('pe', lambda e: e.matmul(pbb[:, 0:128], lhsT=lli, rhs=CM[:, 0, :], start=True, stop=False), reads=[lr, 'CM'], writes=[pbr])
                b.op('pe', lambda e: e.matmul(pbb[:, 0:128], lhsT=lsp, rhs=tri, start=False, stop=True), reads=[lr, 'CM'], writes=[pbr])
                b.op('act', lambda e: e.activation(out=EBK[:, d * 2, t * 128:(t + 1) * 128], in_=pa[:, 0:128], func=AF.Exp, scale=-1.0), reads=[par], writes=['EBK'])
                b.op('act', lambda e: e.activation(out=EBK[:, d * 2 + 1, t * 128:(t + 1) * 128], in_=pbb[:, 0:128], func=AF.Exp), reads=[pbr], writes=['EBK'])
        if STOP == 3:
            b.barrier(); return
        for d in range(2):
            for (a, l) in ((0, 2048), (2048, 256)):
                pp = padpos(a)
                b.op('dve', lambda e: e.tensor_tensor(out=QKs[:, d * 2, a:a + l], in0=CQ[:, 0, pp:pp + l], in1=EBK[:, d * 2, a:a + l], op=ALU.mult),
                     reads=['CQ0', 'EBK'], writes=['QKs'])
                b.op('dve', lambda e: e.scalar_tensor_tensor(out=QKs[:, d * 2 + 1, a:a + l], in0=CQ[:, 1, pp:pp + l], scalar=128.0 ** -0.5, in1=EBK[:, d * 2 + 1, a:a + l],
                                                             op0=ALU.mult, op1=ALU.mult), reads=['CQ1', 'EBK'], writes=['QKs'])
        n = 0
        for d in range(2):
            for t0 in range(0, NTIL, 8):
                k_ = n % 2; n += 1
                p = PS.psT[k_]; pr = 'psT%d' % k_
                m = min(8, NTIL - t0)
                for cc in range(m):
                    t = t0 + cc
                    b.op('pe', lambda e: e.transpose(p[:, cc, :], QKs[:, d * 2 + 1, t * 128:(t + 1) * 128], c.idb[:]), reads=['QKs', 'idb'], writes=[pr])
                b.op('act', lambda e: e.copy(out=KT[:, d, t0:t0 + m, :], in_=p[:, 0:m, :]), reads=[pr], writes=['KT'])
        if STOP == 4:
            b.barrier(); return
        b.op('dve', lambda e: e.memset(CS[:], 0.0), writes=['CS0', 'CS1'])
        b.op('dve', lambda e: e.memset(CSb[:], 0.0), writes=['CSb0', 'CSb1'])
        for i in range(NTIL):
            for d in range(2):
                t = (order_f if d == 0 else order_b)[i]
                pS = PS.pb[d * 3]; pN = PS.pb[d * 3 + 1]; pU = PS.pb[d * 3 + 2]
                rS = 'pb%d' % (d * 3); rN = 'pb%d' % (d * 3 + 1); rU = 'pb%d' % (d * 3 + 2)
                qT = QKs[:, d * 2, t * 128:(t + 1) * 128]
                kT = QKs[:, d * 2 + 1, t * 128:(t + 1) * 128]
                sw = SW[:, d * 2 + (i % 2), :]; swr = 'SW%d' % (d * 2 + (i % 2))
                b.op('pe', lambda e: e.matmul(pS[:, 0:128], lhsT=kT, rhs=qT, start=True, stop=True), reads=['QKs'], writes=[rS])
                b.op('dve', lambda e: e.tensor_tensor(out=sw, in0=pS[:, 0:128], in1=TRB[:, d, :], op=ALU.mult), reads=[rS, 'TRB'], writes=[swr])
                b.op('pe', lambda e: e.matmul(pN[:, 0:129], lhsT=sw, rhs=VX[:, h, t, 0:129], start=True, stop=(i == 0)), reads=[swr, 'VX'], writes=[rN])
                if i > 0:
                    b.op('pe', lambda e: e.matmul(pN[:, 0:129], lhsT=qT, rhs=CSb[:, d, 0:129], start=False, stop=True), reads=['QKs', 'CSb%d' % d], writes=[rN])
                b.op('pe', lambda e: e.matmul(pU[:, 0:129], lhsT=KT[:, d, t, :], rhs=VX[:, h, t, 0:129], start=True, stop=True), reads=['KT', 'VX'], writes=[rU])
                dn = ST[:, d * 2:d * 2 + 1]; dr = 'STd%d' % d
                b.op('act', lambda e: e.activation(out=dn, in_=pN[:, 128:129], func=AF.Abs), reads=[rN], writes=[dr])
                b.op('dve', lambda e: e.tensor_scalar(out=dn, in0=dn, scalar1=1.0, scalar2=None, op0=ALU.max), reads=[dr], writes=[dr])
                b.op('dve', lambda e: e.reciprocal(out=dn, in_=dn), reads=[dr], writes=[dr])
                if d == 0:
                    b.op('dve', lambda e: e.tensor_scalar(out=HA[:, t, :], in0=pN[:, 0:128], scalar1=dn, scalar2=None, op0=ALU.mult), reads=[rN, dr], writes=['HA'])
                else:
                    b.op('dve', lambda e: e.tensor_scalar(out=HS[:, t, :], in0=pN[:, 0:128], scalar1=dn, scalar2=None, op0=ALU.mult), reads=[rN, dr], writes=['HS'])
                if i < NTIL - 1:
                    col = t * 128 + (127 if d == 0 else 0)
                    dec = EBK[:, d * 2, col:col + 1]
                    b.op('dve', lambda e: e.tensor_tensor(out=CS[:, d, 0:129], in0=pU[:, 0:129], in1=CS[:, d, 0:129], op=ALU.add), reads=[rU, 'CS%d' % d], writes=['CS%d' % d])
                    b.op('dve', lambda e: e.tensor_scalar(out=CS[:, d, :], in0=CS[:, d, :], scalar1=dec, scalar2=None, op0=ALU.mult), reads=['CS%d' % d, 'EBK'], writes=['CS%d' % d])
                    b.op('act', lambda e: e.copy(out=CSb[:, d, :], in_=CS[:, d, :]), reads=['CS%d' % d], writes=['CSb%d' % d])
        if STOP == 5:
            b.barrier(); return
        b.op('dve', lambda e: e.tensor_tensor(out=HA[:], in0=HA[:], in1=HS[:], op=ALU.add), reads=['HA', 'HS'], writes=['HA'])
        b.op('dve', lambda e: e.tensor_tensor(out=HS[:], in0=HA[:], in1=HA[:], op=ALU.mult), reads=['HA'], writes=['HS'])
        ssq = ST[:, 8:8 + NTIL]
        b.op('dve', lambda e: e.tensor_reduce(out=ssq, in_=HS[:], axis=AX.X, op=ALU.add), reads=['HS'], writes=['STs'])
        b.op('dve', lambda e: e.tensor_scalar(out=ssq, in0=ssq, scalar1=1.0 / 128, scalar2=EPS, op0=ALU.mult, op1=ALU.add), reads=['STs'], writes=['STs'])
        b.op('act', lambda e: e.sqrt(out=ssq, in_=ssq), reads=['STs'], writes=['STs'])
        b.op('dve', lambda e: e.reciprocal(out=ssq, in_=ssq), reads=['STs'], writes=['STs'])
        for t in range(NTIL):
            b.op('dve', lambda e: e.scalar_tensor_tensor(out=HS[:, t, :], in0=HA[:, t, :], scalar=ST[:, 8 + t:9 + t], in1=HG[:, h * 128:(h + 1) * 128],
                                                         op0=ALU.mult, op1=ALU.mult), reads=['HA', 'STs', 'HG'], writes=['HS'])
        b.op('act', lambda e: e.activation(out=HA[:], in_=OS[:, h, :, :], func=AF.Exp, scale=-1.0), reads=['OS', 'HA'], writes=['HA'])
        b.op('dve', lambda e: e.tensor_scalar(out=HA[:], in0=HA[:], scalar1=1.0, scalar2=None, op0=ALU.add), reads=['HA'], writes=['HA'])
        b.op('dve', lambda e: e.reciprocal(out=HA[:], in_=HA[:]), reads=['HA'], writes=['HA'])
        b.op('dve', lambda e: e.tensor_tensor(out=HO[:], in0=HS[:], in1=HA[:], op=ALU.mult), reads=['HS', 'HA'], writes=['HO'])
        if callable(hmix):
            hmix(b, h * 128, HO, 'HO')
        else:
            b.dma('sp', hmix.rearrange("(t p) n -> p t n", p=128)[:, :, h * 128:(h + 1) * 128], HO[:], reads=['HO'], writes=[hmix_res], waw=False)
    b.barrier()


def emit_diff(b, c, ar, nc, PS, u_dram, w_dh, cos_d, sin_d, lam_p, subln_g, cmat, hmix, hmix_res, lam_init):
    P0 = SB_BASE
    QD, o = ar.at(P0, [128, 4, NTOK], BF16, "QD")
    KD, o = ar.at(o, [128, 4, NTOK], BF16, "KD")
    VD, o = ar.at(o, [128, NTIL, 4, 130], BF16, "VD")
    LP, o = ar.at(o, [128, 4, 64], F32, "LP")
    SGN, o = ar.at(o, [128, 128], F32, "SGN")
    LAM, o = ar.at(o, [128, 8], F32, "LAM")
    ROT, o = ar.at(o, [128, 128], F32, "ROT")
    NBG, o = ar.at(o, [128, 8], F32, "NBG")
    KX, o = ar.at(o, [128, 16], F32, "KX")
    CM, o = ar.at(o, [128, 5, 128], F32, "CMd")
    P1 = o
    b.dma('sp', LP[:], lam_p.partition_broadcast(128), writes=['LP'])
    b.dma('sp', SGN[:], subln_g.partition_broadcast(128), writes=['SGN'])
    b.dma('sp', ROT[:], cmat[:, 4, :], writes=['ROT'])
    b.dma('sp', CM[:], cmat, writes=['CM'])
    b.op('dve', lambda e: e.memset(VD[:], 1.0), writes=['VD'])
    b.op('dve', lambda e: e.tensor_tensor(out=LP[:, 0, :], in0=LP[:, 0, :], in1=LP[:, 1, :], op=ALU.mult), reads=['LP'], writes=['LP'])
    b.op('dve', lambda e: e.tensor_tensor(out=LP[:, 2, :], in0=LP[:, 2, :], in1=LP[:, 3, :], op=ALU.mult), reads=['LP'], writes=['LP'])
    b.op('dve', lambda e: e.tensor_reduce(out=LAM[:, 0:1], in_=LP[:, 0, :], axis=AX.X, op=ALU.add), reads=['LP'], writes=['LAM'])
    b.op('dve', lambda e: e.tensor_reduce(out=LAM[:, 1:2], in_=LP[:, 2, :], axis=AX.X, op=ALU.add), reads=['LP'], writes=['LAM'])
    b.op('act', lambda e: e.activation(out=LAM[:, 0:2], in_=LAM[:, 0:2], func=AF.Exp), reads=['LAM'], writes=['LAM'])
    b.op('dve', lambda e: e.tensor_tensor(out=LAM[:, 2:3], in0=LAM[:, 0:1], in1=LAM[:, 1:2], op=ALU.subtract), reads=['LAM'], writes=['LAM'])
    b.op('dve', lambda e: e.tensor_scalar(out=LAM[:, 2:3], in0=LAM[:, 2:3], scalar1=float(lam_init), scalar2=None, op0=ALU.add), reads=['LAM'], writes=['LAM'])
    b.op('dve', lambda e: e.tensor_scalar(out=LAM[:, 3:4], in0=LAM[:, 2:3], scalar1=-1.0, scalar2=None, op0=ALU.mult), reads=['LAM'], writes=['LAM'])

    uT, q = ar.at(P1, [128, KC, NTOK], BF16, "uT")
    WD = []
    for i in range(2):
        t_, q = ar.at(q, [128, KC, 384], BF16, "WD"); WD.append(t_)
    COS, q = ar.at(q, [128, 2048], F32, "COS")
    SIN, q = ar.at(q, [128, 2048], F32, "SIN")
    XF, q = ar.at(q, [128, 2, 512], F32, "XF")
    T1, q = ar.at(q, [128, 2, 512], F32, "T1")
    SQ, q = ar.at(q, [128, 2, NTOK], F32, "SQ")
    b.dma('sp', COS[:], cos_d, writes=['COS'])
    b.dma('sp', SIN[:], sin_d, writes=['SIN'])
    emit_load_uT(b, c, ar, q, u_dram, 'u', NTIL, uT, PS)
    blks = blocks_of(NTOK)
    it = 0
    for h in range(4):
        wd = WD[h % 2]; wr = 'WD%d' % (h % 2)
        b.dma('pool', wd[:], w_dh[h].rearrange("(c p) n -> p c n", p=128), reads=['w'], writes=[wr])
        for qk in range(2):
            dst = QD if qk == 0 else KD
            dres = 'QD' if qk == 0 else 'KD'
            for (s0, sl) in blks:
                par = it % 2; it += 1
                pb = PS.pb[par * 2]; pr = 'pb%d' % (par * 2)
                p2 = PS.pb[par * 2 + 1]; p2r = 'pb%d' % (par * 2 + 1)
                for k in range(KC):
                    b.op('pe', lambda e: e.matmul(pb[:, 0:sl], lhsT=wd[:, k, qk * 128:(qk + 1) * 128], rhs=uT[:, k, s0:s0 + sl],
                                                  start=(k == 0), stop=(k == KC - 1)), reads=[wr, 'uT'], writes=[pr])
                if s0 >= 2048:
                    b.op('act', lambda e: e.copy(out=dst[:, h, s0:s0 + sl], in_=pb[:, 0:sl]), reads=[pr], writes=[dres])
                    continue
                xf = XF[:, par, 0:sl]; xr = 'XF%d' % par
                t1 = T1[:, par, 0:sl]; tr = 'T1%d' % par
                b.op('act', lambda e: e.copy(out=xf, in_=pb[:, 0:sl]), reads=[pr], writes=[xr])
                b.op('pe', lambda e: e.matmul(p2[:, 0:sl], lhsT=ROT[:], rhs=xf, start=True, stop=True), reads=['ROT', xr], writes=[p2r])
                b.op('dve', lambda e: e.tensor_tensor(out=t1, in0=p2[:, 0:sl], in1=SIN[:, s0:s0 + sl], op=ALU.mult), reads=[p2r, 'SIN'], writes=[tr])
                b.op('dve', lambda e: e.tensor_tensor(out=xf, in0=xf, in1=COS[:, s0:s0 + sl], op=ALU.mult), reads=[xr, 'COS'], writes=[xr])
                b.op('dve', lambda e: e.tensor_tensor(out=dst[:, h, s0:s0 + sl], in0=xf, in1=t1, op=ALU.add), reads=[xr, tr], writes=[dres])
        b.op('dve', lambda e: e.tensor_tensor(out=SQ[:, 0, :], in0=QD[:, h, :], in1=QD[:, h, :], op=ALU.mult), reads=['QD'], writes=['SQ0'])
        b.op('dve', lambda e: e.tensor_tensor(out=SQ[:, 1, :], in0=KD[:, h, :], in1=KD[:, h, :], op=ALU.mult), reads=['KD'], writes=['SQ1'])
        for g in range(2):
            gs = slice(g * 64, (g + 1) * 64)
            pk = PS.pb[6]; pkr = 'pb6'
            for which in range(2):
                for bi, (s0, sl) in enumerate(blks):
                    b.op('pe', lambda e: e.matmul(pk[0:1, 0:sl], lhsT=CM[gs, 3, 0:1], rhs=SQ[gs, which, s0:s0 + sl], start=True, stop=True), reads=['CM', 'SQ%d' % which], writes=[pkr])
                    b.op('dve', lambda e: e.tensor_reduce(out=KX[0:1, bi:bi + 1], in_=pk[0:1, 0:sl], axis=AX.X, op=ALU.max), reads=[pkr], writes=['KX'])
                b.op('dve', lambda e: e.tensor_reduce(out=KX[0:1, 8 + which:9 + which], in_=KX[0:1, 0:len(blks)], axis=AX.X, op=ALU.max), reads=['KX'], writes=['KX'])
            b.op('dve', lambda e: e.tensor_tensor(out=KX[0:1, 10:11], in0=KX[0:1, 8:9], in1=KX[0:1, 9:10], op=ALU.mult), reads=['KX'], writes=['KX'])
            pq_ = PS.pb[7]; pqr = 'pb7'
            b.op('pe', lambda e: e.matmul(pq_[:, 0:1], lhsT=CM[0:1, 3, :], rhs=KX[0:1, 10:11], start=True, stop=True), reads=['CM', 'KX'], writes=[pqr])
            nbg = NBG[:, h * 2 + g:h * 2 + g + 1]
            b.op('act', lambda e: e.sqrt(out=nbg, in_=pq_[:, 0:1]), reads=[pqr], writes=['NBG'])
            b.op('dve', lambda e: e.tensor_scalar(out=nbg, in0=nbg, scalar1=-0.125, scalar2=None, op0=ALU.mult), reads=['NBG'], writes=['NBG'])
        for t in range(NTIL):
            pb = PS.pb[4 + t % 2]; pr = 'pb%d' % (4 + t % 2)
            for k in range(KC):
                b.op('pe', lambda e: e.matmul(pb[:, 0:128], lhsT=uT[:, k, t * 128:(t + 1) * 128], rhs=wd[:, k, 256:384], start=(k == 0), stop=(k == KC - 1)),
                     reads=['uT', wr], writes=[pr])
            if t % 2 == 0:
                b.op('dve', lambda e: e.tensor_copy(out=VD[:, t, h, 0:128], in_=pb[:, 0:128]), reads=[pr], writes=['VD'])
            else:
                b.op('act', lambda e: e.copy(out=VD[:, t, h, 0:128], in_=pb[:, 0:128]), reads=[pr], writes=['VD'])
    b.barrier()

    if STOP == 40:
        return
    q = P1
    PT = []
    for i in range(4):
        t_, q = ar.at(q, [128, 512], BF16, "PT"); PT.append(t_)
    OA, q = ar.at(q, [128, 2, 8, 129], F32, "OA")
    OT, q = ar.at(q, [128, 2, 4, 128], F32, "OT")
    OQ, q = ar.at(q, [128, 2, 4, 128], F32, "OQ")
    HOD, q = ar.at(q, [128, 2, NTIL, 128], BF16, "HOD")
    ST, q = ar.at(q, [128, 2, 16], F32, "STD")
    nblk = 0
    nit = 0
    for h in range(4):
        hr = 'HOD%d' % (h % 2)
        for qb in range(5):
            if qb < 4:
                q0, nq, qts, kts = qb * 512, 512, [qb * 4 + j for j in range(4)], list(range(NTIL))
            else:
                q0, nq, qts, kts = 2048, 256, [16, 17], [16, 17]
            bp = nblk % 2; nblk += 1
            items = [(kt, g) for kt in kts for g in range(2)]
            started = set()

            def acc(j, g):
                a = j * 2 + g
                return PS.pb[4 + a // 3][:, (a % 3) * 129:(a % 3) * 129 + 129], 'pb%d' % (4 + a // 3)

            def emit_s(ii, n_):
                kt, g = items[ii]
                gs = slice(g * 64, (g + 1) * 64)
                r = n_ % 4
                b.op('pe', lambda e: e.matmul(PS.pb[r][:, 0:nq], lhsT=KD[gs, h, kt * 128:(kt + 1) * 128], rhs=QD[gs, h, q0:q0 + nq], start=True, stop=True),
                     reads=['QD', 'KD'], writes=['pb%d' % r])

            def emit_e(ii, n_):
                kt, g = items[ii]
                r = n_ % 4
                b.op('act', lambda e: e.activation(out=PT[r][:, 0:nq], in_=PS.pb[r][:, 0:nq], func=AF.Exp, bias=NBG[:, h * 2 + g:h * 2 + g + 1], scale=0.125),
                     reads=['pb%d' % r, 'NBG'], writes=['PT%d' % r])

            def emit_pv(ii, n_):
                kt, g = items[ii]
                r = n_ % 4
                for j in range(len(qts)):
                    ap_, res_ = acc(j, g)
                    st_ = res_ not in started
                    started.add(res_)
                    b.op('pe', lambda e: e.matmul(ap_, lhsT=PT[r][:, j * 128:(j + 1) * 128], rhs=VD[:, kt, h, 0:129], start=st_, stop=(kt == kts[-1])),
                         reads=['PT%d' % r, 'VD'], writes=[res_])

            emit_s(0, nit)
            for ii in range(len(items)):
                if ii + 1 < len(items):
                    emit_s(ii + 1, nit + ii + 1)
                emit_e(ii, nit + ii)
                emit_pv(ii, nit + ii)
            nit += len(items)
            nacc = 2 * len(qts)
            oa = OA[:, bp]; oar = 'OA%d' % bp
            for bk in range((nacc + 2) // 3):
                na = min(3, nacc - bk * 3)
                src = PS.pb[4 + bk][:, 0:na * 129]
                dst = oa[:, bk * 3:bk * 3 + na, :].rearrange("p a c -> p (a c)")
                if bk % 2 == 0:
                    b.op('act', lambda e: e.copy(out=dst, in_=src), reads=['pb%d' % (4 + bk)], writes=[oar], )
                else:
                    b.op('dve', lambda e: e.tensor_copy(out=dst, in_=src), reads=['pb%d' % (4 + bk)], writes=[oar])
            nj = len(qts)
            st = ST[:, bp]; sr = 'STD%d' % bp
            b.op('dve', lambda e: e.reciprocal(out=st[:, 0:nacc], in_=oa[:, 0:nacc, 128]), reads=[oar], writes=[sr])
            b.op('dve', lambda e: e.tensor_scalar(out=st[:, 8:8 + nj], in0=st[:, 0:nacc].rearrange("p (j g) -> p j g", g=2)[:, :, 1], scalar1=LAM[:, 3:4], scalar2=None, op0=ALU.mult),
                 reads=[sr, 'LAM'], writes=[sr])
            ot = OT[:, bp]; otr = 'OT%d' % bp
            oq = OQ[:, bp]; oqr = 'OQ%d' % bp
            for j in range(nj):
                b.op('dve', lambda e: e.tensor_scalar(out=ot[:, j, :], in0=oa[:, 2 * j + 1, 0:128], scalar1=st[:, 8 + j:9 + j], scalar2=None, op0=ALU.mult), reads=[oar, sr], writes=[otr])
                b.op('dve', lambda e: e.scalar_tensor_tensor(out=ot[:, j, :], in0=oa[:, 2 * j, 0:128], scalar=st[:, 2 * j:2 * j + 1], in1=ot[:, j, :], op0=ALU.mult, op1=ALU.add),
                     reads=[oar, sr, otr], writes=[otr])
            b.op('dve', lambda e: e.tensor_tensor(out=oq[:, 0:nj, :], in0=ot[:, 0:nj, :], in1=ot[:, 0:nj, :], op=ALU.mult), reads=[otr], writes=[oqr])
            ss = st[:, 12:12 + nj]
            b.op('dve', lambda e: e.tensor_reduce(out=ss, in_=oq[:, 0:nj, :], axis=AX.X, op=ALU.add), reads=[oqr], writes=[sr + 'n'])
            b.op('dve', lambda e: e.tensor_scalar(out=ss, in0=ss, scalar1=1.0 / 128, scalar2=EPS, op0=ALU.mult, op1=ALU.add), reads=[sr + 'n'], writes=[sr + 'n'])
            b.op('act', lambda e: e.sqrt(out=ss, in_=ss), reads=[sr + 'n'], writes=[sr + 'n'])
            b.op('dve', lambda e: e.reciprocal(out=ss, in_=ss), reads=[sr + 'n'], writes=[sr + 'n'])
            b.op('dve', lambda e: e.tensor_scalar(out=ss, in0=ss, scalar1=float(1.0 - lam_init), scalar2=None, op0=ALU.mult), reads=[sr + 'n'], writes=[sr + 'n'])
            for j in range(nj):
                b.op('dve', lambda e: e.scalar_tensor_tensor(out=HOD[:, h % 2, qts[j], :], in0=ot[:, j, :], scalar=st[:, 12 + j:13 + j], in1=SGN[:], op0=ALU.mult, op1=ALU.mult),
                     reads=[otr, sr + 'n', 'SGN'], writes=[hr])
        if callable(hmix):
            hmix(b, 512 + h * 128, HOD[:, h % 2], hr)
        else:
            b.dma('sp', hmix.rearrange("(t p) n -> p t n", p=128)[:, :, 512 + h * 128:512 + (h + 1) * 128], HOD[:, h % 2], reads=[hr], writes=[hmix_res], waw=False)
    b.barrier()


def emit_na(b, c, ar, nc, PS, u_dram, w_nh, w_nv, tt_d, cmat, hmix, hmix_res):
    P0 = SB_BASE
    uT, o = ar.at(P0, [128, KC, NTOK], BF16, "uT")
    QN, o = ar.at(o, [128, 4, 2048], BF16, "QN")
    KN, o = ar.at(o, [128, 4, NTOK], BF16, "KN")
    VN, o = ar.at(o, [128, NTIL, 512], BF16, "VN")
    VS, o = ar.at(o, [128, 15, 512], BF16, "VS")
    TT, o = ar.at(o, [64, 4, 16, 64], F32, "TT")
    WN = []
    for i in range(2):
        t_, o = ar.at(o, [128, KC, 256], BF16, "WN"); WN.append(t_)
    WV, o = ar.at(o, [128, KC, 512], BF16, "WV")
    NBN, o = ar.at(o, [128, 8], F32, "NBN")
    KX, o = ar.at(o, [128, 16], F32, "KXn")
    ONE, o = ar.at(o, [128, 128], F32, "ONE")
    SQ, o = ar.at(o, [128, 512], F32, "SQn")
    SWB, o = ar.at(o, [64, 2, 512], F32, "SWB")
    PW, o = ar.at(o, [64, 2, 768], BF16, "PW")
    PT, o = ar.at(o, [128, 2, 6, 64], BF16, "PT")
    o_hout = o
    HOUT, o = ar.at(o, [64, 1, 32, 128], BF16, "HOUT")
    ST, o = ar.at(o, [64, 32], F32, "STN")
    b.dma('sp', ONE[:], cmat[:, 3, :], writes=['ONE'])
    emit_load_uT(b, c, ar, o_hout, u_dram, 'u', NTIL, uT, PS)
    blks = blocks_of(NTOK)
    it = 0
    n = 0
    for grp in range(2):
        b.dma('sp', TT[:], tt_d[grp * 4:(grp + 1) * 4].rearrange("h w d c -> w h d c"), reads=['w'], writes=['TT'])
        for h in range(4):
            H = grp * 4 + h
            wn = WN[H % 2]; wr = 'WN%d' % (H % 2)
            b.dma('pool', wn[:], w_nh[H].rearrange("(c p) n -> p c n", p=128), reads=['w'], writes=[wr])
            for qk in range(2):
                for (s0, sl) in blks:
                    if qk == 0 and s0 >= 2048:
                        continue
                    pb = PS.pb[it % 4]; pr = 'pb%d' % (it % 4); it += 1
                    for k in range(KC):
                        b.op('pe', lambda e: e.matmul(pb[:, 0:sl], lhsT=wn[:, k, qk * 128:(qk + 1) * 128], rhs=uT[:, k, s0:s0 + sl],
                                                      start=(k == 0), stop=(k == KC - 1)), reads=[wr, 'uT'], writes=[pr])
                    if qk == 0:
                        b.op('act', lambda e: e.mul(out=QN[:, h, s0:s0 + sl], in_=pb[:, 0:sl], mul=128.0 ** -0.5), reads=[pr], writes=['QN'])
                    else:
                        b.op('dve', lambda e: e.tensor_copy(out=KN[:, h, s0:s0 + sl], in_=pb[:, 0:sl]), reads=[pr], writes=['KN'])
            pk = PS.pb[6]; pkr = 'pb6'
            for which, (src, sres, ntok) in enumerate(((QN, 'QN', 2048), (KN, 'KN', NTOK))):
                nbk = 0
                for s0 in range(0, ntok, 512):
                    sl = min(512, ntok - s0)
                    b.op('dve', lambda e: e.tensor_tensor(out=SQ[:, 0:sl], in0=src[:, h, s0:s0 + sl], in1=src[:, h, s0:s0 + sl], op=ALU.mult), reads=[sres], writes=['SQn'])
                    b.op('pe', lambda e: e.matmul(pk[0:1, 0:sl], lhsT=ONE[:, 0:1], rhs=SQ[:, 0:sl], start=True, stop=True), reads=['ONE', 'SQn'], writes=[pkr])
                    b.op('dve', lambda e: e.tensor_reduce(out=KX[0:1, nbk:nbk + 1], in_=pk[0:1, 0:sl], axis=AX.X, op=ALU.max), reads=[pkr], writes=['KXn'])
                    nbk += 1
                b.op('dve', lambda e: e.tensor_reduce(out=KX[0:1, 8 + which:9 + which], in_=KX[0:1, 0:nbk], axis=AX.X, op=ALU.max), reads=['KXn'], writes=['KXn'])
            b.op('dve', lambda e: e.tensor_tensor(out=KX[0:1, 10:11], in0=KX[0:1, 8:9], in1=KX[0:1, 9:10], op=ALU.mult), reads=['KXn'], writes=['KXn'])
            pq_ = PS.pb[7]; pqr = 'pb7'
            b.op('pe', lambda e: e.matmul(pq_[:, 0:1], lhsT=ONE[0:1, :], rhs=KX[0:1, 10:11], start=True, stop=True), reads=['ONE', 'KXn'], writes=[pqr])
            nbn = NBN[0:64, H:H + 1]
            b.op('act', lambda e: e.sqrt(out=nbn, in_=pq_[0:64, 0:1]), reads=[pqr], writes=['NBN'])
            b.op('dve', lambda e: e.tensor_reduce(out=KX[0:64, 12:13], in_=TT[:, h, 0:15, :].rearrange("p a c -> p (a c)"), axis=AX.X, op=ALU.max), reads=['TT'], writes=['KXn'])
            b.op('dve', lambda e: e.tensor_tensor(out=nbn, in0=nbn, in1=KX[0:64, 12:13], op=ALU.add), reads=['NBN', 'KXn'], writes=['NBN'])
            b.op('dve', lambda e: e.tensor_scalar(out=nbn, in0=nbn, scalar1=-1.0, scalar2=None, op0=ALU.mult), reads=['NBN'], writes=['NBN'])
        b.dma('pool', WV[:], w_nv[grp].rearrange("(c p) n -> p c n", p=128), reads=['w'], writes=['WV'])
        for t in range(NTIL + 15):
            pb = PS.pb[4 + t % 2]; pr = 'pb%d' % (4 + t % 2)
            if t < NTIL:
                tok0 = t * 128; dst = VN[:, t, :]; dr = 'VN'
            else:
                tok0 = 64 + (t - NTIL) * 128; dst = VS[:, t - NTIL, :]; dr = 'VS'
            for k in range(KC):
                b.op('pe', lambda e: e.matmul(pb[:, 0:512], lhsT=uT[:, k, tok0:tok0 + 128], rhs=WV[:, k, :], start=(k == 0), stop=(k == KC - 1)),
                     reads=['uT', 'WV'], writes=[pr])
            if t % 2 == 0:
                b.op('dve', lambda e: e.tensor_copy(out=dst, in_=pb[:, 0:512]), reads=[pr], writes=[dr])
            else:
                b.op('act', lambda e: e.copy(out=dst, in_=pb[:, 0:512]), reads=[pr], writes=[dr])
        if STOP == 30:
            b.barrier(); return
        units = [(h, r) for h in range(4) for r in range(32)]

        def se(i):
            h, r = units[i]
            par = i % 2
            rs = min(max(r - 4, 0), 24)
            dr0 = rs - r + 7
            pA = PS.pb[par]; rA = 'pb%d' % par
            pB = PS.pb[2 + par]; rB = 'pb%d' % (2 + par)
            qT = QN[:, h, r * 64:(r + 1) * 64]
            b.op('pe', lambda e: e.matmul(pA[0:64, 0:512], lhsT=qT, rhs=KN[:, h, rs * 64:rs * 64 + 512], start=True, stop=True), reads=['QN', 'KN'], writes=[rA])
            b.op('pe', lambda e: e.matmul(pB[0:64, 0:256], lhsT=qT, rhs=KN[:, h, 2048:2304], start=True, stop=True), reads=['QN', 'KN'], writes=[rB])
            sw = SWB[:, par, :]; swr = 'SWB%d' % par
            b.op('dve', lambda e: e.tensor_tensor(out=sw.rearrange("p (a c) -> p a c", c=64), in0=pA[0:64, 0:512].rearrange("p (a c) -> p a c", c=64),
                                                  in1=TT[:, h, dr0:dr0 + 8, :], op=ALU.add), reads=[rA, 'TT'], writes=[swr])
            s1 = ST[:, par * 8 + 3:par * 8 + 4]; s2 = ST[:, par * 8 + 4:par * 8 + 5]
            sr = 'STN%d' % par
            nb_ = NBN[0:64, grp * 4 + h:grp * 4 + h + 1]
            pw = PW[:, par, :]; pwr = 'PW%d' % par
            b.op('act', lambda e: e.activation(out=pw[:, 0:512], in_=sw, func=AF.Exp, bias=nb_, scale=1.0, accum_out=s1), reads=[swr, 'NBN'], writes=[pwr, sr + 'a'])
            b.op('act', lambda e: e.activation(out=pw[:, 512:768], in_=pB[0:64, 0:256], func=AF.Exp, bias=nb_, scale=1.0, accum_out=s2), reads=[rB, 'NBN'], writes=[pwr, sr + 'b'])
            b.op('dve', lambda e: e.tensor_tensor(out=s2, in0=s1, in1=s2, op=ALU.add), reads=[sr + 'a', sr + 'b'], writes=[sr + 'b'])
            b.op('dve', lambda e: e.reciprocal(out=s2, in_=s2), reads=[sr + 'b'], writes=[sr + 'b'])

        def tp(i):
            h, r = units[i]
            H = grp * 4 + h
            par = i % 2
            rs = min(max(r - 4, 0), 24)
            pO = PS.pb[4 + par]; rO = 'pb%d' % (4 + par)
            p = PS.psT[par]; rT = 'psT%d' % par
            pw = PW[:, par, :]; pwr = 'PW%d' % par
            s2 = ST[:, par * 8 + 4:par * 8 + 5]
            sr = 'STN%d' % par
            for cc in range(6):
                b.op('pe', lambda e: e.transpose(p[:, cc, 0:64], pw[:, cc * 128:(cc + 1) * 128], c.idb[0:64, 0:64]), reads=[pwr, 'idb'], writes=[rT])
            pt = PT[:, par]; ptr_ = 'PT%d' % par
            b.op('act', lambda e: e.copy(out=pt, in_=p[:, 0:6, 0:64]), reads=[rT], writes=[ptr_])
            for cc in range(6):
                if cc < 4:
                    if rs % 2 == 0:
                        rhs = VN[:, rs // 2 + cc, h * 128:(h + 1) * 128]; vr = 'VN'
                    else:
                        rhs = VS[:, (rs - 1) // 2 + cc, h * 128:(h + 1) * 128]; vr = 'VS'
                else:
                    rhs = VN[:, 16 + cc - 4, h * 128:(h + 1) * 128]; vr = 'VN'
                b.op('pe', lambda e: e.matmul(pO[0:64, 0:128], lhsT=pt[:, cc, :], rhs=rhs, start=(cc == 0), stop=(cc == 5)), reads=[ptr_, vr], writes=[rO])
            b.op('dve', lambda e: e.tensor_scalar(out=HOUT[:, 0, r, :], in0=pO[0:64, 0:128], scalar1=s2, scalar2=None, op0=ALU.mult), reads=[rO, sr + 'b'], writes=['HOUT0'])
            if r == 31:
                if callable(hmix):
                    hmix(b, H * 128, HOUT[:, 0], 'HOUT0')
                else:
                    b.dma('sp', hmix.rearrange("(r w) n -> w r n", w=64)[:, :, H * 128:(H + 1) * 128], HOUT[:, 0], reads=['HOUT0'], writes=[hmix_res], waw=False)

        se(0)
        for i in range(len(units)):
            if i + 1 < len(units):
                se(i + 1)
            tp(i)
    b.barrier()


class PSum:
    def __init__(self, nc, es):
        big = [es.enter_context(nc.psum_tensor("pq%d" % i, [128, 2048], F32)) for i in range(2)]
        self.pq = big
        self.pb = [big[i // 4][:, (i % 4) * 512:(i % 4 + 1) * 512] for i in range(8)]
        self.psT = [self.pb[6 + i].bitcast(BF16).rearrange("p (c n) -> p c n", n=128) for i in range(2)]


def emit_mod(b, c, ar, nc, PS, cvec, w_ada_l, b_ada_l, modrow, mres):
    o = SB_BASE
    CT, o = ar.at(o, [128, KC, 2], F32, "CT")
    WA = []
    for i in range(2):
        t_, o = ar.at(o, [128, KC, 512], F32, "WA"); WA.append(t_)
    BA, o = ar.at(o, [2, 4608], F32, "BA")
    MR, o = ar.at(o, [2, 4608], F32, "MR")
    for r in range(2):
        b.dma('sp', CT[:, :, r:r + 1], cvec[r].rearrange("(c p o) -> p c o", p=128, o=1), writes=['CT'], waw=False, allow_slow_non_contiguous=True)
    b.op('act', lambda e: e.activation(out=CT[:], in_=CT[:], func=AF.Silu), reads=['CT'], writes=['CT'])
    wv = w_ada_l.rearrange("(c p) n -> p c n", p=128)
    for qq in range(4):
        b.dma('sp', BA[:], b_ada_l[qq * 4608:(qq + 1) * 4608].partition_broadcast(2), writes=['BA'])
        for j in range(9):
            nb = qq * 9 + j
            wa = WA[nb % 2]; wr = 'WA%d' % (nb % 2)
            b.dma('sp' if nb % 2 == 0 else 'pool', wa[:], wv[:, :, nb * 512:(nb + 1) * 512], reads=['w'], writes=[wr])
            pb = PS.pb[nb % 2]; pr = 'pb%d' % (nb % 2)
            for k in range(KC):
                b.op('pe', lambda e: e.matmul(pb[0:2, 0:512], lhsT=CT[:, k, :], rhs=wa[:, k, :], start=(k == 0), stop=(k == KC - 1)), reads=['CT', wr], writes=[pr])
            b.op('dve', lambda e: e.tensor_tensor(out=MR[:, j * 512:(j + 1) * 512], in0=pb[0:2, 0:512], in1=BA[:, j * 512:(j + 1) * 512], op=ALU.add),
                 reads=[pr, 'BA'], writes=['MR'])
        b.dma('sp', modrow[:, qq * 4608:(qq + 1) * 4608], MR[:], reads=['MR'], writes=[mres], waw=False)
    b.barrier()


def emit_ffn(b, c, ar, nc, PS, x_dram, x_res, xo_dram, xo_res, tiles, T, modrow, g_pre, g_post, imod, w_in, w_out, premix=None):
    actT, R1 = ar.at(SB_BASE, [128, FC, T], BF16, "actT")
    uT, e1 = ar.at(R1, [128, KC, T], BF16, "uT")
    emit_pre(b, c, ar, nc, e1, x_dram, x_res, tiles, modrow, g_pre, imod[0], imod[1], uT=uT, ps=PS.psT)
    b.barrier()
    SG, e2 = ar.at(e1 + 4 * 8192, [128, T], F32, "SG")
    emit_ffn_m1(b, ar, nc, e1, uT, T, w_in, 'w', actT, PS.pb, SG)
    b.barrier()
    outT, e3 = ar.at(R1, [128, KC, T], F32, "outT")
    emit_linear_T(b, ar, nc, e3, actT, 'actT', FC, T, w_out, 'w', D, outT, 'outT', PS.pb, colblk=128, nbuf=3)
    b.barrier()
    emit_post(b, c, ar, nc, SB_BASE, outT, 'outT', x_dram, x_res, xo_dram, xo_res, tiles, modrow, g_post, imod[2], 0.5, PS.pq, premix=premix)
    b.barrier()


def emit_premix(b, c, ar, nc, PS, x_dram, x_res, u_dram, u_res, tiles, modrow, g_pre):
    emit_pre(b, c, ar, nc, SB_BASE, x_dram, x_res, tiles, modrow, g_pre, 3, 4, uT=None, u_out=u_dram, u_res=u_res)
    b.barrier()


def emit_postmix(b, c, ar, nc, PS, h_dram, h_res, w_mix, x_dram, x_res, xo_dram, xo_res, tiles, T, modrow, g_post):
    hT, o = ar.at(SB_BASE, [128, KC, T], BF16, "hT")
    outT, o2 = ar.at(o, [128, KC, T], F32, "outT")
    off = o2
    UB = []
    for i in range(2):
        t_, off = ar.at(off, [128, D], BF16, "HBl"); UB.append(t_)
    off2 = off
    off = off + 3 * 4096 + 64
    for n, (t, mr) in enumerate(tiles):
        ub = UB[n % 2]; ur = 'HBl%d' % (n % 2)
        if callable(h_dram):
            h_dram(b, n, ub, ur, off2)
        else:
            b.dma('sp', ub[:], h_dram[t * 128:(t + 1) * 128, :], reads=[h_res], writes=[ur])
        emit_transpose_tile(b, c, ub, ur, hT, 'hT', t, KC, PS.psT, n)
    emit_linear_T(b, ar, nc, off, hT, 'hT', KC, T, w_mix, 'w', D, outT, 'outT', PS.pb, colblk=256, nbuf=2)
    b.barrier()
    emit_post(b, c, ar, nc, o2, outT, 'outT', x_dram, x_res, xo_dram, xo_res, tiles, modrow, g_post, 5, 1.0, PS.pq)
    b.barrier()


MODC = 9 * D // 8


def emit_mod_sharded(b, c, ar, nc, PS, cvec_all, w_ada_sl, b_ada_sl, selmod, ml_t, G_t, modrows):
    o = SB_BASE
    CT, o = ar.at(o, [128, KC, 5], F32, "CT")
    WA = []
    for i in range(2):
        t_, o = ar.at(o, [128, KC, 512], F32, "WA"); WA.append(t_)
    BA, o = ar.at(o, [5, MODC], F32, "BA")
    ML, o = ar.at(o, [5, MODC], F32, "ML")
    GS = []
    for i in range(2):
        t_, o = ar.at(o, [5, MODC], F32, "GS"); GS.append(t_)
    MR = []
    for i in range(2):
        t_, o = ar.at(o, [2, MODC], F32, "MR"); MR.append(t_)
    SM, o = ar.at(o, [5, 2], F32, "SM")
    b.dma('sp', SM[:], selmod, writes=['SM'])
    for r in range(5):
        b.dma('sp', CT[:, :, r:r + 1], cvec_all[r].rearrange("(c p o) -> p c o", p=128, o=1), writes=['CT'], waw=False, allow_slow_non_contiguous=True)
    b.op('act', lambda e: e.activation(out=CT[:], in_=CT[:], func=AF.Silu), reads=['CT'], writes=['CT'])
    cblks = [(0, 512), (512, 512), (1024, 512), (1536, 512), (2048, 256)]
    n = 0
    for l in range(2):
        wv = w_ada_sl[l].rearrange("(c p) n -> p c n", p=128)
        b.dma('sp', BA[:], b_ada_sl[l].partition_broadcast(5), reads=['w'], writes=['BA'])
        for (c0, cl) in cblks:
            wa = WA[n % 2]; wr = 'WA%d' % (n % 2)
            pb = PS.pb[n % 2]; pr = 'pb%d' % (n % 2)
            b.dma('sp' if n % 2 == 0 else 'pool', wa[:, :, 0:cl], wv[:, :, c0:c0 + cl], reads=['w'], writes=[wr])
            n += 1
            for k in range(KC):
                b.op('pe', lambda e: e.matmul(pb[0:5, 0:cl], lhsT=CT[:, k, :], rhs=wa[:, k, 0:cl], start=(k == 0), stop=(k == KC - 1)), reads=['CT', wr], writes=[pr])
            b.op('dve', lambda e: e.tensor_tensor(out=ML[:, c0:c0 + cl], in0=pb[0:5, 0:cl], in1=BA[:, c0:c0 + cl], op=ALU.add), reads=[pr, 'BA'], writes=['ML'])
        b.dma('sp', ml_t[l].ap(), ML[:], reads=['ML'], writes=['ml%d' % l])
        b.allgather(ml_t[l], G_t[l], [list(range(8))], ['ml%d' % l], ['G%d' % l])
    m = 0
    for l in range(2):
        for r in range(8):
            gs = GS[m % 2]; gr = 'GS%d' % (m % 2)
            mr = MR[m % 2]; mrr = 'MR%d' % (m % 2)
            m += 1
            b.dma('sp', gs[:], G_t[l].ap()[r * 5:(r + 1) * 5, :], reads=['G%d' % l], writes=[gr])
            for (c0, cl) in cblks:
                pb = PS.pb[2 + n % 2]; pr = 'pb%d' % (2 + n % 2); n += 1
                b.op('pe', lambda e: e.matmul(pb[0:2, 0:cl], lhsT=SM[:], rhs=gs[:, c0:c0 + cl], start=True, stop=True), reads=['SM', gr], writes=[pr])
                b.op('act', lambda e: e.copy(out=mr[:, c0:c0 + cl], in_=pb[0:2, 0:cl]), reads=[pr], writes=[mrr])
            b.dma('sp', modrows[l][:, r * MODC:(r + 1) * MODC], mr[:], reads=[mrr], writes=['modrow'], waw=False)
    b.barrier()


import numpy as np
import ml_dtypes
BF = ml_dtypes.bfloat16

def cmat():
    s = np.arange(128)[:, None]; t = np.arange(128)[None, :]
    cm = np.zeros((128, 5, 128), np.float32)
    cm[:, 0] = (s == t)
    cm[:, 1] = (s <= t)
    cm[:, 2] = (s >= t)
    cm[:, 3] = 1.0
    for p2 in range(128):
        j = (p2 // 16) % 2
        if j == 0:
            cm[p2 + 16, 4, p2] = -1.0
        else:
            cm[p2 - 16, 4, p2] = 1.0
    return cm

def pack_even_mlstm(z, hh):
    W = z['w_in_even'][0]
    Hs = [4 * hh + h for h in range(4)]
    w_mh = np.stack([np.concatenate([W[:, g * 1024 + H * 128: g * 1024 + (H + 1) * 128] for g in range(4)], 1) for H in Hs], 0)
    w_g = np.stack([W[:, 4096 + g * 8 + H] for g in range(4) for H in Hs], 1)
    conv = z['mlstm_conv'][0]
    cw = np.zeros((128, 8, 5), np.float32)
    for h, H in enumerate(Hs):
        cw[:, h * 2 + 0, :] = conv[:, H * 128:(H + 1) * 128].T
        cw[:, h * 2 + 1, :] = conv[:, 1024 + H * 128:1024 + (H + 1) * 128].T
    gb = np.array([z['mlstm_gate_b'][0][g, H] for g in range(4) for H in Hs], np.float32)
    hg = z['mlstm_head_g'][0][4 * hh * 128:(4 * hh + 4) * 128]
    return dict(w_mh=np.ascontiguousarray(w_mh), w_g=np.ascontiguousarray(w_g), conv_w=cw, gate_b=gb, head_g=np.ascontiguousarray(hg))

def pack_even_diff(z, hh):
    W = z['w_in_even'][0]
    Hs = [4 * hh + h for h in range(4)]
    base = 4128
    w_dh = np.stack([np.concatenate([W[:, base + g * 1024 + H * 128: base + g * 1024 + (H + 1) * 128] for g in range(3)], 1) for H in Hs], 0)
    return dict(w_dh=np.ascontiguousarray(w_dh), lam_p=np.ascontiguousarray(z['diff_lambda'][0].reshape(-1)), subln_g=z['diff_subln_g'][0])

def rope_tables():
    t = np.arange(2048)
    row = (t // 64).astype(np.float32); col = (t % 64).astype(np.float32)
    inv = (np.float32(10000.0) ** (-np.arange(16, dtype=np.float32) / 16)).astype(np.float32)
    ang = np.stack([row[:, None] * inv, col[:, None] * inv], 1)
    cos = np.zeros((128, 2048), np.float32); sin = np.zeros((128, 2048), np.float32)
    for p in range(128):
        A = (p // 32) % 2; i = p % 16
        cos[p] = np.cos(ang[:, A, i]); sin[p] = np.sin(ang[:, A, i])
    return cos, sin

NEG = -30000.0
def pack_na(z, hh):
    W = z['w_in_odd'][0]
    Hs = [8 * hh + h for h in range(8)]
    w_nh = np.stack([np.concatenate([W[:, g * 2048 + H * 128: g * 2048 + (H + 1) * 128] for g in range(2)], 1) for H in Hs], 0)
    w_nv = np.stack([W[:, 4096 + Hs[4 * gg] * 128: 4096 + (Hs[4 * gg] + 4) * 128] for gg in range(2)], 0)
    rpb = z['na_rpb'][0]
    w = np.arange(64)
    c_start = np.clip(w - 8, 0, 48)
    kc = np.arange(64)
    inwin = (kc[None, :] >= c_start[:, None]) & (kc[None, :] < c_start[:, None] + 16)
    cidx = np.clip(kc[None, :] - w[:, None] + 15, 0, 30)
    tt = np.full((8, 64, 16, 64), NEG, np.float32)
    for i, H in enumerate(Hs):
        for dr in range(15):
            vals = rpb[H, dr][cidx]
            tt[i, :, dr, :] = np.where(inwin, vals, np.float32(NEG))
    return dict(w_nh=np.ascontiguousarray(w_nh), w_nv=np.ascontiguousarray(w_nv), tt=tt)


TX = 1152
TILES9 = [(t, 0) for t in range(8)] + [(8, 1)]
TILES8 = [(t, 0) for t in range(8)]
PAIRS = [[0, 1], [2, 3], [4, 5], [6, 7]]


def _din(nc, name, shape, dt=F32):
    return nc.dram_tensor(name, list(shape), dt, kind="ExternalInput").ap()


def _dout(nc, name, shape, dt=F32):
    return nc.dram_tensor(name, list(shape), dt, kind="ExternalOutput").ap()


class Exch:
    def __init__(self, nc, name, ntile, tpc, ncol):
        self.name = name; self.ntile = ntile; self.tpc = tpc; self.ncol = ncol
        self.nch = (ntile + tpc - 1) // tpc
        self.src = [nc.dram_tensor("%s_s%d" % (name, k), [tpc * 128, ncol], BF16) for k in range(self.nch)]
        self.dst = [nc.dram_tensor("%s_g%d" % (name, k), [2 * tpc * 128, ncol], BF16) for k in range(self.nch)]

    def sres(self, k):
        return "%s_s%d" % (self.name, k)

    def gres(self, k):
        return "%s_g%d" % (self.name, k)

    def local_tile(self, t):
        k = t // self.tpc
        r0 = (t % self.tpc) * 128
        return self.src[k].ap()[r0:r0 + 128, :], self.sres(k)

    def gathered_tile(self, rank, t):
        k = t // self.tpc
        r0 = rank * self.tpc * 128 + (t % self.tpc) * 128
        return self.dst[k].ap()[r0:r0 + 128, :], self.gres(k)

    def gather(self, b):
        for k in range(self.nch):
            b.allgather(self.src[k], self.dst[k], PAIRS, [self.sres(k)], [self.gres(k)])


def build_fused():
    nc = bass.Bass("TRN2", target_bir_lowering=False)
    x = _din(nc, "x", [TX, D]); cvec = _din(nc, "cvec", [5, D]); sel = _din(nc, "sel", [128, 2]); selmod = _din(nc, "selmod", [5, 2])
    w_ada = _din(nc, "w_ada", [2, D, MODC]); b_ada = _din(nc, "b_ada", [2, MODC]); g = _din(nc, "g", [2, 6, D])
    w_in = _din(nc, "w_in", [2, 2, D, 2 * DFF]); w_out = _din(nc, "w_out", [2, 2, DFF, D]); ident = _din(nc, "ident", [128, 128])
    w_mh = _din(nc, "w_mh", [4, D, 512]); w_g = _din(nc, "w_g", [D, 16]); conv_w = _din(nc, "conv_w", [128, 8, 5])
    gate_b = _din(nc, "gate_b", [16]); head_g = _din(nc, "head_g", [512]); cm = _din(nc, "cmat", [128, 5, 128])
    w_dh = _din(nc, "w_dh", [4, D, 384]); cos_d = _din(nc, "cos", [128, 2048]); sin_d = _din(nc, "sin", [128, 2048])
    lam_p = _din(nc, "lam_p", [256]); subln_g = _din(nc, "subln_g", [128])
    w_mix0 = _din(nc, "w_mix0", [D, D]); w_mix1 = _din(nc, "w_mix1", [D, D])
    w_nh = _din(nc, "w_nh", [8, D, 256]); w_nv = _din(nc, "w_nv", [2, D, 512]); tt = _din(nc, "tt", [8, 64, 16, 64])
    y = _dout(nc, "y", [1024, D])
    modrow0 = nc.dram_tensor("modrow0", [2, 9 * D], F32).ap(); modrow1 = nc.dram_tensor("modrow1", [2, 9 * D], F32).ap()
    ml_t = [nc.dram_tensor("ml%d" % l, [5, MODC], F32) for l in range(2)]
    G_t = [nc.dram_tensor("mg%d" % l, [40, MODC], F32) for l in range(2)]
    xs = [nc.dram_tensor("xs%d" % i, [TX, D], F32).ap() for i in range(6)]
    EU0 = Exch(nc, "eu0", 9, 3, D); EU1 = Exch(nc, "eu1", 9, 3, D)
    EH0 = Exch(nc, "eh0", 18, 6, 1024); EH1 = Exch(nc, "eh1", 16, 4, 1024)

    def full_u(E):
        def f(T):
            if T < 8:
                return E.gathered_tile(0, T)
            if T < 16:
                return E.gathered_tile(1, T - 8)
            return E.gathered_tile(T - 16, 8)
        return f

    def h_writer0(b, col0, HO, hres):
        for k in range(EH0.nch):
            dst = EH0.src[k].ap().rearrange("(t p) n -> p t n", p=128)[:, :, col0:col0 + 128]
            b.dma('sp', dst, HO[:, k * 6:(k + 1) * 6, :], reads=[hres], writes=[EH0.sres(k)], waw=False)

    def h_writer1(b, col0, HOUT, hres):
        for k in range(EH1.nch):
            dst = EH1.src[k].ap().rearrange("(r w) n -> w r n", w=64)[:, :, col0:col0 + 128]
            b.dma('sp', dst, HOUT[:, k * 8:(k + 1) * 8, :], reads=[hres], writes=[EH1.sres(k)], waw=False)

    with ExitStack() as es:
        b = Bld(nc, es); ar = Arena(nc); PS = PSum(nc, es)
        c = setup_consts(b, ar, nc, ident, CONST_OFF)
        SEL, _ = ar.at(c.off, [128, 2], F32, "SEL")
        b.dma('sp', SEL[:], sel, writes=['SEL'])

        def h_loader(E, layer):
            def f(b, n, ub, ur, off2):
                A, o = ar.at(off2, [128, D], BF16, "hA")
                B, o = ar.at(o, [128, D], BF16, "hB")
                for (cand, buf, br) in ((0, A, 'hA'), (1, B, 'hB')):
                    if layer == 0:
                        T = (n if n < 8 else 16) if cand == 0 else (8 + n if n < 8 else 17)
                        for r in range(2):
                            sap, sres = E.gathered_tile(r, T)
                            b.dma('sp', buf[:].rearrange("p (k r c) -> p k r c", k=2, r=2, c=512)[:, :, r, :], sap.rearrange("p (k c) -> p k c", k=2),
                                  reads=[sres], writes=[br], waw=False)
                    else:
                        T = n if cand == 0 else 8 + n
                        for r in range(2):
                            sap, sres = E.gathered_tile(r, T)
                            b.dma('sp', buf[:, r * 1024:(r + 1) * 1024], sap, reads=[sres], writes=[br], waw=False)
                b.op('dve', lambda e: e.tensor_scalar(out=A[:], in0=A[:], scalar1=SEL[:, 0:1], scalar2=None, op0=ALU.mult), reads=['hA', 'SEL'], writes=['hA'])
                b.op('dve', lambda e: e.scalar_tensor_tensor(out=ub[:], in0=B[:], scalar=SEL[:, 1:2], in1=A[:], op0=ALU.mult, op1=ALU.add),
                     reads=['hB', 'hA', 'SEL'], writes=[ur])
            return f

        emit_mod_sharded(b, c, ar, nc, PS, cvec, w_ada, b_ada, selmod, ml_t, G_t, [modrow0, modrow1])
        emit_ffn(b, c, ar, nc, PS, x, 'x', xs[0], 'xs0', TILES9, TX, modrow0, g[0, 0], g[0, 1], (0, 1, 2), w_in[0, 0], w_out[0, 0],
                 premix=dict(u_out=EU0.local_tile, g_pre=g[0, 2], i_shift=3, i_scale=4))
        EU0.gather(b)
        emit_mlstm(b, c, ar, nc, PS, full_u(EU0), w_mh, w_g, conv_w, gate_b, head_g, cm, h_writer0, None)
        emit_diff(b, c, ar, nc, PS, full_u(EU0), w_dh, cos_d, sin_d, lam_p, subln_g, cm, h_writer0, None, 0.2)
        EH0.gather(b)
        emit_postmix(b, c, ar, nc, PS, h_loader(EH0, 0), None, w_mix0, xs[0], 'xs0', xs[1], 'xs1', TILES9, TX, modrow0, g[0, 3])
        emit_ffn(b, c, ar, nc, PS, xs[1], 'xs1', xs[2], 'xs2', TILES9, TX, modrow0, g[0, 4], g[0, 5], (6, 7, 8), w_in[0, 1], w_out[0, 1])
        emit_ffn(b, c, ar, nc, PS, xs[2], 'xs2', xs[3], 'xs3', TILES9, TX, modrow1, g[1, 0], g[1, 1], (0, 1, 2), w_in[1, 0], w_out[1, 0],
                 premix=dict(u_out=EU1.local_tile, g_pre=g[1, 2], i_shift=3, i_scale=4))
        EU1.gather(b)
        emit_na(b, c, ar, nc, PS, full_u(EU1), w_nh, w_nv, tt, cm, h_writer1, None)
        EH1.gather(b)
        emit_postmix(b, c, ar, nc, PS, h_loader(EH1, 1), None, w_mix1, xs[3], 'xs3', xs[4], 'xs4', TILES8, 1024, modrow1, g[1, 3])
        emit_ffn(b, c, ar, nc, PS, xs[4], 'xs4', y, 'y', TILES8, 1024, modrow1, g[1, 4], g[1, 5], (6, 7, 8), w_in[1, 1], w_out[1, 1])
        b.finish(['y'])
        print('fused program: ninst', b.ninst, 'nwaits', b.nwaits, 'sems', len(b.semh), 'cnt', {k: v for k, v in b.cnt.items() if isinstance(k, str)})
    return nc


def kernel(x, c, ctx, c_ctx, w_ada, b_ada, norm_g, w_ffn_in, w_ffn_out, w_in_even, w_out_even,
           mlstm_conv, mlstm_gate_b, mlstm_head_g, diff_lambda, diff_subln_g, w_in_odd, w_out_odd, na_rpb):
    z = dict(x=x, c=c, ctx=ctx, c_ctx=c_ctx, w_ada=w_ada, b_ada=b_ada, norm_g=norm_g, w_ffn_in=w_ffn_in, w_ffn_out=w_ffn_out,
             w_in_even=w_in_even, w_out_even=w_out_even, mlstm_conv=mlstm_conv, mlstm_gate_b=mlstm_gate_b, mlstm_head_g=mlstm_head_g,
             diff_lambda=diff_lambda, diff_subln_g=diff_subln_g, w_in_odd=w_in_odd, w_out_odd=w_out_odd, na_rpb=na_rpb)
    z = {k: np.ascontiguousarray(np.asarray(v, dtype=np.float32)) for k, v in z.items()}
    ident = np.eye(128, dtype=np.float32)
    ncores = 8
    cm = cmat(); cos, sin = rope_tables()
    packs = []
    for hh in range(2):
        m = pack_even_mlstm(z, hh); m.update(pack_even_diff(z, hh)); m.update(pack_na(z, hh))
        packs.append(m)
    maps = []
    for cid in range(ncores):
        bb, hh = cid // 2, cid % 2
        xt = np.concatenate([z['x'][bb, hh * 1024:(hh + 1) * 1024], z['ctx'][bb, hh * 128:(hh + 1) * 128]], 0)
        selv = np.zeros((128, 2), np.float32); selv[:, hh] = 1.0
        selm = np.zeros((5, 2), np.float32); selm[bb, 0] = 1.0; selm[4, 1] = 1.0
        m = dict(x=np.ascontiguousarray(xt), cvec=np.concatenate([z['c'], z['c_ctx'][None]], 0), sel=selv, selmod=selm,
                 w_ada=np.ascontiguousarray(z['w_ada'][:, :, cid * MODC:(cid + 1) * MODC]), b_ada=np.ascontiguousarray(z['b_ada'][:, cid * MODC:(cid + 1) * MODC]), g=z['norm_g'],
                 w_in=z['w_ffn_in'], w_out=z['w_ffn_out'], ident=ident, cmat=cm, cos=cos, sin=sin,
                 w_mix0=z['w_out_even'][0], w_mix1=z['w_out_odd'][0])
        m.update(packs[hh])
        maps.append(m)
    res = run_bass_kernel_spmd(build_fused(), maps, core_ids=list(range(ncores)))
    out = np.zeros((4, 2048, D), np.float32)
    for cid in range(ncores):
        bb, hh = cid // 2, cid % 2
        out[bb, hh * 1024:(hh + 1) * 1024] = res.results[cid]['y']
    return out
```

```python
import numpy as np
import concourse.bass as bass
import concourse.mybir as mybir
from concourse.bass_utils import run_bass_kernel_spmd
from contextlib import ExitStack

F32 = mybir.dt.float32
BF16 = mybir.dt.bfloat16
AF = mybir.ActivationFunctionType
ALU = mybir.AluOpType
AX = mybir.AxisListType

SAME_ENGINE_SYNC = True


class Bld:
    def __init__(self, nc, es):
        self.nc = nc
        self.es = es
        self.eng = {'pe': nc.tensor, 'act': nc.scalar, 'dve': nc.vector, 'pool': nc.gpsimd, 'sp': nc.sync}
        self.semh = {}
        self.cnt = {}
        for k in ['pe', 'act', 'dve', 'pool']:
            self.semh[k] = es.enter_context(nc.semaphore("s_" + k))
            self.cnt[k] = 0
        self.seen = {k: {} for k in self.eng}
        self.lastw = {}
        self.readers = {}
        self.nwaits = 0
        self.ninst = 0

    def _deps(self, reads, writes, waw=True):
        need = {}
        def add(t):
            if t is None:
                return
            s, v = t
            if need.get(s, 0) < v:
                need[s] = v
        for r in reads:
            add(self.lastw.get(r))
        for w in writes:
            if waw:
                add(self.lastw.get(w))
            for s, v in self.readers.get(w, {}).items():
                add((s, v))
        return need

    def _wait(self, e, need):
        for s, v in need.items():
            if s == e and (e == 'pe' or not SAME_ENGINE_SYNC):
                continue
            if self.seen[e].get(s, 0) >= v:
                continue
            self.eng[e].wait_ge(self.semh[s], v)
            self.seen[e][s] = v
            self.nwaits += 1

    def _commit(self, tok, reads, writes):
        s, v = tok
        for r in reads:
            d = self.readers.setdefault(r, {})
            if d.get(s, 0) < v:
                d[s] = v
        for w in writes:
            self.lastw[w] = tok
            self.readers[w] = {}

    def op(self, e, fn, reads=(), writes=()):
        need = self._deps(reads, writes)
        self._wait(e, need)
        inst = fn(self.eng[e])
        self.cnt[e] += 1
        inst.then_inc(self.semh[e], 1)
        self._commit((e, self.cnt[e]), reads, writes)
        self.ninst += 1
        return inst

    def dma(self, q, out, in_, reads=(), writes=(), stream=None, waw=True, **kw):
        need = self._deps(reads, writes, waw=waw)
        self._wait(q, need)
        if stream is None:
            stream = writes[0]
        key = ('dma', stream)
        if key not in self.semh:
            self.semh[key] = self.es.enter_context(self.nc.semaphore("d%d" % len(self.semh)))
            self.cnt[key] = 0
        inst = self.eng[q].dma_start(out=out, in_=in_, **kw)
        self.cnt[key] += 16
        inst.then_inc(self.semh[key], 16)
        self._commit((key, self.cnt[key]), reads, writes)
        self.ninst += 1
        return inst

    def finish(self, resources, e='sp'):
        need = {}
        for r in resources:
            t = self.lastw.get(r)
            if t is not None and need.get(t[0], 0) < t[1]:
                need[t[0]] = t[1]
        self._wait(e, need)


def _barrier(self):
    for e in self.eng:
        need = {s: v for s, v in self.cnt.items() if v > 0}
        self._wait(e, need)
Bld.barrier = _barrier


def _allgather(self, src_t, dst_t, groups, reads, writes):
    need = self._deps(reads, writes)
    self._wait('pool', need)
    key = ('cc', 'cc')
    if key not in self.semh:
        self.semh[key] = self.es.enter_context(self.nc.semaphore("ccsem"))
        self.cnt[key] = 0
    inst = self.nc.gpsimd.collective_compute("AllGather", ALU.bypass, replica_groups=groups, ins=[src_t.ap().opt()], outs=[dst_t.ap().opt()])
    self.cnt[key] += 1
    inst.then_inc(self.semh[key])
    self._commit((key, self.cnt[key]), reads, writes)
    self.ninst += 1
    return inst
Bld.allgather = _allgather


def _dma_split(self, q, out, in_, nsplit, reads=(), writes=(), **kw):
    nc_ = out.shape[1]
    step = (nc_ + nsplit - 1) // nsplit
    first = True
    for c0 in range(0, nc_, step):
        c1 = min(nc_, c0 + step)
        self.dma(q, out[:, c0:c1], in_[:, c0:c1], reads=reads, writes=writes, waw=first, **kw)
        first = False
Bld.dma_split = _dma_split


D = 2048
DFF = 5632
KC = D // 128
FC = DFF // 128
EPS = 1e-6

SB_BASE = 16512
SB_TOTAL = 229344
CONST_OFF = SB_TOTAL - 1152


class Arena:
    def __init__(self, nc):
        self.nc = nc
        self.n = 0
    def at(self, off, shape, dt, name=None):
        self.n += 1
        nm = "%s_%d" % (name or "t", self.n)
        esz = 4 if dt == F32 else 2
        sz = esz
        for s in shape[1:]:
            sz *= s
        assert off % 32 == 0 and off + sz <= SB_TOTAL, (nm, off, sz)
        return self.nc.alloc_sbuf_tensor_at(nm, list(shape), dt, offset=off), (off + sz + 31) // 32 * 32


def blocks_of(T):
    nt = T // 128
    nb = (T + 511) // 512
    per = (nt + nb - 1) // nb
    out = []
    s = 0
    while s < nt:
        e = min(nt, s + per)
        out.append((s * 128, (e - s) * 128))
        s = e
    return out


class Ctx:
    pass


def setup_consts(b, ar, nc, ident_dram, off):
    c = Ctx()
    c.idb, off = ar.at(off, [128, 128], BF16, "idb")
    c.idf, off = ar.at(off, [128, 128], F32, "idf")
    c.stat, off = ar.at(off, [128, 64], F32, "stat")
    b.dma('pool', c.idb[:], ident_dram, writes=['idb'])
    b.dma('sp', c.idf[:], ident_dram, writes=['idf'])
    c.off = off
    c.statn = 0
    return c


def emit_rstd(b, c, src_ap, src_res, junk_ap, junk_res, n):
    i = c.statn % 16
    c.statn += 1
    ss = c.stat[:, 4 * i:4 * i + 1]
    rs = c.stat[:, 4 * i + 1:4 * i + 2]
    r1 = 'stat%d' % i
    b.op('act', lambda e: e.activation(out=junk_ap, in_=src_ap, func=AF.Square, accum_out=ss),
         reads=[src_res], writes=[junk_res, r1])
    b.op('dve', lambda e: e.tensor_scalar(out=rs, in0=ss, scalar1=1.0 / n, scalar2=EPS, op0=ALU.mult, op1=ALU.add),
         reads=[r1], writes=[r1 + 'b'])
    b.op('act', lambda e: e.sqrt(out=rs, in_=rs), reads=[r1 + 'b'], writes=[r1 + 'b'])
    b.op('dve', lambda e: e.reciprocal(out=rs, in_=rs), reads=[r1 + 'b'], writes=[r1 + 'b'])
    return rs, r1 + 'b'


def emit_pre(b, c, ar, nc, base, x_dram, x_res, tiles, modrow, g_dram, i_shift, i_scale, uT=None, u_out=None, u_res=None, ps=None):
    off = base
    XT = []
    for i in range(2):
        t_, off = ar.at(off, [128, D], F32, "XT"); XT.append(t_)
    Uf, off = ar.at(off, [128, D], F32, "Uf")
    UB = []
    for i in range(2):
        t_, off = ar.at(off, [128, D], BF16, "Ub"); UB.append(t_)
    A, off = ar.at(off, [128, D], F32, "A")
    SH, off = ar.at(off, [128, D], F32, "SH")
    GP, off = ar.at(off, [128, D], F32, "GP")
    b.dma('sp', GP[:], g_dram.partition_broadcast(128), writes=['GP'])
    cur = [None]

    def stage_a(n):
        t, mr = tiles[n]
        if mr != cur[0]:
            cur[0] = mr
            b.dma('sp', A[:], modrow[mr, i_scale * D:(i_scale + 1) * D].partition_broadcast(128), reads=['modrow'], writes=['A'])
            b.dma('sp', SH[:], modrow[mr, i_shift * D:(i_shift + 1) * D].partition_broadcast(128), reads=['modrow'], writes=['SH'])
            b.op('dve', lambda e: e.scalar_tensor_tensor(out=A[:], in0=A[:], scalar=1.0, in1=GP[:], op0=ALU.add, op1=ALU.mult),
                 reads=['A', 'GP'], writes=['A'])
        xt = XT[n % 2]; xr = 'XT%d' % (n % 2)
        ub = UB[n % 2]; ur = 'UB%d' % (n % 2)
        b.dma('sp', xt[:], x_dram[t * 128:(t + 1) * 128, :], reads=[x_res], writes=[xr])
        rs, rr = emit_rstd(b, c, xt[:], xr, Uf[:], 'Uf', D)
        b.op('dve', lambda e: e.scalar_tensor_tensor(out=Uf[:], in0=xt[:], scalar=rs, in1=A[:], op0=ALU.mult, op1=ALU.mult),
             reads=[xr, rr, 'A'], writes=['Uf'])
        b.op('dve', lambda e: e.tensor_tensor(out=ub[:], in0=Uf[:], in1=SH[:], op=ALU.add), reads=['Uf', 'SH'], writes=[ur])

    def stage_b(n):
        t, mr = tiles[n]
        ub = UB[n % 2]; ur = 'UB%d' % (n % 2)
        if uT is not None:
            emit_transpose_tile(b, c, ub, ur, uT, 'uT', t, KC, ps, n)
        elif callable(u_out):
            dap, dres = u_out(t)
            b.dma('sp', dap, ub[:], reads=[ur], writes=[dres], waw=False)
        else:
            b.dma('sp', u_out[t * 128:(t + 1) * 128, :], ub[:], reads=[ur], writes=[u_res], waw=False)

    stage_a(0)
    for n in range(len(tiles)):
        if n + 1 < len(tiles) and tiles[n + 1][1] == tiles[n][1]:
            stage_a(n + 1)
            stage_b(n)
        else:
            stage_b(n)
            if n + 1 < len(tiles):
                stage_a(n + 1)


def emit_transpose_tile(b, c, src, src_res, dstT, dst_res, t, nch, ps, n):
    for h in range(0, nch, 8):
        k = (n * ((nch + 7) // 8) + h // 8) % len(ps)
        p = ps[k]; pr = 'pb%d' % (6 + k)
        m = min(8, nch - h)
        for cc in range(m):
            ch = h + cc
            b.op('pe', lambda e: e.transpose(p[:, cc, :], src[:, ch * 128:(ch + 1) * 128], c.idb[:]),
                 reads=[src_res, 'idb'], writes=[pr])
        eng = 'act' if (k % 2 == 0) else 'dve'
        if eng == 'act':
            b.op('act', lambda e: e.copy(out=dstT[:, h:h + m, t * 128:(t + 1) * 128], in_=p[:, 0:m, :]), reads=[pr], writes=[dst_res])
        else:
            b.op('dve', lambda e: e.tensor_copy(out=dstT[:, h:h + m, t * 128:(t + 1) * 128], in_=p[:, 0:m, :]), reads=[pr], writes=[dst_res])


def emit_linear_T(b, ar, nc, base, srcT, src_res, nk, T, w_dram, w_res, ncols, outT, out_res, pbanks, colblk=256, nbuf=2):
    off = base
    WB = []
    for i in range(nbuf):
        t_, off = ar.at(off, [128, nk, colblk], BF16, "WB"); WB.append(t_)
    blks = blocks_of(T)
    nb = len(blks)
    wv = w_dram.rearrange("(c p) n -> p c n", p=128)
    it = 0
    for cb in range(ncols // colblk):
        wb = WB[cb % nbuf]; wr = 'WB%d' % (cb % nbuf)
        b.dma_split('pool', wb[:], wv[:, :, cb * colblk:(cb + 1) * colblk], max(2, nk // 11), reads=[w_res], writes=[wr])
        for dd in range(colblk // 128):
            cch = cb * (colblk // 128) + dd
            par = it % 2; it += 1
            for k in range(nk):
                for bi, (s0, sl) in enumerate(blks):
                    pb = pbanks[par * nb + bi]; pr = 'pb%d' % (par * nb + bi)
                    b.op('pe', lambda e: e.matmul(pb[:, 0:sl], lhsT=wb[:, k, dd * 128:(dd + 1) * 128], rhs=srcT[:, k, s0:s0 + sl],
                                                  start=(k == 0), stop=(k == nk - 1)),
                         reads=[wr, src_res], writes=[pr])
            for bi, (s0, sl) in enumerate(blks):
                pb = pbanks[par * nb + bi]; pr = 'pb%d' % (par * nb + bi)
                if bi % 2 == 0:
                    b.op('act', lambda e: e.copy(out=outT[:, cch, s0:s0 + sl], in_=pb[:, 0:sl]), reads=[pr], writes=[out_res])
                else:
                    b.op('dve', lambda e: e.tensor_copy(out=outT[:, cch, s0:s0 + sl], in_=pb[:, 0:sl]), reads=[pr], writes=[out_res])


def emit_ffn_m1(b, ar, nc, base, uT, T, w_in, w_res, actT, pbanks, SG, colblk=256):
    off = base
    WG = []; WU = []
    for i in range(2):
        t_, off = ar.at(off, [128, KC, colblk], BF16, "WG"); WG.append(t_)
        t_, off = ar.at(off, [128, KC, colblk], BF16, "WU"); WU.append(t_)
    blks = blocks_of(T)
    nb = len(blks)
    wv = w_in.rearrange("(c p) n -> p c n", p=128)
    for cb in range(DFF // colblk):
        wg = WG[cb % 2]; wu = WU[cb % 2]
        gr = 'WG%d' % (cb % 2); ur = 'WU%d' % (cb % 2)
        b.dma_split('pool', wg[:], wv[:, :, cb * colblk:(cb + 1) * colblk], 2, reads=[w_res], writes=[gr])
        b.dma_split('pool', wu[:], wv[:, :, DFF + cb * colblk:DFF + (cb + 1) * colblk], 2, reads=[w_res], writes=[ur])
        for dd in range(colblk // 128):
            j = cb * (colblk // 128) + dd
            pbase = (j % 2) * 2 * nb if 4 * nb <= len(pbanks) else 0
            for (wt, wr, pofs) in ((wg, gr, pbase), (wu, ur, pbase + nb)):
                for k in range(KC):
                    for bi, (s0, sl) in enumerate(blks):
                        pb = pbanks[pofs + bi]; pr = 'pb%d' % (pofs + bi)
                        b.op('pe', lambda e: e.matmul(pb[:, 0:sl], lhsT=wt[:, k, dd * 128:(dd + 1) * 128], rhs=uT[:, k, s0:s0 + sl],
                                                      start=(k == 0), stop=(k == KC - 1)),
                             reads=[wr, 'uT'], writes=[pr])
            for bi, (s0, sl) in enumerate(blks):
                pg = pbanks[pbase + bi]; pu = pbanks[pbase + nb + bi]
                b.op('act', lambda e: e.activation(out=SG[:, s0:s0 + sl], in_=pg[:, 0:sl], func=AF.Silu),
                     reads=['pb%d' % (pbase + bi)], writes=['SG%d' % bi])
                b.op('dve', lambda e: e.tensor_tensor(out=actT[:, j, s0:s0 + sl], in0=pu[:, 0:sl], in1=SG[:, s0:s0 + sl], op=ALU.mult),
                     reads=['pb%d' % (pbase + nb + bi), 'SG%d' % bi], writes=['actT'])


def emit_post(b, c, ar, nc, base, outT, outT_res, x_dram, x_res, xo_dram, xo_res, tiles, modrow, g_dram, i_gate, res_scale, pq, premix=None):
    off = base
    XT = []; XO = []
    for i in range(2):
        t_, off = ar.at(off, [128, D], F32, "QX"); XT.append(t_)
        t_, off = ar.at(off, [128, D], F32, "QO"); XO.append(t_)
    G, off = ar.at(off, [128, D], F32, "G")
    GP, off = ar.at(off, [128, D], F32, "GPq")
    junk, off = ar.at(off, [128, D], F32, "junk")
    b.dma('sp', GP[:], g_dram.partition_broadcast(128), writes=['GPq'])
    if premix is not None:
        A2, off = ar.at(off, [128, D], F32, "A2")
        SH2, off = ar.at(off, [128, D], F32, "SH2")
        GP2, off = ar.at(off, [128, D], F32, "GP2")
        Uf2, off = ar.at(off, [128, D], F32, "Uf2")
        UB2 = []
        for i in range(2):
            t_, off = ar.at(off, [128, D], BF16, "UB2"); UB2.append(t_)
        b.dma('sp', GP2[:], premix['g_pre'].partition_broadcast(128), writes=['GP2'])
    cur = None
    for n, (t, mr) in enumerate(tiles):
        if mr != cur:
            cur = mr
            b.dma('sp', G[:], modrow[mr, i_gate * D:(i_gate + 1) * D].partition_broadcast(128), reads=['modrow'], writes=['G'])
            b.op('dve', lambda e: e.scalar_tensor_tensor(out=G[:], in0=G[:], scalar=float(res_scale), in1=GP[:], op0=ALU.mult, op1=ALU.mult),
                 reads=['G', 'GPq'], writes=['G'])
            if premix is not None:
                isc, ish = premix['i_scale'], premix['i_shift']
                b.dma('sp', A2[:], modrow[mr, isc * D:(isc + 1) * D].partition_broadcast(128), reads=['modrow'], writes=['A2'])
                b.dma('sp', SH2[:], modrow[mr, ish * D:(ish + 1) * D].partition_broadcast(128), reads=['modrow'], writes=['SH2'])
                b.op('dve', lambda e: e.scalar_tensor_tensor(out=A2[:], in0=A2[:], scalar=1.0, in1=GP2[:], op0=ALU.add, op1=ALU.mult),
                     reads=['A2', 'GP2'], writes=['A2'])
        p = pq[n % 2]; pr = 'pq%d' % (n % 2)
        xt = XT[n % 2]; xr = 'QX%d' % (n % 2)
        xo = XO[n % 2]; xor_ = 'QO%d' % (n % 2)
        b.dma('sp', xt[:], x_dram[t * 128:(t + 1) * 128, :], reads=[x_res], writes=[xr])
        for ch in range(KC):
            b.op('pe', lambda e: e.transpose(p[:, ch * 128:(ch + 1) * 128], outT[:, ch, t * 128:(t + 1) * 128], c.idf[:]),
                 reads=[outT_res, 'idf'], writes=[pr])
        rs, rr = emit_rstd(b, c, p[:], pr, junk[:], 'junk', D)
        b.op('dve', lambda e: e.scalar_tensor_tensor(out=xo[:], in0=p[:], scalar=rs, in1=G[:], op0=ALU.mult, op1=ALU.mult),
             reads=[pr, rr, 'G'], writes=[xor_])
        b.op('dve', lambda e: e.tensor_tensor(out=xo[:], in0=xo[:], in1=xt[:], op=ALU.add), reads=[xor_, xr], writes=[xor_])
        b.dma('sp', xo_dram[t * 128:(t + 1) * 128, :], xo[:], reads=[xor_], writes=[xo_res], waw=False)
        if premix is not None:
            rs2, rr2 = emit_rstd(b, c, xo[:], xor_, junk[:], 'junk', D)
            ub2 = UB2[n % 2]; ur2 = 'UB2%d' % (n % 2)
            b.op('dve', lambda e: e.scalar_tensor_tensor(out=Uf2[:], in0=xo[:], scalar=rs2, in1=A2[:], op0=ALU.mult, op1=ALU.mult),
                 reads=[xor_, rr2, 'A2'], writes=['Uf2'])
            b.op('dve', lambda e: e.tensor_tensor(out=ub2[:], in0=Uf2[:], in1=SH2[:], op=ALU.add), reads=['Uf2', 'SH2'], writes=[ur2])
            dap, dres = premix['u_out'](t)
            b.dma('sp', dap, ub2[:], reads=[ur2], writes=[dres], waw=False)


STOP = 99

NTOK = 2304
NTIL = 18
PADW = 2312
POS_X = 2
POS_C = 2054


def padpos(tok):
    return POS_X + tok if tok < 2048 else POS_C + (tok - 2048)


def emit_load_uT(b, c, ar, base, u_dram, u_res, ntile, uT, PS):
    off = base
    UB = []
    for i in range(2):
        t_, off = ar.at(off, [128, D], BF16, "UBl"); UB.append(t_)
    for t in range(ntile):
        ub = UB[t % 2]; ur = 'UBl%d' % (t % 2)
        if callable(u_dram):
            sap, sres = u_dram(t)
            b.dma('sp', ub[:], sap, reads=[sres], writes=[ur])
        else:
            b.dma('sp', ub[:], u_dram[t * 128:(t + 1) * 128, :], reads=[u_res], writes=[ur])
        emit_transpose_tile(b, c, ub, ur, uT, 'uT', t, KC, PS.psT, t)
    return off


def emit_mlstm(b, c, ar, nc, PS, u_dram, w_mh, w_g, conv_w, gate_b, head_g, cmat, hmix, hmix_res):
    P0 = SB_BASE
    QK, o1 = ar.at(P0, [128, 8, PADW], BF16, "QK")
    VX, o2 = ar.at(o1, [128, 4, NTIL, 130], BF16, "VX")
    OS, o3 = ar.at(o2, [128, 4, NTIL, 128], BF16, "OS")
    GT, o4 = ar.at(o3, [128, NTIL, 16], F32, "GT")
    SP, o5 = ar.at(o4, [128, NTIL, 16], F32, "SP")
    CW, o6 = ar.at(o5, [128, 8, 5], F32, "CW")
    GB, o7 = ar.at(o6, [128, 16], F32, "GB")
    HG, o8 = ar.at(o7, [128, 512], F32, "HG")
    CM, o9 = ar.at(o8, [128, 5, 128], F32, "CM")
    TRB, o10 = ar.at(o9, [128, 2, 128], BF16, "TRB")
    P1 = (o10 + 31) // 32 * 32
    b.dma('sp', CW[:], conv_w, writes=['CW'])
    b.dma('sp', GB[:], gate_b.partition_broadcast(128), writes=['GB'])
    b.dma('sp', HG[:], head_g.partition_broadcast(128), writes=['HG'])
    b.dma('sp', CM[:], cmat, writes=['CM'])
    b.op('dve', lambda e: e.tensor_copy(out=TRB[:], in_=CM[:, 1:3, :]), reads=['CM'], writes=['TRB'])
    b.op('dve', lambda e: e.memset(QK[:], 0.0), writes=['QK'])
    b.op('dve', lambda e: e.memset(VX[:], 1.0), writes=['VX'])

    if STOP == 10:
        b.barrier(); return
    uT, q1 = ar.at(P1, [128, KC, NTOK], BF16, "uT")
    WM = []
    q = q1
    for i in range(2):
        t_, q = ar.at(q, [128, KC, 512], BF16, "WM"); WM.append(t_)
    WG, q = ar.at(q, [128, KC, 16], BF16, "WG")
    emit_load_uT(b, c, ar, q, u_dram, 'u', NTIL, uT, PS)
    b.dma('pool', WG[:], w_g.rearrange("(c p) n -> p c n", p=128), writes=['WG'])
    if STOP == 11:
        b.barrier(); return
    for t in range(NTIL):
        pb = PS.pb[t % 2]; pr = 'pb%d' % (t % 2)
        for k in range(KC):
            b.op('pe', lambda e: e.matmul(pb[:, 0:16], lhsT=uT[:, k, t * 128:(t + 1) * 128], rhs=WG[:, k, :], start=(k == 0), stop=(k == KC - 1)),
                 reads=['uT', 'WG'], writes=[pr])
        b.op('dve', lambda e: e.tensor_tensor(out=GT[:, t, :], in0=pb[:, 0:16], in1=GB[:], op=ALU.add), reads=[pr, 'GB'], writes=['GT'])
    if STOP == 12:
        b.barrier(); return
    b.op('act', lambda e: e.activation(out=SP[:], in_=GT[:], func=AF.Exp, scale=-1.0), reads=['GT'], writes=['SP'])
    b.op('dve', lambda e: e.tensor_scalar(out=SP[:], in0=SP[:], scalar1=1.0, scalar2=None, op0=ALU.add), reads=['SP'], writes=['SP'])
    b.op('act', lambda e: e.activation(out=SP[:], in_=SP[:], func=AF.Ln), reads=['SP'], writes=['SP'])
    if STOP == 13:
        b.barrier(); return
    blks = blocks_of(NTOK)
    it = 0
    for h in range(4):
        if STOP in (15, 20, 21) and h == 1:
            b.barrier(); return
        if STOP == 17 and h == 2:
            b.barrier(); return
        wm = WM[h % 2]; wr = 'WM%d' % (h % 2)
        b.dma_split('pool', wm[:], w_mh[h].rearrange("(c p) n -> p c n", p=128), 2, reads=['w'], writes=[wr])
        for qk in range(2):
            for (s0, sl) in blks:
                pb = PS.pb[2 + it % 4]; pr = 'pb%d' % (2 + it % 4); it += 1
                for k in range(KC):
                    b.op('pe', lambda e: e.matmul(pb[:, 0:sl], lhsT=wm[:, k, qk * 128:(qk + 1) * 128], rhs=uT[:, k, s0:s0 + sl],
                                                  start=(k == 0), stop=(k == KC - 1)), reads=[wr, 'uT'], writes=[pr])
                segs = []
                if s0 < 2048:
                    e_ = min(s0 + sl, 2048)
                    segs.append((s0, e_ - s0))
                if s0 + sl > 2048:
                    s_ = max(s0, 2048)
                    segs.append((s_, s0 + sl - s_))
                for (a, l) in segs:
                    pp = padpos(a)
                    b.op('act', lambda e: e.copy(out=QK[:, h * 2 + qk, pp:pp + l], in_=pb[:, a - s0:a - s0 + l]), reads=[pr], writes=['QK'])
        if STOP == 14 and h == 0:
            b.barrier(); return
        if STOP == 16 and h == 1:
            b.barrier(); return
        for t in range(NTIL):
            pb = PS.pb[t % 2]; pr = 'pb%d' % (t % 2)
            for k in range(KC):
                b.op('pe', lambda e: e.matmul(pb[:, 0:256], lhsT=uT[:, k, t * 128:(t + 1) * 128], rhs=wm[:, k, 256:512], start=(k == 0), stop=(k == KC - 1)),
                     reads=['uT', wr], writes=[pr])
            if t % 2 == 0:
                b.op('dve', lambda e: e.tensor_copy(out=VX[:, h, t, 0:128], in_=pb[:, 0:128]), reads=[pr], writes=['VX'])
                b.op('dve', lambda e: e.tensor_copy(out=OS[:, h, t, :], in_=pb[:, 128:256]), reads=[pr], writes=['OS'])
            else:
                b.op('act', lambda e: e.copy(out=VX[:, h, t, 0:128], in_=pb[:, 0:128]), reads=[pr], writes=['VX'])
                b.op('act', lambda e: e.copy(out=OS[:, h, t, :], in_=pb[:, 128:256]), reads=[pr], writes=['OS'])
    b.barrier()
    if STOP == 1:
        return

    q = P1
    CQ, q = ar.at(q, [128, 2, PADW], F32, "CQ")
    EBK, q = ar.at(q, [128, 4, NTOK], F32, "EBK")
    QKs, q = ar.at(q, [128, 4, NTOK], BF16, "QKs")
    KT, q = ar.at(q, [128, 2, NTIL, 128], BF16, "KT")
    HA, q = ar.at(q, [128, NTIL, 128], F32, "HA")
    HS, q = ar.at(q, [128, NTIL, 128], F32, "HS")
    HO, q = ar.at(q, [128, NTIL, 128], BF16, "HO")
    LB, q = ar.at(q, [128, 4, 128], F32, "LB")
    CS, q = ar.at(q, [128, 2, 130], F32, "CS")
    CSb, q = ar.at(q, [128, 2, 130], BF16, "CSb")
    SW, q = ar.at(q, [128, 4, 128], BF16, "SW")
    ST, q = ar.at(q, [128, 64], F32, "ST")
    order_f = [16, 17] + list(range(16))
    order_b = [17, 16] + list(range(15, -1, -1))
    for h in range(4):
        for qk in range(2):
            src = QK[:, h * 2 + qk, :]
            dst = CQ[:, qk, 2:PADW - 2]
            L = PADW - 4
            b.op('dve', lambda e: e.tensor_scalar(out=dst, in0=src[:, 0:L], scalar1=CW[:, h * 2 + qk, 0:1], scalar2=None, op0=ALU.mult),
                 reads=['QK', 'CW'], writes=['CQ%d' % qk])
            for kk in range(1, 5):
                b.op('dve', lambda e: e.scalar_tensor_tensor(out=dst, in0=src[:, kk:kk + L], scalar=CW[:, h * 2 + qk, kk:kk + 1], in1=dst,
                                                             op0=ALU.mult, op1=ALU.add), reads=['QK', 'CW', 'CQ%d' % qk], writes=['CQ%d' % qk])
            b.op('act', lambda e: e.activation(out=dst, in_=dst, func=AF.Silu), reads=['CQ%d' % qk], writes=['CQ%d' % qk])
        if STOP == 2:
            b.barrier(); return
        n = 0
        for d in range(2):
            gi = 0 if d == 0 else 2
            for t in range(NTIL):
                lsp = LB[:, (n % 2) * 2, :]; lli = LB[:, (n % 2) * 2 + 1, :]
                lr = 'LB%d' % (n % 2)
                pa = PS.pb[(n % 2) * 2]; pbb = PS.pb[(n % 2) * 2 + 1]
                par = 'pb%d' % ((n % 2) * 2); pbr = 'pb%d' % ((n % 2) * 2 + 1)
                n += 1
                csp = SP[:, t, (gi + 1) * 4 + h:(gi + 1) * 4 + h + 1]
                cli = GT[:, t, gi * 4 + h:gi * 4 + h + 1]
                b.op('dve', lambda e: e.tensor_scalar(out=lsp, in0=CM[:, 3, :], scalar1=csp, scalar2=None, op0=ALU.mult), reads=['CM', 'SP'], writes=[lr])
                b.op('dve', lambda e: e.tensor_scalar(out=lli, in0=CM[:, 3, :], scalar1=cli, scalar2=None, op0=ALU.mult), reads=['CM', 'GT'], writes=[lr])
                tri = CM[:, 1 + d, :]
                b.op('pe', lambda e: e.matmul(pa[:, 0:128], lhsT=lsp, rhs=tri, start=True, stop=True), reads=[lr, 'CM'], writes=[par])
                b.op('pe', lambda e: e.matmul(pbb[:, 0:128], lhsT=lli, rhs=CM[:, 0, :], start=True, stop=False), reads=[lr, 'CM'], writes=[pbr])
                b.op('pe', lambda e: e.matmul(pbb[:, 0:128], lhsT=lsp, rhs=tri, start=False, stop=True), reads=[lr, 'CM'], writes=[pbr])
                b.op('act', lambda e: e.activation(out=EBK[:, d * 2, t * 128:(t + 1) * 128], in_=pa[:, 0:128], func=AF.Exp, scale=-1.0), reads=[par], writes=['EBK'])
                b.op('act', lambda e: e.activation(out=EBK[:, d * 2 + 1, t * 128:(t + 1) * 128], in_=pbb[:, 0:128], func=AF.Exp), reads=[pbr], writes=['EBK'])
        if STOP == 3:
            b.barrier(); return
        for d in range(2):
            for (a, l) in ((0, 2048), (2048, 256)):
                pp = padpos(a)
                b.op('dve', lambda e: e.tensor_tensor(out=QKs[:, d * 2, a:a + l], in0=CQ[:, 0, pp:pp + l], in1=EBK[:, d * 2, a:a + l], op=ALU.mult),
                     reads=['CQ0', 'EBK'], writes=['QKs'])
                b.op('dve', lambda e: e.scalar_tensor_tensor(out=QKs[:, d * 2 + 1, a:a + l], in0=CQ[:, 1, pp:pp + l], scalar=128.0 ** -0.5, in1=EBK[:, d * 2 + 1, a:a + l],
                                                             op0=ALU.mult, op1=ALU.mult), reads=['CQ1', 'EBK'], writes=['QKs'])
        n = 0
        for d in range(2):
            for t0 in range(0, NTIL, 8):
                k_ = n % 2; n += 1
                p = PS.psT[k_]; pr = 'pb%d' % (6 + k_)
                m = min(8, NTIL - t0)
                for cc in range(m):
                    t = t0 + cc
                    b.op('pe', lambda e: e.transpose(p[:, cc, :], QKs[:, d * 2 + 1, t * 128:(t + 1) * 128], c.idb[:]), reads=['QKs', 'idb'], writes=[pr])
                b.op('act', lambda e: e.copy(out=KT[:, d, t0:t0 + m, :], in_=p[:, 0:m, :]), reads=[pr], writes=['KT'])
        if STOP == 4:
            b.barrier(); return
        b.op('dve', lambda e: e.memset(CS[:], 0.0), writes=['CS0', 'CS1'])
        b.op('dve', lambda e: e.memset(CSb[:], 0.0), writes=['CSb0', 'CSb1'])
        for i in range(NTIL):
            for d in range(2):
                t = (order_f if d == 0 else order_b)[i]
                pS = PS.pb[d * 3]; pN = PS.pb[d * 3 + 1]; pU = PS.pb[d * 3 + 2]
                rS = 'pb%d' % (d * 3); rN = 'pb%d' % (d * 3 + 1); rU = 'pb%d' % (d * 3 + 2)
                qT = QKs[:, d * 2, t * 128:(t + 1) * 128]
                kT = QKs[:, d * 2 + 1, t * 128:(t + 1) * 128]
                sw = SW[:, d * 2 + (i % 2), :]; swr = 'SW%d' % (d * 2 + (i % 2))
                b.op('pe', lambda e: e.matmul(pS[:, 0:128], lhsT=kT, rhs=qT, start=True, stop=True), reads=['QKs'], writes=[rS])
                b.op('dve', lambda e: e.tensor_tensor(out=sw, in0=pS[:, 0:128], in1=TRB[:, d, :], op=ALU.mult), reads=[rS, 'TRB'], writes=[swr])
                b.op('pe', lambda e: e.matmul(pN[:, 0:129], lhsT=sw, rhs=VX[:, h, t, 0:129], start=True, stop=(i == 0)), reads=[swr, 'VX'], writes=[rN])
                if i > 0:
                    b.op('pe', lambda e: e.matmul(pN[:, 0:129], lhsT=qT, rhs=CSb[:, d, 0:129], start=False, stop=True), reads=['QKs', 'CSb%d' % d], writes=[rN])
                b.op('pe', lambda e: e.matmul(pU[:, 0:129], lhsT=KT[:, d, t, :], rhs=VX[:, h, t, 0:129], start=True, stop=True), reads=['KT', 'VX'], writes=[rU])
                dn = ST[:, d * 2:d * 2 + 1]; dr = 'STd%d' % d
                b.op('act', lambda e: e.activation(out=dn, in_=pN[:, 128:129], func=AF.Abs), reads=[rN], writes=[dr])
                b.op('dve', lambda e: e.tensor_scalar(out=dn, in0=dn, scalar1=1.0, scalar2=None, op0=ALU.max), reads=[dr], writes=[dr])
                b.op('dve', lambda e: e.reciprocal(out=dn, in_=dn), reads=[dr], writes=[dr])
                if d == 0:
                    b.op('dve', lambda e: e.tensor_scalar(out=HA[:, t, :], in0=pN[:, 0:128], scalar1=dn, scalar2=None, op0=ALU.mult), reads=[rN, dr], writes=['HA'])
                else:
                    b.op('dve', lambda e: e.tensor_scalar(out=HS[:, t, :], in0=pN[:, 0:128], scalar1=dn, scalar2=None, op0=ALU.mult), reads=[rN, dr], writes=['HS'])
                if i < NTIL - 1:
                    col = t * 128 + (127 if d == 0 else 0)
                    dec = EBK[:, d * 2, col:col + 1]
                    b.op('dve', lambda e: e.tensor_tensor(out=CS[:, d, 0:129], in0=pU[:, 0:129], in1=CS[:, d, 0:129], op=ALU.add), reads=[rU, 'CS%d' % d], writes=['CS%d' % d])
                    b.op('dve', lambda e: e.tensor_scalar(out=CS[:, d, :], in0=CS[:, d, :], scalar1=dec, scalar2=None, op0=ALU.mult), reads=['CS%d' % d, 'EBK'], writes=['CS%d' % d])
                    b.op('act', lambda e: e.copy(out=CSb[:, d, :], in_=CS[:, d, :]), reads=['CS%d' % d], writes=['CSb%d' % d])
        if STOP == 5:
            b.barrier(); return
        b.op('dve', lambda e: e.tensor_tensor(out=HA[:], in0=HA[:], in1=HS[:], op=ALU.add), reads=['HA', 'HS'], writes=['HA'])
        b.op('dve', lambda e: e.tensor_tensor(out=HS[:], in0=HA[:], in1=HA[:], op=ALU.mult), reads=['HA'], writes=['HS'])
        ssq = ST[:, 8:8 + NTIL]
        b.op('dve', lambda e: e.tensor_reduce(out=ssq, in_=HS[:], axis=AX.X, op=ALU.add), reads=['HS'], writes=['STs'])
        b.op('dve', lambda e: e.tensor_scalar(out=ssq, in0=ssq, scalar1=1.0 / 128, scalar2=EPS, op0=ALU.mult, op1=ALU.add), reads=['STs'], writes=['STs'])
        b.op('act', lambda e: e.sqrt(out=ssq, in_=ssq), reads=['STs'], writes=['STs'])
        b.op('dve', lambda e: e.reciprocal(out=ssq, in_=ssq), reads=['STs'], writes=['STs'])
        for t in range(NTIL):
            b.op('dve', lambda e: e.scalar_tensor_tensor(out=HS[:, t, :], in0=HA[:, t, :], scalar=ST[:, 8 + t:9 + t], in1=HG[:, h * 128:(h + 1) * 128],
                                                         op0=ALU.mult, op1=ALU.mult), reads=['HA', 'STs', 'HG'], writes=['HS'])
        b.op('act', lambda e: e.activation(out=HA[:], in_=OS[:, h, :, :], func=AF.Exp, scale=-1.0), reads=['OS', 'HA'], writes=['HA'])
        b.op('dve', lambda e: e.tensor_scalar(out=HA[:], in0=HA[:], scalar1=1.0, scalar2=None, op0=ALU.add), reads=['HA'], writes=['HA'])
        b.op('dve', lambda e: e.reciprocal(out=HA[:], in_=HA[:]), reads=['HA'], writes=['HA'])
        b.op('dve', lambda e: e.tensor_tensor(out=HO[:], in0=HS[:], in1=HA[:], op=ALU.mult), reads=['HS', 'HA'], writes=['HO'])
        if callable(hmix):
            hmix(b, h * 128, HO, 'HO')
        else:
            b.dma('sp', hmix.rearrange("(t p) n -> p t n", p=128)[:, :, h * 128:(h + 1) * 128], HO[:], reads=['HO'], writes=[hmix_res], waw=False)
    b.barrier()


def emit_diff(b, c, ar, nc, PS, u_dram, w_dh, w_dv, cos_d, sin_d, lam_p, subln_g, cmat, hmix, hmix_res, lam_init):
    P0 = SB_BASE
    QD, o = ar.at(P0, [128, 4, NTOK], BF16, "QD")
    KD, o = ar.at(o, [128, 4, NTOK], BF16, "KD")
    VD, o = ar.at(o, [128, NTIL, 4, 130], BF16, "VD")
    LP, o = ar.at(o, [128, 4, 64], F32, "LP")
    SGN, o = ar.at(o, [128, 128], F32, "SGN")
    LAM, o = ar.at(o, [128, 8], F32, "LAM")
    ROT, o = ar.at(o, [128, 128], F32, "ROT")
    NBG, o = ar.at(o, [128, 8], F32, "NBG")
    KX, o = ar.at(o, [128, 16], F32, "KX")
    CM, o = ar.at(o, [128, 5, 128], F32, "CMd")
    P1 = o
    b.dma('sp', LP[:], lam_p.partition_broadcast(128), writes=['LP'])
    b.dma('sp', SGN[:], subln_g.partition_broadcast(128), writes=['SGN'])
    b.dma('sp', ROT[:], cmat[:, 4, :], writes=['ROT'])
    b.dma('sp', CM[:], cmat, writes=['CM'])
    b.op('dve', lambda e: e.memset(VD[:], 1.0), writes=['VD'])
    b.op('dve', lambda e: e.tensor_tensor(out=LP[:, 0, :], in0=LP[:, 0, :], in1=LP[:, 1, :], op=ALU.mult), reads=['LP'], writes=['LP'])
    b.op('dve', lambda e: e.tensor_tensor(out=LP[:, 2, :], in0=LP[:, 2, :], in1=LP[:, 3, :], op=ALU.mult), reads=['LP'], writes=['LP'])
    b.op('dve', lambda e: e.tensor_reduce(out=LAM[:, 0:1], in_=LP[:, 0, :], axis=AX.X, op=ALU.add), reads=['LP'], writes=['LAM'])
    b.op('dve', lambda e: e.tensor_reduce(out=LAM[:, 1:2], in_=LP[:, 2, :], axis=AX.X, op=ALU.add), reads=['LP'], writes=['LAM'])
    b.op('act', lambda e: e.activation(out=LAM[:, 0:2], in_=LAM[:, 0:2], func=AF.Exp), reads=['LAM'], writes=['LAM'])
    b.op('dve', lambda e: e.tensor_tensor(out=LAM[:, 2:3], in0=LAM[:, 0:1], in1=LAM[:, 1:2], op=ALU.subtract), reads=['LAM'], writes=['LAM'])
    b.op('dve', lambda e: e.tensor_scalar(out=LAM[:, 2:3], in0=LAM[:, 2:3], scalar1=float(lam_init), scalar2=None, op0=ALU.add), reads=['LAM'], writes=['LAM'])
    b.op('dve', lambda e: e.tensor_scalar(out=LAM[:, 3:4], in0=LAM[:, 2:3], scalar1=-1.0, scalar2=None, op0=ALU.mult), reads=['LAM'], writes=['LAM'])

    uT, q = ar.at(P1, [128, KC, NTOK], BF16, "uT")
    WD = []
    for i in range(2):
        t_, q = ar.at(q, [128, KC, 256], BF16, "WD"); WD.append(t_)
    WDV, q = ar.at(q, [128, KC, 512], BF16, "WDV")
    COS, q = ar.at(q, [128, 2048], F32, "COS")
    SIN, q = ar.at(q, [128, 2048], F32, "SIN")
    XF, q = ar.at(q, [128, 2, 512], F32, "XF")
    T1, q = ar.at(q, [128, 2, 512], F32, "T1")
    q_sq = q
    SQ, q = ar.at(q, [128, 2, NTOK], F32, "SQ")
    b.dma('sp', COS[:], cos_d, writes=['COS'])
    b.dma('sp', SIN[:], sin_d, writes=['SIN'])
    emit_load_uT(b, c, ar, q_sq, u_dram, 'u', NTIL, uT, PS)
    blks = blocks_of(NTOK)
    it = 0
    for h in range(4):
        wd = WD[h % 2]; wr = 'WD%d' % (h % 2)
        b.dma_split('pool', wd[:], w_dh[h].rearrange("(c p) n -> p c n", p=128), 2, reads=['w'], writes=[wr])
        for qk in range(2):
            dst = QD if qk == 0 else KD
            dres = 'QD' if qk == 0 else 'KD'
            for (s0, sl) in blks:
                par = it % 2; it += 1
                pb = PS.pb[par * 2]; pr = 'pb%d' % (par * 2)
                p2 = PS.pb[par * 2 + 1]; p2r = 'pb%d' % (par * 2 + 1)
                for k in range(KC):
                    b.op('pe', lambda e: e.matmul(pb[:, 0:sl], lhsT=wd[:, k, qk * 128:(qk + 1) * 128], rhs=uT[:, k, s0:s0 + sl],
                                                  start=(k == 0), stop=(k == KC - 1)), reads=[wr, 'uT'], writes=[pr])
                if s0 >= 2048:
                    b.op('act', lambda e: e.copy(out=dst[:, h, s0:s0 + sl], in_=pb[:, 0:sl]), reads=[pr], writes=[dres])
                    continue
                xf = XF[:, par, 0:sl]; xr = 'XF%d' % par
                t1 = T1[:, par, 0:sl]; tr = 'T1%d' % par
                b.op('act', lambda e: e.copy(out=xf, in_=pb[:, 0:sl]), reads=[pr], writes=[xr])
                b.op('pe', lambda e: e.matmul(p2[:, 0:sl], lhsT=ROT[:], rhs=xf, start=True, stop=True), reads=['ROT', xr], writes=[p2r])
                b.op('dve', lambda e: e.tensor_tensor(out=t1, in0=p2[:, 0:sl], in1=SIN[:, s0:s0 + sl], op=ALU.mult), reads=[p2r, 'SIN'], writes=[tr])
                b.op('dve', lambda e: e.tensor_tensor(out=xf, in0=xf, in1=COS[:, s0:s0 + sl], op=ALU.mult), reads=[xr, 'COS'], writes=[xr])
                b.op('dve', lambda e: e.tensor_tensor(out=dst[:, h, s0:s0 + sl], in0=xf, in1=t1, op=ALU.add), reads=[xr, tr], writes=[dres])
        b.op('dve', lambda e: e.tensor_tensor(out=SQ[:, 0, :], in0=QD[:, h, :], in1=QD[:, h, :], op=ALU.mult), reads=['QD'], writes=['SQ0'])
        b.op('dve', lambda e: e.tensor_tensor(out=SQ[:, 1, :], in0=KD[:, h, :], in1=KD[:, h, :], op=ALU.mult), reads=['KD'], writes=['SQ1'])
        for g in range(2):
            gs = slice(g * 64, (g + 1) * 64)
            pk = PS.pb[6]; pkr = 'pb6'
            for which in range(2):
                for bi, (s0, sl) in enumerate(blks):
                    b.op('pe', lambda e: e.matmul(pk[0:1, 0:sl], lhsT=CM[gs, 3, 0:1], rhs=SQ[gs, which, s0:s0 + sl], start=True, stop=True), reads=['CM', 'SQ%d' % which], writes=[pkr])
                    b.op('dve', lambda e: e.tensor_reduce(out=KX[0:1, bi:bi + 1], in_=pk[0:1, 0:sl], axis=AX.X, op=ALU.max), reads=[pkr], writes=['KX'])
                b.op('dve', lambda e: e.tensor_reduce(out=KX[0:1, 8 + which:9 + which], in_=KX[0:1, 0:len(blks)], axis=AX.X, op=ALU.max), reads=['KX'], writes=['KX'])
            b.op('dve', lambda e: e.tensor_tensor(out=KX[0:1, 10:11], in0=KX[0:1, 8:9], in1=KX[0:1, 9:10], op=ALU.mult), reads=['KX'], writes=['KX'])
            pq_ = PS.pb[7]; pqr = 'pb7'
            b.op('pe', lambda e: e.matmul(pq_[:, 0:1], lhsT=CM[0:1, 3, :], rhs=KX[0:1, 10:11], start=True, stop=True), reads=['CM', 'KX'], writes=[pqr])
            nbg = NBG[:, h * 2 + g:h * 2 + g + 1]
            b.op('act', lambda e: e.sqrt(out=nbg, in_=pq_[:, 0:1]), reads=[pqr], writes=['NBG'])
            b.op('dve', lambda e: e.tensor_scalar(out=nbg, in0=nbg, scalar1=-0.125, scalar2=None, op0=ALU.mult), reads=['NBG'], writes=['NBG'])
    b.dma_split('pool', WDV[:], w_dv.rearrange("(c p) n -> p c n", p=128), 2, reads=['w'], writes=['WDV'])
    for t in range(NTIL):
        pb = PS.pb[4 + t % 2]; pr = 'pb%d' % (4 + t % 2)
        for k in range(KC):
            b.op('pe', lambda e: e.matmul(pb[:, 0:512], lhsT=uT[:, k, t * 128:(t + 1) * 128], rhs=WDV[:, k, :], start=(k == 0), stop=(k == KC - 1)),
                 reads=['uT', 'WDV'], writes=[pr])
        src = pb[:, 0:512].rearrange("p (h e) -> p h e", h=4)
        if t % 2 == 0:
            b.op('dve', lambda e: e.tensor_copy(out=VD[:, t, :, 0:128], in_=src), reads=[pr], writes=['VD'])
        else:
            b.op('act', lambda e: e.copy(out=VD[:, t, :, 0:128], in_=src), reads=[pr], writes=['VD'])
    b.barrier()

    if STOP == 40:
        return
    q = P1
    PT = []
    for i in range(4):
        t_, q = ar.at(q, [128, 512], BF16, "PT"); PT.append(t_)
    OA, q = ar.at(q, [128, 2, 8, 129], F32, "OA")
    OT, q = ar.at(q, [128, 2, 4, 128], F32, "OT")
    OQ, q = ar.at(q, [128, 2, 4, 128], F32, "OQ")
    HOD, q = ar.at(q, [128, 2, NTIL, 128], BF16, "HOD")
    ST, q = ar.at(q, [128, 2, 16], F32, "STD")
    nblk = 0
    nit = 0
    for h in range(4):
        hr = 'HOD%d' % (h % 2)
        for qb in range(5):
            if qb < 4:
                q0, nq, qts, kts = qb * 512, 512, [qb * 4 + j for j in range(4)], list(range(NTIL))
            else:
                q0, nq, qts, kts = 2048, 256, [16, 17], [16, 17]
            bp = nblk % 2; nblk += 1
            items = [(kt, g) for kt in kts for g in range(2)]
            started = set()

            def acc(j, g):
                a = j * 2 + g
                return PS.pb[4 + a // 3][:, (a % 3) * 129:(a % 3) * 129 + 129], 'pb%d' % (4 + a // 3)

            def emit_s(ii, n_):
                kt, g = items[ii]
                gs = slice(g * 64, (g + 1) * 64)
                r = n_ % 4
                b.op('pe', lambda e: e.matmul(PS.pb[r][:, 0:nq], lhsT=KD[gs, h, kt * 128:(kt + 1) * 128], rhs=QD[gs, h, q0:q0 + nq], start=True, stop=True),
                     reads=['QD', 'KD'], writes=['pb%d' % r])

            def emit_e(ii, n_):
                kt, g = items[ii]
                r = n_ % 4
                b.op('act', lambda e: e.activation(out=PT[r][:, 0:nq], in_=PS.pb[r][:, 0:nq], func=AF.Exp, bias=NBG[:, h * 2 + g:h * 2 + g + 1], scale=0.125),
                     reads=['pb%d' % r, 'NBG'], writes=['PT%d' % r])

            def emit_pv(ii, n_):
                kt, g = items[ii]
                r = n_ % 4
                for j in range(len(qts)):
                    ap_, res_ = acc(j, g)
                    st_ = res_ not in started
                    started.add(res_)
                    b.op('pe', lambda e: e.matmul(ap_, lhsT=PT[r][:, j * 128:(j + 1) * 128], rhs=VD[:, kt, h, 0:129], start=st_, stop=(kt == kts[-1])),
                         reads=['PT%d' % r, 'VD'], writes=[res_])

            emit_s(0, nit)
            for ii in range(len(items)):
                if ii + 1 < len(items):
                    emit_s(ii + 1, nit + ii + 1)
                emit_e(ii, nit + ii)
                emit_pv(ii, nit + ii)
            nit += len(items)
            nacc = 2 * len(qts)
            oa = OA[:, bp]; oar = 'OA%d' % bp
            for bk in range((nacc + 2) // 3):
                na = min(3, nacc - bk * 3)
                src = PS.pb[4 + bk][:, 0:na * 129]
                dst = oa[:, bk * 3:bk * 3 + na, :].rearrange("p a c -> p (a c)")
                if bk % 2 == 0:
                    b.op('act', lambda e: e.copy(out=dst, in_=src), reads=['pb%d' % (4 + bk)], writes=[oar], )
                else:
                    b.op('dve', lambda e: e.tensor_copy(out=dst, in_=src), reads=['pb%d' % (4 + bk)], writes=[oar])
            nj = len(qts)
            st = ST[:, bp]; sr = 'STD%d' % bp
            b.op('dve', lambda e: e.reciprocal(out=st[:, 0:nacc], in_=oa[:, 0:nacc, 128]), reads=[oar], writes=[sr])
            b.op('dve', lambda e: e.tensor_scalar(out=st[:, 8:8 + nj], in0=st[:, 0:nacc].rearrange("p (j g) -> p j g", g=2)[:, :, 1], scalar1=LAM[:, 3:4], scalar2=None, op0=ALU.mult),
                 reads=[sr, 'LAM'], writes=[sr])
            ot = OT[:, bp]; otr = 'OT%d' % bp
            oq = OQ[:, bp]; oqr = 'OQ%d' % bp
            for j in range(nj):
                b.op('dve', lambda e: e.tensor_scalar(out=ot[:, j, :], in0=oa[:, 2 * j + 1, 0:128], scalar1=st[:, 8 + j:9 + j], scalar2=None, op0=ALU.mult), reads=[oar, sr], writes=[otr])
                b.op('dve', lambda e: e.scalar_tensor_tensor(out=ot[:, j, :], in0=oa[:, 2 * j, 0:128], scalar=st[:, 2 * j:2 * j + 1], in1=ot[:, j, :], op0=ALU.mult, op1=ALU.add),
                     reads=[oar, sr, otr], writes=[otr])
            b.op('dve', lambda e: e.tensor_tensor(out=oq[:, 0:nj, :], in0=ot[:, 0:nj, :], in1=ot[:, 0:nj, :], op=ALU.mult), reads=[otr], writes=[oqr])
            ss = st[:, 12:12 + nj]
            b.op('dve', lambda e: e.tensor_reduce(out=ss, in_=oq[:, 0:nj, :], axis=AX.X, op=ALU.add), reads=[oqr], writes=[sr + 'n'])
            b.op('dve', lambda e: e.tensor_scalar(out=ss, in0=ss, scalar1=1.0 / 128, scalar2=EPS, op0=ALU.mult, op1=ALU.add), reads=[sr + 'n'], writes=[sr + 'n'])
            b.op('act', lambda e: e.sqrt(out=ss, in_=ss), reads=[sr + 'n'], writes=[sr + 'n'])
            b.op('dve', lambda e: e.reciprocal(out=ss, in_=ss), reads=[sr + 'n'], writes=[sr + 'n'])
            b.op('dve', lambda e: e.tensor_scalar(out=ss, in0=ss, scalar1=float(1.0 - lam_init), scalar2=None, op0=ALU.mult), reads=[sr + 'n'], writes=[sr + 'n'])
            for j in range(nj):
                b.op('dve', lambda e: e.scalar_tensor_tensor(out=HOD[:, h % 2, qts[j], :], in0=ot[:, j, :], scalar=st[:, 12 + j:13 + j], in1=SGN[:], op0=ALU.mult, op1=ALU.mult),
                     reads=[otr, sr + 'n', 'SGN'], writes=[hr])
        if callable(hmix):
            hmix(b, 512 + h * 128, HOD[:, h % 2], hr)
        else:
            b.dma('sp', hmix.rearrange("(t p) n -> p t n", p=128)[:, :, 512 + h * 128:512 + (h + 1) * 128], HOD[:, h % 2], reads=[hr], writes=[hmix_res], waw=False)
    b.barrier()


def emit_na(b, c, ar, nc, PS, u_dram, w_nh, w_nv, tt_d, cmat, hmix, hmix_res):
    P0 = SB_BASE
    uT, o = ar.at(P0, [128, KC, NTOK], BF16, "uT")
    QN, o = ar.at(o, [128, 4, 2048], BF16, "QN")
    KN, o = ar.at(o, [128, 4, NTOK], BF16, "KN")
    VN, o = ar.at(o, [128, NTIL, 512], BF16, "VN")
    VS, o = ar.at(o, [128, 15, 512], BF16, "VS")
    TT, o = ar.at(o, [64, 4, 16, 64], F32, "TT")
    WN = []
    for i in range(2):
        t_, o = ar.at(o, [128, KC, 256], BF16, "WN"); WN.append(t_)
    WV, o = ar.at(o, [128, KC, 512], BF16, "WV")
    NBN, o = ar.at(o, [128, 8], F32, "NBN")
    KX, o = ar.at(o, [128, 16], F32, "KXn")
    ONE, o = ar.at(o, [128, 128], F32, "ONE")
    SQ, o = ar.at(o, [128, 512], F32, "SQn")
    SWB, o = ar.at(o, [64, 2, 512], F32, "SWB")
    PW, o = ar.at(o, [64, 2, 768], BF16, "PW")
    PT, o = ar.at(o, [128, 2, 6, 64], BF16, "PT")
    o_hout = o
    HOUT, o = ar.at(o, [64, 1, 32, 128], BF16, "HOUT")
    ST, o = ar.at(o, [64, 32], F32, "STN")
    b.dma('sp', ONE[:], cmat[:, 3, :], writes=['ONE'])
    emit_load_uT(b, c, ar, o_hout, u_dram, 'u', NTIL, uT, PS)
    blks = blocks_of(NTOK)
    it = 0
    n = 0
    for grp in range(2):
        b.dma('sp', TT[:], tt_d[grp * 4:(grp + 1) * 4].rearrange("h w d c -> w h d c"), reads=['w'], writes=['TT'])
        for h in range(4):
            H = grp * 4 + h
            wn = WN[H % 2]; wr = 'WN%d' % (H % 2)
            b.dma_split('pool', wn[:], w_nh[H].rearrange("(c p) n -> p c n", p=128), 2, reads=['w'], writes=[wr])
            for qk in range(2):
                for (s0, sl) in blks:
                    if qk == 0 and s0 >= 2048:
                        continue
                    pb = PS.pb[it % 4]; pr = 'pb%d' % (it % 4); it += 1
                    for k in range(KC):
                        b.op('pe', lambda e: e.matmul(pb[:, 0:sl], lhsT=wn[:, k, qk * 128:(qk + 1) * 128], rhs=uT[:, k, s0:s0 + sl],
                                                      start=(k == 0), stop=(k == KC - 1)), reads=[wr, 'uT'], writes=[pr])
                    if qk == 0:
                        b.op('act', lambda e: e.mul(out=QN[:, h, s0:s0 + sl], in_=pb[:, 0:sl], mul=128.0 ** -0.5), reads=[pr], writes=['QN'])
                    else:
                        b.op('dve', lambda e: e.tensor_copy(out=KN[:, h, s0:s0 + sl], in_=pb[:, 0:sl]), reads=[pr], writes=['KN'])
            pk = PS.pb[6]; pkr = 'pb6'
            for which, (src, sres, ntok) in enumerate(((QN, 'QN', 2048), (KN, 'KN', NTOK))):
                nbk = 0
                for s0 in range(0, ntok, 512):
                    sl = min(512, ntok - s0)
                    b.op('dve', lambda e: e.tensor_tensor(out=SQ[:, 0:sl], in0=src[:, h, s0:s0 + sl], in1=src[:, h, s0:s0 + sl], op=ALU.mult), reads=[sres], writes=['SQn'])
                    b.op('pe', lambda e: e.matmul(pk[0:1, 0:sl], lhsT=ONE[:, 0:1], rhs=SQ[:, 0:sl], start=True, stop=True), reads=['ONE', 'SQn'], writes=[pkr])
                    b.op('dve', lambda e: e.tensor_reduce(out=KX[0:1, nbk:nbk + 1], in_=pk[0:1, 0:sl], axis=AX.X, op=ALU.max), reads=[pkr], writes=['KXn'])
                    nbk += 1
                b.op('dve', lambda e: e.tensor_reduce(out=KX[0:1, 8 + which:9 + which], in_=KX[0:1, 0:nbk], axis=AX.X, op=ALU.max), reads=['KXn'], writes=['KXn'])
            b.op('dve', lambda e: e.tensor_tensor(out=KX[0:1, 10:11], in0=KX[0:1, 8:9], in1=KX[0:1, 9:10], op=ALU.mult), reads=['KXn'], writes=['KXn'])
            pq_ = PS.pb[7]; pqr = 'pb7'
            b.op('pe', lambda e: e.matmul(pq_[:, 0:1], lhsT=ONE[0:1, :], rhs=KX[0:1, 10:11], start=True, stop=True), reads=['ONE', 'KXn'], writes=[pqr])
            nbn = NBN[0:64, H:H + 1]
            b.op('act', lambda e: e.sqrt(out=nbn, in_=pq_[0:64, 0:1]), reads=[pqr], writes=['NBN'])
            b.op('dve', lambda e: e.tensor_reduce(out=KX[0:64, 12:13], in_=TT[:, h, 0:15, :].rearrange("p a c -> p (a c)"), axis=AX.X, op=ALU.max), reads=['TT'], writes=['KXn'])
            b.op('dve', lambda e: e.tensor_tensor(out=nbn, in0=nbn, in1=KX[0:64, 12:13], op=ALU.add), reads=['NBN', 'KXn'], writes=['NBN'])
            b.op('dve', lambda e: e.tensor_scalar(out=nbn, in0=nbn, scalar1=-1.0, scalar2=None, op0=ALU.mult), reads=['NBN'], writes=['NBN'])
        b.dma_split('pool', WV[:], w_nv[grp].rearrange("(c p) n -> p c n", p=128), 2, reads=['w'], writes=['WV'])
        for t in range(NTIL + 15):
            pb = PS.pb[4 + t % 2]; pr = 'pb%d' % (4 + t % 2)
            if t < NTIL:
                tok0 = t * 128; dst = VN[:, t, :]; dr = 'VN'
            else:
                tok0 = 64 + (t - NTIL) * 128; dst = VS[:, t - NTIL, :]; dr = 'VS'
            for k in range(KC):
                b.op('pe', lambda e: e.matmul(pb[:, 0:512], lhsT=uT[:, k, tok0:tok0 + 128], rhs=WV[:, k, :], start=(k == 0), stop=(k == KC - 1)),
                     reads=['uT', 'WV'], writes=[pr])
            if t % 2 == 0:
                b.op('dve', lambda e: e.tensor_copy(out=dst, in_=pb[:, 0:512]), reads=[pr], writes=[dr])
            else:
                b.op('act', lambda e: e.copy(out=dst, in_=pb[:, 0:512]), reads=[pr], writes=[dr])
        if STOP == 30:
            b.barrier(); return
        units = [(h, r) for h in range(4) for r in range(32)]

        def se(i):
            h, r = units[i]
            par = i % 2
            rs = min(max(r - 4, 0), 24)
            dr0 = rs - r + 7
            pA = PS.pb[par]; rA = 'pb%d' % par
            pB = PS.pb[2 + par]; rB = 'pb%d' % (2 + par)
            qT = QN[:, h, r * 64:(r + 1) * 64]
            b.op('pe', lambda e: e.matmul(pA[0:64, 0:512], lhsT=qT, rhs=KN[:, h, rs * 64:rs * 64 + 512], start=True, stop=True), reads=['QN', 'KN'], writes=[rA])
            b.op('pe', lambda e: e.matmul(pB[0:64, 0:256], lhsT=qT, rhs=KN[:, h, 2048:2304], start=True, stop=True), reads=['QN', 'KN'], writes=[rB])
            sw = SWB[:, par, :]; swr = 'SWB%d' % par
            b.op('dve', lambda e: e.tensor_tensor(out=sw.rearrange("p (a c) -> p a c", c=64), in0=pA[0:64, 0:512].rearrange("p (a c) -> p a c", c=64),
                                                  in1=TT[:, h, dr0:dr0 + 8, :], op=ALU.add), reads=[rA, 'TT'], writes=[swr])
            s1 = ST[:, par * 8 + 3:par * 8 + 4]; s2 = ST[:, par * 8 + 4:par * 8 + 5]
            sr = 'STN%d' % par
            nb_ = NBN[0:64, grp * 4 + h:grp * 4 + h + 1]
            pw = PW[:, par, :]; pwr = 'PW%d' % par
            b.op('act', lambda e: e.activation(out=pw[:, 0:512], in_=sw, func=AF.Exp, bias=nb_, scale=1.0, accum_out=s1), reads=[swr, 'NBN'], writes=[pwr, sr + 'a'])
            b.op('act', lambda e: e.activation(out=pw[:, 512:768], in_=pB[0:64, 0:256], func=AF.Exp, bias=nb_, scale=1.0, accum_out=s2), reads=[rB, 'NBN'], writes=[pwr, sr + 'b'])
            b.op('dve', lambda e: e.tensor_tensor(out=s2, in0=s1, in1=s2, op=ALU.add), reads=[sr + 'a', sr + 'b'], writes=[sr + 'b'])
            b.op('dve', lambda e: e.reciprocal(out=s2, in_=s2), reads=[sr + 'b'], writes=[sr + 'b'])

        def tp(i):
            h, r = units[i]
            H = grp * 4 + h
            par = i % 2
            rs = min(max(r - 4, 0), 24)
            pO = PS.pb[4 + par]; rO = 'pb%d' % (4 + par)
            p = PS.psT[par]; rT = 'pb%d' % (6 + par)
            pw = PW[:, par, :]; pwr = 'PW%d' % par
            s2 = ST[:, par * 8 + 4:par * 8 + 5]
            sr = 'STN%d' % par
            for cc in range(6):
                b.op('pe', lambda e: e.transpose(p[:, cc, 0:64], pw[:, cc * 128:(cc + 1) * 128], c.idb[0:64, 0:64]), reads=[pwr, 'idb'], writes=[rT])
            pt = PT[:, par]; ptr_ = 'PT%d' % par
            b.op('act', lambda e: e.copy(out=pt, in_=p[:, 0:6, 0:64]), reads=[rT], writes=[ptr_])
            for cc in range(6):
                if cc < 4:
                    if rs % 2 == 0:
                        rhs = VN[:, rs // 2 + cc, h * 128:(h + 1) * 128]; vr = 'VN'
                    else:
                        rhs = VS[:, (rs - 1) // 2 + cc, h * 128:(h + 1) * 128]; vr = 'VS'
                else:
                    rhs = VN[:, 16 + cc - 4, h * 128:(h + 1) * 128]; vr = 'VN'
                b.op('pe', lambda e: e.matmul(pO[0:64, 0:128], lhsT=pt[:, cc, :], rhs=rhs, start=(cc == 0), stop=(cc == 5)), reads=[ptr_, vr], writes=[rO])
            b.op('dve', lambda e: e.tensor_scalar(out=HOUT[:, 0, r, :], in0=pO[0:64, 0:128], scalar1=s2, scalar2=None, op0=ALU.mult), reads=[rO, sr + 'b'], writes=['HOUT0'])
            if r == 31:
                if callable(hmix):
                    hmix(b, H * 128, HOUT[:, 0], 'HOUT0')
                else:
                    b.dma('sp', hmix.rearrange("(r w) n -> w r n", w=64)[:, :, H * 128:(H + 1) * 128], HOUT[:, 0], reads=['HOUT0'], writes=[hmix_res], waw=False)

        se(0)
        for i in range(len(units)):
            if i + 1 < len(units):
                se(i + 1)
            tp(i)
    b.barrier()


class PSum:
    def __init__(self, nc, es):
        big = [es.enter_context(nc.psum_tensor("pq%d" % i, [128, 2048], F32)) for i in range(2)]
        self.pq = big
        self.pb = [big[i // 4][:, (i % 4) * 512:(i % 4 + 1) * 512] for i in range(8)]
        self.psT = [self.pb[6 + i].bitcast(BF16).rearrange("p (c n) -> p c n", n=128) for i in range(2)]


def emit_mod(b, c, ar, nc, PS, cvec, w_ada_l, b_ada_l, modrow, mres):
    o = SB_BASE
    CT, o = ar.at(o, [128, KC, 2], F32, "CT")
    WA = []
    for i in range(2):
        t_, o = ar.at(o, [128, KC, 512], F32, "WA"); WA.append(t_)
    BA, o = ar.at(o, [2, 4608], F32, "BA")
    MR, o = ar.at(o, [2, 4608], F32, "MR")
    for r in range(2):
        b.dma('sp', CT[:, :, r:r + 1], cvec[r].rearrange("(c p o) -> p c o", p=128, o=1), writes=['CT'], waw=False, allow_slow_non_contiguous=True)
    b.op('act', lambda e: e.activation(out=CT[:], in_=CT[:], func=AF.Silu), reads=['CT'], writes=['CT'])
    wv = w_ada_l.rearrange("(c p) n -> p c n", p=128)
    for qq in range(4):
        b.dma('sp', BA[:], b_ada_l[qq * 4608:(qq + 1) * 4608].partition_broadcast(2), writes=['BA'])
        for j in range(9):
            nb = qq * 9 + j
            wa = WA[nb % 2]; wr = 'WA%d' % (nb % 2)
            b.dma('sp' if nb % 2 == 0 else 'pool', wa[:], wv[:, :, nb * 512:(nb + 1) * 512], reads=['w'], writes=[wr])
            pb = PS.pb[nb % 2]; pr = 'pb%d' % (nb % 2)
            for k in range(KC):
                b.op('pe', lambda e: e.matmul(pb[0:2, 0:512], lhsT=CT[:, k, :], rhs=wa[:, k, :], start=(k == 0), stop=(k == KC - 1)), reads=['CT', wr], writes=[pr])
            b.op('dve', lambda e: e.tensor_tensor(out=MR[:, j * 512:(j + 1) * 512], in0=pb[0:2, 0:512], in1=BA[:, j * 512:(j + 1) * 512], op=ALU.add),
                 reads=[pr, 'BA'], writes=['MR'])
        b.dma('sp', modrow[:, qq * 4608:(qq + 1) * 4608], MR[:], reads=['MR'], writes=[mres], waw=False)
    b.barrier()


def emit_ffn(b, c, ar, nc, PS, x_dram, x_res, xo_dram, xo_res, tiles, T, modrow, g_pre, g_post, imod, w_in, w_out, premix=None):
    actT, R1 = ar.at(SB_BASE, [128, FC, T], BF16, "actT")
    uT, e1 = ar.at(R1, [128, KC, T], BF16, "uT")
    emit_pre(b, c, ar, nc, e1, x_dram, x_res, tiles, modrow, g_pre, imod[0], imod[1], uT=uT, ps=PS.psT)
    b.barrier()
    SG, e2 = ar.at(e1 + 4 * 8192, [128, T], F32, "SG")
    emit_ffn_m1(b, ar, nc, e1, uT, T, w_in, 'w', actT, PS.pb, SG)
    b.barrier()
    outT, e3 = ar.at(R1, [128, KC, T], F32, "outT")
    emit_linear_T(b, ar, nc, e3, actT, 'actT', FC, T, w_out, 'w', D, outT, 'outT', PS.pb, colblk=128, nbuf=3)
    b.barrier()
    emit_post(b, c, ar, nc, SB_BASE, outT, 'outT', x_dram, x_res, xo_dram, xo_res, tiles, modrow, g_post, imod[2], 0.5, PS.pq, premix=premix)
    b.barrier()


def emit_premix(b, c, ar, nc, PS, x_dram, x_res, u_dram, u_res, tiles, modrow, g_pre):
    emit_pre(b, c, ar, nc, SB_BASE, x_dram, x_res, tiles, modrow, g_pre, 3, 4, uT=None, u_out=u_dram, u_res=u_res)
    b.barrier()


def emit_postmix(b, c, ar, nc, PS, h_dram, h_res, w_mix, x_dram, x_res, xo_dram, xo_res, tiles, T, modrow, g_post):
    hT, o = ar.at(SB_BASE, [128, KC, T], BF16, "hT")
    outT, o2 = ar.at(o, [128, KC, T], F32, "outT")
    off = o2
    UB = []
    for i in range(2):
        t_, off = ar.at(off, [128, D], BF16, "HBl"); UB.append(t_)
    off2 = off
    off = off + 3 * 4096 + 64
    def _load(n):
        t, mr = tiles[n]
        ub = UB[n % 2]; ur = 'HBl%d' % (n % 2)
        if callable(h_dram):
            h_dram(b, n, ub, ur, off2)
        else:
            b.dma('sp', ub[:], h_dram[t * 128:(t + 1) * 128, :], reads=[h_res], writes=[ur])

    _load(0)
    for n, (t, mr) in enumerate(tiles):
        if n + 1 < len(tiles):
            _load(n + 1)
        emit_transpose_tile(b, c, UB[n % 2], 'HBl%d' % (n % 2), hT, 'hT', t, KC, PS.psT, n)
    emit_linear_T(b, ar, nc, off, hT, 'hT', KC, T, w_mix, 'w', D, outT, 'outT', PS.pb, colblk=256, nbuf=2)
    b.barrier()
    emit_post(b, c, ar, nc, o2, outT, 'outT', x_dram, x_res, xo_dram, xo_res, tiles, modrow, g_post, 5, 1.0, PS.pq)
    b.barrier()


MODC = 9 * D // 8


def emit_mod_sharded(b, c, ar, nc, PS, cvec_all, w_ada_sl, b_ada_sl, selmod, ml_t, G_t, modrows):
    o = SB_BASE
    CT, o = ar.at(o, [128, KC, 5], F32, "CT")
    WA = []
    for i in range(2):
        t_, o = ar.at(o, [128, KC, 512], BF16, "WA"); WA.append(t_)
    CTb, o = ar.at(o, [128, KC, 5], BF16, "CTb")
    BA, o = ar.at(o, [5, MODC], F32, "BA")
    ML, o = ar.at(o, [5, MODC], F32, "ML")
    GS = []
    for i in range(2):
        t_, o = ar.at(o, [5, MODC], F32, "GS"); GS.append(t_)
    MR = []
    for i in range(2):
        t_, o = ar.at(o, [2, MODC], F32, "MR"); MR.append(t_)
    SM, o = ar.at(o, [5, 2], F32, "SM")
    b.dma('sp', SM[:], selmod, writes=['SM'])
    for r in range(5):
        b.dma('sp', CT[:, :, r:r + 1], cvec_all[r].rearrange("(c p o) -> p c o", p=128, o=1), writes=['CT'], waw=False, allow_slow_non_contiguous=True)
    b.op('act', lambda e: e.activation(out=CT[:], in_=CT[:], func=AF.Silu), reads=['CT'], writes=['CT'])
    b.op('dve', lambda e: e.tensor_copy(out=CTb[:], in_=CT[:]), reads=['CT'], writes=['CTb'])
    cblks = [(0, 512), (512, 512), (1024, 512), (1536, 512), (2048, 256)]
    n = 0
    for l in range(2):
        wv = w_ada_sl[l].rearrange("(c p) n -> p c n", p=128)
        b.dma('sp', BA[:], b_ada_sl[l].partition_broadcast(5), reads=['w'], writes=['BA'])
        for (c0, cl) in cblks:
            wa = WA[n % 2]; wr = 'WA%d' % (n % 2)
            pb = PS.pb[n % 2]; pr = 'pb%d' % (n % 2)
            b.dma_split('pool', wa[:, :, 0:cl], wv[:, :, c0:c0 + cl], 2, reads=['w'], writes=[wr])
            n += 1
            for k in range(KC):
                b.op('pe', lambda e: e.matmul(pb[0:5, 0:cl], lhsT=CTb[:, k, :], rhs=wa[:, k, 0:cl], start=(k == 0), stop=(k == KC - 1)), reads=['CTb', wr], writes=[pr])
            b.op('dve', lambda e: e.tensor_tensor(out=ML[:, c0:c0 + cl], in0=pb[0:5, 0:cl], in1=BA[:, c0:c0 + cl], op=ALU.add), reads=[pr, 'BA'], writes=['ML'])
        b.dma('sp', ml_t[l].ap(), ML[:], reads=['ML'], writes=['ml%d' % l])
        b.allgather(ml_t[l], G_t[l], [list(range(8))], ['ml%d' % l], ['G%d' % l])
    m = 0
    for l in range(2):
        for r in range(8):
            gs = GS[m % 2]; gr = 'GS%d' % (m % 2)
            mr = MR[m % 2]; mrr = 'MR%d' % (m % 2)
            m += 1
            b.dma('sp', gs[:], G_t[l].ap()[r * 5:(r + 1) * 5, :], reads=['G%d' % l], writes=[gr])
            for (c0, cl) in cblks:
                pb = PS.pb[2 + n % 2]; pr = 'pb%d' % (2 + n % 2); n += 1
                b.op('pe', lambda e: e.matmul(pb[0:2, 0:cl], lhsT=SM[:], rhs=gs[:, c0:c0 + cl], start=True, stop=True), reads=['SM', gr], writes=[pr])
                b.op('act', lambda e: e.copy(out=mr[:, c0:c0 + cl], in_=pb[0:2, 0:cl]), reads=[pr], writes=[mrr])
            b.dma('sp', modrows[l][:, r * MODC:(r + 1) * MODC], mr[:], reads=[mrr], writes=['modrow'], waw=False)
    b.barrier()


import numpy as np
import ml_dtypes
BF = ml_dtypes.bfloat16

def cmat():
    s = np.arange(128)[:, None]; t = np.arange(128)[None, :]
    cm = np.zeros((128, 5, 128), np.float32)
    cm[:, 0] = (s == t)
    cm[:, 1] = (s <= t)
    cm[:, 2] = (s >= t)
    cm[:, 3] = 1.0
    for p2 in range(128):
        j = (p2 // 16) % 2
        if j == 0:
            cm[p2 + 16, 4, p2] = -1.0
        else:
            cm[p2 - 16, 4, p2] = 1.0
    return cm

def pack_even_mlstm(z, hh):
    W = z['w_in_even'][0]
    Hs = [4 * hh + h for h in range(4)]
    w_mh = np.stack([np.concatenate([W[:, g * 1024 + H * 128: g * 1024 + (H + 1) * 128] for g in range(4)], 1) for H in Hs], 0)
    w_g = np.stack([W[:, 4096 + g * 8 + H] for g in range(4) for H in Hs], 1)
    conv = z['mlstm_conv'][0]
    cw = np.zeros((128, 8, 5), np.float32)
    for h, H in enumerate(Hs):
        cw[:, h * 2 + 0, :] = conv[:, H * 128:(H + 1) * 128].T
        cw[:, h * 2 + 1, :] = conv[:, 1024 + H * 128:1024 + (H + 1) * 128].T
    gb = np.array([z['mlstm_gate_b'][0][g, H] for g in range(4) for H in Hs], np.float32)
    hg = z['mlstm_head_g'][0][4 * hh * 128:(4 * hh + 4) * 128]
    return dict(w_mh=np.ascontiguousarray(w_mh), w_g=np.ascontiguousarray(w_g), conv_w=cw, gate_b=gb, head_g=np.ascontiguousarray(hg))

def pack_even_diff(z, hh):
    W = z['w_in_even'][0]
    Hs = [4 * hh + h for h in range(4)]
    base = 4128
    w_dh = np.stack([np.concatenate([W[:, base + g * 1024 + H * 128: base + g * 1024 + (H + 1) * 128] for g in range(2)], 1) for H in Hs], 0)
    w_dv = W[:, base + 2048 + Hs[0] * 128: base + 2048 + (Hs[0] + 4) * 128]
    return dict(w_dh=np.ascontiguousarray(w_dh), w_dv=np.ascontiguousarray(w_dv), lam_p=np.ascontiguousarray(z['diff_lambda'][0].reshape(-1)), subln_g=z['diff_subln_g'][0])

def rope_tables():
    t = np.arange(2048)
    row = (t // 64).astype(np.float32); col = (t % 64).astype(np.float32)
    inv = (np.float32(10000.0) ** (-np.arange(16, dtype=np.float32) / 16)).astype(np.float32)
    ang = np.stack([row[:, None] * inv, col[:, None] * inv], 1)
    cos = np.zeros((128, 2048), np.float32); sin = np.zeros((128, 2048), np.float32)
    for p in range(128):
        A = (p // 32) % 2; i = p % 16
        cos[p] = np.cos(ang[:, A, i]); sin[p] = np.sin(ang[:, A, i])
    return cos, sin

NEG = -30000.0
def pack_na(z, hh):
    W = z['w_in_odd'][0]
    Hs = [8 * hh + h for h in range(8)]
    w_nh = np.stack([np.concatenate([W[:, g * 2048 + H * 128: g * 2048 + (H + 1) * 128] for g in range(2)], 1) for H in Hs], 0)
    w_nv = np.stack([W[:, 4096 + Hs[4 * gg] * 128: 4096 + (Hs[4 * gg] + 4) * 128] for gg in range(2)], 0)
    rpb = z['na_rpb'][0]
    w = np.arange(64)
    c_start = np.clip(w - 8, 0, 48)
    kc = np.arange(64)
    inwin = (kc[None, :] >= c_start[:, None]) & (kc[None, :] < c_start[:, None] + 16)
    cidx = np.clip(kc[None, :] - w[:, None] + 15, 0, 30)
    tt = np.full((8, 64, 16, 64), NEG, np.float32)
    for i, H in enumerate(Hs):
        for dr in range(15):
            vals = rpb[H, dr][cidx]
            tt[i, :, dr, :] = np.where(inwin, vals, np.float32(NEG))
    return dict(w_nh=np.ascontiguousarray(w_nh), w_nv=np.ascontiguousarray(w_nv), tt=tt)


TX = 1152
TILES9 = [(t, 0) for t in range(8)] + [(8, 1)]
TILES8 = [(t, 0) for t in range(8)]
PAIRS = [[0, 1], [2, 3], [4, 5], [6, 7]]


def _din(nc, name, shape, dt=F32):
    return nc.dram_tensor(name, list(shape), dt, kind="ExternalInput").ap()


def _dout(nc, name, shape, dt=F32):
    return nc.dram_tensor(name, list(shape), dt, kind="ExternalOutput").ap()


class Exch:
    def __init__(self, nc, name, ntile, tpc, ncol):
        self.name = name; self.ntile = ntile; self.tpc = tpc; self.ncol = ncol
        self.nch = (ntile + tpc - 1) // tpc
        self.src = [nc.dram_tensor("%s_s%d" % (name, k), [tpc * 128, ncol], BF16) for k in range(self.nch)]
        self.dst = [nc.dram_tensor("%s_g%d" % (name, k), [2 * tpc * 128, ncol], BF16) for k in range(self.nch)]

    def sres(self, k):
        return "%s_s%d" % (self.name, k)

    def gres(self, k):
        return "%s_g%d" % (self.name, k)

    def local_tile(self, t):
        k = t // self.tpc
        r0 = (t % self.tpc) * 128
        return self.src[k].ap()[r0:r0 + 128, :], self.sres(k)

    def gathered_tile(self, rank, t):
        k = t // self.tpc
        r0 = rank * self.tpc * 128 + (t % self.tpc) * 128
        return self.dst[k].ap()[r0:r0 + 128, :], self.gres(k)

    def gather(self, b):
        for k in range(self.nch):
            b.allgather(self.src[k], self.dst[k], PAIRS, [self.sres(k)], [self.gres(k)])


def build_fused():
    nc = bass.Bass("TRN2", target_bir_lowering=False)
    x = _din(nc, "x", [TX, D]); cvec = _din(nc, "cvec", [5, D]); sel = _din(nc, "sel", [128, 2]); selmod = _din(nc, "selmod", [5, 2])
    w_ada = _din(nc, "w_ada", [2, D, MODC]); b_ada = _din(nc, "b_ada", [2, MODC]); g = _din(nc, "g", [2, 6, D])
    w_in = _din(nc, "w_in", [2, 2, D, 2 * DFF]); w_out = _din(nc, "w_out", [2, 2, DFF, D]); ident = _din(nc, "ident", [128, 128])
    w_mh = _din(nc, "w_mh", [4, D, 512]); w_g = _din(nc, "w_g", [D, 16]); conv_w = _din(nc, "conv_w", [128, 8, 5])
    gate_b = _din(nc, "gate_b", [16]); head_g = _din(nc, "head_g", [512]); cm = _din(nc, "cmat", [128, 5, 128])
    w_dh = _din(nc, "w_dh", [4, D, 256]); w_dv = _din(nc, "w_dv", [D, 512]); cos_d = _din(nc, "cos", [128, 2048]); sin_d = _din(nc, "sin", [128, 2048])
    lam_p = _din(nc, "lam_p", [256]); subln_g = _din(nc, "subln_g", [128])
    w_mix0 = _din(nc, "w_mix0", [D, D]); w_mix1 = _din(nc, "w_mix1", [D, D])
    w_nh = _din(nc, "w_nh", [8, D, 256]); w_nv = _din(nc, "w_nv", [2, D, 512]); tt = _din(nc, "tt", [8, 64, 16, 64])
    y = _dout(nc, "y", [1024, D])
    modrow0 = nc.dram_tensor("modrow0", [2, 9 * D], F32).ap(); modrow1 = nc.dram_tensor("modrow1", [2, 9 * D], F32).ap()
    ml_t = [nc.dram_tensor("ml%d" % l, [5, MODC], F32) for l in range(2)]
    G_t = [nc.dram_tensor("mg%d" % l, [40, MODC], F32) for l in range(2)]
    xs = [nc.dram_tensor("xs%d" % i, [TX, D], F32).ap() for i in range(6)]
    EU0 = Exch(nc, "eu0", 9, 3, D); EU1 = Exch(nc, "eu1", 9, 3, D)
    EH0 = Exch(nc, "eh0", 18, 6, 1024); EH1 = Exch(nc, "eh1", 16, 4, 1024)

    def full_u(E):
        def f(T):
            if T < 8:
                return E.gathered_tile(0, T)
            if T < 16:
                return E.gathered_tile(1, T - 8)
            return E.gathered_tile(T - 16, 8)
        return f

    def h_writer0(b, col0, HO, hres):
        for k in range(EH0.nch):
            dst = EH0.src[k].ap().rearrange("(t p) n -> p t n", p=128)[:, :, col0:col0 + 128]
            b.dma('sp', dst, HO[:, k * 6:(k + 1) * 6, :], reads=[hres], writes=[EH0.sres(k)], waw=False)

    def h_writer1(b, col0, HOUT, hres):
        for k in range(EH1.nch):
            dst = EH1.src[k].ap().rearrange("(r w) n -> w r n", w=64)[:, :, col0:col0 + 128]
            b.dma('sp', dst, HOUT[:, k * 8:(k + 1) * 8, :], reads=[hres], writes=[EH1.sres(k)], waw=False)

    with ExitStack() as es:
        b = Bld(nc, es); ar = Arena(nc); PS = PSum(nc, es)
        c = setup_consts(b, ar, nc, ident, CONST_OFF)
        SEL, _ = ar.at(c.off, [128, 2], F32, "SEL")
        b.dma('sp', SEL[:], sel, writes=['SEL'])

        def h_loader(E, layer):
            def f(b, n, ub, ur, off2):
                A, o = ar.at(off2, [128, D], BF16, "hA")
                B, o = ar.at(o, [128, D], BF16, "hB")
                for (cand, buf, br) in ((0, A, 'hA'), (1, B, 'hB')):
                    if layer == 0:
                        T = (n if n < 8 else 16) if cand == 0 else (8 + n if n < 8 else 17)
                        for r in range(2):
                            sap, sres = E.gathered_tile(r, T)
                            b.dma('sp', buf[:].rearrange("p (k r c) -> p k r c", k=2, r=2, c=512)[:, :, r, :], sap.rearrange("p (k c) -> p k c", k=2),
                                  reads=[sres], writes=[br], waw=False)
                    else:
                        T = n if cand == 0 else 8 + n
                        for r in range(2):
                            sap, sres = E.gathered_tile(r, T)
                            b.dma('sp', buf[:, r * 1024:(r + 1) * 1024], sap, reads=[sres], writes=[br], waw=False)
                b.op('dve', lambda e: e.tensor_scalar(out=A[:], in0=A[:], scalar1=SEL[:, 0:1], scalar2=None, op0=ALU.mult), reads=['hA', 'SEL'], writes=['hA'])
                b.op('dve', lambda e: e.scalar_tensor_tensor(out=ub[:], in0=B[:], scalar=SEL[:, 1:2], in1=A[:], op0=ALU.mult, op1=ALU.add),
                     reads=['hB', 'hA', 'SEL'], writes=[ur])
            return f

        emit_mod_sharded(b, c, ar, nc, PS, cvec, w_ada, b_ada, selmod, ml_t, G_t, [modrow0, modrow1])
        emit_ffn(b, c, ar, nc, PS, x, 'x', xs[0], 'xs0', TILES9, TX, modrow0, g[0, 0], g[0, 1], (0, 1, 2), w_in[0, 0], w_out[0, 0],
                 premix=dict(u_out=EU0.local_tile, g_pre=g[0, 2], i_shift=3, i_scale=4))
        EU0.gather(b)
        emit_mlstm(b, c, ar, nc, PS, full_u(EU0), w_mh, w_g, conv_w, gate_b, head_g, cm, h_writer0, None)
        emit_diff(b, c, ar, nc, PS, full_u(EU0), w_dh, w_dv, cos_d, sin_d, lam_p, subln_g, cm, h_writer0, None, 0.2)
        EH0.gather(b)
        emit_postmix(b, c, ar, nc, PS, h_loader(EH0, 0), None, w_mix0, xs[0], 'xs0', xs[1], 'xs1', TILES9, TX, modrow0, g[0, 3])
        emit_ffn(b, c, ar, nc, PS, xs[1], 'xs1', xs[2], 'xs2', TILES9, TX, modrow0, g[0, 4], g[0, 5], (6, 7, 8), w_in[0, 1], w_out[0, 1])
        emit_ffn(b, c, ar, nc, PS, xs[2], 'xs2', xs[3], 'xs3', TILES9, TX, modrow1, g[1, 0], g[1, 1], (0, 1, 2), w_in[1, 0], w_out[1, 0],
                 premix=dict(u_out=EU1.local_tile, g_pre=g[1, 2], i_shift=3, i_scale=4))
        EU1.gather(b)
        emit_na(b, c, ar, nc, PS, full_u(EU1), w_nh, w_nv, tt, cm, h_writer1, None)
        EH1.gather(b)
        emit_postmix(b, c, ar, nc, PS, h_loader(EH1, 1), None, w_mix1, xs[3], 'xs3', xs[4], 'xs4', TILES8, 1024, modrow1, g[1, 3])
        emit_ffn(b, c, ar, nc, PS, xs[4], 'xs4', y, 'y', TILES8, 1024, modrow1, g[1, 4], g[1, 5], (6, 7, 8), w_in[1, 1], w_out[1, 1])
        b.finish(['y'])
        print('fused program: ninst', b.ninst, 'nwaits', b.nwaits, 'sems', len(b.semh), 'cnt', {k: v for k, v in b.cnt.items() if isinstance(k, str)})
    return nc


def kernel(x, c, ctx, c_ctx, w_ada, b_ada, norm_g, w_ffn_in, w_ffn_out, w_in_even, w_out_even,
           mlstm_conv, mlstm_gate_b, mlstm_head_g, diff_lambda, diff_subln_g, w_in_odd, w_out_odd, na_rpb):
    z = dict(x=x, c=c, ctx=ctx, c_ctx=c_ctx, w_ada=w_ada, b_ada=b_ada, norm_g=norm_g, w_ffn_in=w_ffn_in, w_ffn_out=w_ffn_out,
             w_in_even=w_in_even, w_out_even=w_out_even, mlstm_conv=mlstm_conv, mlstm_gate_b=mlstm_gate_b, mlstm_head_g=mlstm_head_g,
             diff_lambda=diff_lambda, diff_subln_g=diff_subln_g, w_in_odd=w_in_odd, w_out_odd=w_out_odd, na_rpb=na_rpb)
    z = {k: np.ascontiguousarray(np.asarray(v, dtype=np.float32)) for k, v in z.items()}
    ident = np.eye(128, dtype=np.float32)
    ncores = 8
    cm = cmat(); cos, sin = rope_tables()
    packs = []
    for hh in range(2):
        m = pack_even_mlstm(z, hh); m.update(pack_even_diff(z, hh)); m.update(pack_na(z, hh))
        packs.append(m)
    maps = []
    for cid in range(ncores):
        bb, hh = cid // 2, cid % 2
        xt = np.concatenate([z['x'][bb, hh * 1024:(hh + 1) * 1024], z['ctx'][bb, hh * 128:(hh + 1) * 128]], 0)
        selv = np.zeros((128, 2), np.float32); selv[:, hh] = 1.0
        selm = np.zeros((5, 2), np.float32); selm[bb, 0] = 1.0; selm[4, 1] = 1.0
        m = dict(x=np.ascontiguousarray(xt), cvec=np.concatenate([z['c'], z['c_ctx'][None]], 0), sel=selv, selmod=selm,
                 w_ada=np.ascontiguousarray(z['w_ada'][:, :, cid * MODC:(cid + 1) * MODC]), b_ada=np.ascontiguousarray(z['b_ada'][:, cid * MODC:(cid + 1) * MODC]), g=z['norm_g'],
                 w_in=z['w_ffn_in'], w_out=z['w_ffn_out'], ident=ident, cmat=cm, cos=cos, sin=sin,
                 w_mix0=z['w_out_even'][0], w_mix1=z['w_out_odd'][0])
        m.update(packs[hh])
        maps.append(m)
    res = run_bass_kernel_spmd(build_fused(), maps, core_ids=list(range(ncores)))
    out = np.zeros((4, 2048, D), np.float32)
    for cid in range(ncores):
        bb, hh = cid // 2, cid % 2
        out[bb, hh * 1024:(hh + 1) * 1024] = res.results[cid]['y']
    return out
```
